# Optimizing a Trainium2 kernel written in Bass

```python
import math
import jax, jax.numpy as jnp
from jax import lax
import numpy as np

D_MODEL = 1024
BATCH = 8
SEQ = 2048
DEPTH = 4
DEC_BATCH = 128
DEC_SEQ = 1
PAST_LEN = 16384
PAGE_SIZE = 128

HG_HEADS = 8
HG_DK = 128
HG_DV = D_MODEL // HG_HEADS
HG_WIDTH = HG_HEADS * HG_DK
HG_VWIDTH = HG_HEADS * HG_DV
HG_CHUNK = 32
SSM_EXPAND = 2
SSM_INNER = SSM_EXPAND * D_MODEL
SSM_HEADDIM = 64
SSM_HEADS = SSM_INNER // SSM_HEADDIM
SSM_GROUPS = 8
SSM_HPG = SSM_HEADS // SSM_GROUPS
SSM_STATE = 128
SSM_CONV = 4
SSM_CONV_CH = SSM_INNER + 2 * SSM_GROUPS * SSM_STATE
SSM_CHUNK = 64
D_FF = 4 * D_MODEL
N_ADA = 6
EPS = 1e-6
IN_SIZES = (HG_WIDTH, HG_WIDTH, HG_VWIDTH, HG_VWIDTH, SSM_INNER, SSM_CONV_CH, SSM_HEADS, D_MODEL, D_MODEL)
IN_WIDTH = sum(IN_SIZES)
IN_OFFSETS = tuple(int(v) for v in np.cumsum(IN_SIZES)[:-1])

kernel_name = "hgrn2_mamba2_gated_hybrid_step"


def _rmsnorm(x, w):
    xf = x.astype(jnp.float32)
    xf = xf * lax.rsqrt(jnp.mean(jnp.square(xf), axis=-1, keepdims=True) + EPS)
    return (xf * w.astype(jnp.float32)).astype(x.dtype)


def _chunk(a, C):
    B, T = a.shape[:2]
    pad = (-T) % C
    a = jnp.pad(a, [(0, 0), (0, pad)] + [(0, 0)] * (a.ndim - 2))
    a = a.reshape((B, (T + pad) // C, C) + a.shape[2:])
    return jnp.moveaxis(a, 1, 0)


def _unchunk(a, T):
    a = jnp.moveaxis(a, 0, 1)
    a = a.reshape((a.shape[0], -1) + a.shape[3:])
    return a[:, :T]


def _hgrn2_recurrence(q, k, v, logf, S0):
    T = q.shape[1]
    C = min(HG_CHUNK, T)
    xs = tuple(_chunk(a, C) for a in (q, k, v, logf))
    causal = jnp.tril(jnp.ones((C, C), dtype=bool))[None, :, :, None, None]

    def step(S, inp):
        qc, kc, vc, gc = inp
        b = jnp.cumsum(gc, axis=1)
        b_end = b[:, -1]
        o_inter = jnp.einsum('bihd,bhde->bihe', qc * jnp.exp(b), S)
        rel = b[:, :, None] - b[:, None, :]
        decay = jnp.exp(jnp.where(causal, rel, -jnp.inf))
        scores = jnp.einsum('bihd,bjhd,bijhd->bhij', qc, kc, decay)
        o_intra = jnp.einsum('bhij,bjhe->bihe', scores, vc)
        S = jnp.exp(b_end)[..., None] * S + jnp.einsum(
            'bjhd,bjhe->bhde', kc * jnp.exp(b_end[:, None] - b), vc)
        return S, o_inter + o_intra

    S, o = lax.scan(step, S0, xs)
    return _unchunk(o, T), S


def _ssd_recurrence(x, la, Bm, Cm, h0):
    T = x.shape[1]
    C = min(SSM_CHUNK, T)
    xs = tuple(_chunk(a, C) for a in (x, la, Bm, Cm))
    causal = jnp.tril(jnp.ones((C, C), dtype=bool))[None, :, :, None, None]

    def step(h, inp):
        xc, lc, bc, cc = inp
        cum = jnp.cumsum(lc, axis=1)
        y_inter = jnp.einsum('bign,bgkpn,bigk->bigkp', cc, h, jnp.exp(cum))
        rel = cum[:, :, None] - cum[:, None, :]
        L = jnp.exp(jnp.where(causal, rel, -jnp.inf))
        cb = jnp.einsum('bign,bjgn->bgij', cc, bc)
        y_intra = jnp.einsum('bgij,bijgk,bjgkp->bigkp', cb, L, xc)
        h = jnp.exp(cum[:, -1])[..., None, None] * h + jnp.einsum(
            'bjgn,bjgk,bjgkp->bgkpn', bc, jnp.exp(cum[:, -1:] - cum), xc)
        return h, y_inter + y_intra

    h, y = lax.scan(step, h0, xs)
    return _unchunk(y, T), h


def _causal_conv(u, prev, w, b):
    T = u.shape[1]
    full = jnp.concatenate([prev, u], axis=1)
    y = b + sum(full[:, j:j + T] * w[j] for j in range(SSM_CONV))
    return y, full[:, -(SSM_CONV - 1):]


def _layer(x, c, S_hg, h_ssm, conv_prev, p, lb, first_layer):
    f32 = jnp.float32
    Bsz, T, _ = x.shape
    mod = jnp.dot(jax.nn.silu(c), p['w_ada']) + p['b_ada']
    sh1, sc1, g1, sh2, sc2, g2 = [m[:, None, :] for m in jnp.split(mod, N_ADA, axis=-1)]
    h = _rmsnorm(x, p['norm_mix']) * (1 + sc1) + sh1
    proj = h @ p['w_in']
    q, f, i, og, z, xbc, dt, ga, gb = jnp.split(proj, IN_OFFSETS, axis=-1)

    ff = f.astype(f32)
    if first_layer:
        logf = jax.nn.log_sigmoid(ff)
        k = jax.nn.sigmoid(-ff)
    else:
        fg = lb + (1 - lb) * jax.nn.sigmoid(ff)
        logf = jnp.log(fg)
        k = 1 - fg
    qh = (jax.nn.silu(q.astype(f32)) * HG_DK ** -0.5).reshape(Bsz, T, HG_HEADS, HG_DK)
    kh = k.reshape(Bsz, T, HG_HEADS, HG_DK)
    lh = logf.reshape(Bsz, T, HG_HEADS, HG_DK)
    vh = i.astype(f32).reshape(Bsz, T, HG_HEADS, HG_DV)
    o_hg, S_new = _hgrn2_recurrence(qh, kh, vh, lh, S_hg.astype(f32))
    o_hg = _rmsnorm(o_hg, p['hg_norm']) * jax.nn.silu(og.astype(f32).reshape(Bsz, T, HG_HEADS, HG_DV))
    o_hg = o_hg.reshape(Bsz, T, HG_VWIDTH).astype(x.dtype)

    xbc_c, conv_new = _causal_conv(xbc, conv_prev.astype(xbc.dtype), p['conv_w'], p['conv_b'])
    xbc_c = jax.nn.silu(xbc_c).astype(f32)
    xs_, Bm, Cm = jnp.split(xbc_c, [SSM_INNER, SSM_INNER + SSM_GROUPS * SSM_STATE], axis=-1)
    dtv = jax.nn.softplus(dt.astype(f32) + p['dt_bias'].astype(f32))
    A = -jnp.exp(p['a_log'].astype(f32))
    la = (dtv * A).reshape(Bsz, T, SSM_GROUPS, SSM_HPG)
    xh = xs_.reshape(Bsz, T, SSM_GROUPS, SSM_HPG, SSM_HEADDIM)
    y, h_new = _ssd_recurrence(
        xh * dtv.reshape(Bsz, T, SSM_GROUPS, SSM_HPG, 1), la,
        Bm.reshape(Bsz, T, SSM_GROUPS, SSM_STATE), Cm.reshape(Bsz, T, SSM_GROUPS, SSM_STATE),
        h_ssm.astype(f32).reshape(Bsz, SSM_GROUPS, SSM_HPG, SSM_HEADDIM, SSM_STATE))
    y = y + p['d_skip'].astype(f32).reshape(SSM_GROUPS, SSM_HPG, 1) * xh
    y = y.reshape(Bsz, T, SSM_INNER) * jax.nn.silu(z.astype(f32))
    y = _rmsnorm(y.reshape(Bsz, T, SSM_GROUPS, SSM_INNER // SSM_GROUPS),
                 p['ssm_norm'].reshape(SSM_GROUPS, SSM_INNER // SSM_GROUPS))
    y = y.reshape(Bsz, T, SSM_INNER).astype(x.dtype)

    bma, bmb = jnp.split(p['b_merge'], 2, axis=-1)
    u = jax.nn.sigmoid(ga + bma) * (o_hg @ p['w_br_a']) + jax.nn.sigmoid(gb + bmb) * (y @ p['w_br_b'])
    x = x + g1 * (u @ p['w_out'])

    h2 = _rmsnorm(x, p['norm_mlp']) * (1 + sc2) + sh2
    x = x + g2 * (jnp.square(jax.nn.relu(h2 @ p['w_up'])) @ p['w_down'])

    S_out = S_new.astype(x.dtype)
    h_out = h_new.reshape(Bsz, SSM_HEADS, SSM_HEADDIM, SSM_STATE).astype(x.dtype)
    return x, S_out, h_out, conv_new.astype(x.dtype)


def setup_inputs(seed: int = 0) -> dict:
    key = jax.random.key(seed)
    ks = jax.random.split(key, 32)
    D = D_MODEL

    def nrm(k, shape, s):
        return s * jax.random.normal(k, shape, jnp.float32)

    dt0 = jnp.exp(jax.random.uniform(ks[16], (DEPTH, SSM_HEADS), jnp.float32,
                                     math.log(1e-3), math.log(1e-1)))
    return {
        'x_prompt': nrm(ks[0], (BATCH, SEQ, D), 1.0),
        'x_sample': nrm(ks[1], (DEC_BATCH, DEC_SEQ, D), 1.0),
        'state_hgrn': nrm(ks[2], (DEPTH, DEC_BATCH, HG_HEADS, HG_DK, HG_DV), 0.5),
        'state_ssm': nrm(ks[3], (DEPTH, DEC_BATCH, SSM_HEADS, SSM_HEADDIM, SSM_STATE), 0.5),
        'state_conv': nrm(ks[4], (DEPTH, DEC_BATCH, SSM_CONV - 1, SSM_CONV_CH), 1.0),
        'c_prompt': nrm(ks[5], (BATCH, D), 1.0),
        'c_sample': nrm(ks[6], (DEC_BATCH, D), 1.0),
        'w_ada': nrm(ks[7], (DEPTH, D, N_ADA * D), 0.5 * D ** -0.5),
        'b_ada': nrm(ks[8], (DEPTH, N_ADA * D), 0.01),
        'norm_mix': 1.0 + nrm(ks[9], (DEPTH, D), 0.05),
        'w_in': nrm(ks[10], (DEPTH, D, IN_WIDTH), D ** -0.5),
        'b_merge': nrm(ks[11], (DEPTH, 2 * D), 0.01),
        'lower_bounds': nrm(ks[12], (DEPTH, HG_WIDTH), 0.1),
        'hg_norm': 1.0 + nrm(ks[13], (DEPTH, HG_DV), 0.05),
        'conv_w': nrm(ks[14], (DEPTH, SSM_CONV, SSM_CONV_CH), SSM_CONV ** -0.5),
        'conv_b': nrm(ks[15], (DEPTH, SSM_CONV_CH), 0.01),
        'dt_bias': dt0 + jnp.log(-jnp.expm1(-dt0)),
        'a_log': jnp.log(jax.random.uniform(ks[17], (DEPTH, SSM_HEADS), jnp.float32, 1.0, 16.0)),
        'd_skip': 1.0 + nrm(ks[18], (DEPTH, SSM_HEADS), 0.05),
        'ssm_norm': 1.0 + nrm(ks[19], (DEPTH, SSM_INNER), 0.05),
        'w_br_a': nrm(ks[20], (DEPTH, HG_VWIDTH, D), HG_VWIDTH ** -0.5),
        'w_br_b': nrm(ks[21], (DEPTH, SSM_INNER, D), SSM_INNER ** -0.5),
        'w_out': nrm(ks[22], (DEPTH, D, D), D ** -0.5),
        'norm_mlp': 1.0 + nrm(ks[23], (DEPTH, D), 0.05),
        'w_up': nrm(ks[24], (DEPTH, D, D_FF), D ** -0.5),
        'w_down': nrm(ks[25], (DEPTH, D_FF, D), D_FF ** -0.5),
        'norm_final': 1.0 + nrm(ks[26], (D,), 0.05),
    }


def reference(x_prompt, x_sample, state_hgrn, state_ssm, state_conv, c_prompt, c_sample,
              w_ada, b_ada, norm_mix, w_in, b_merge, lower_bounds, hg_norm, conv_w, conv_b,
              dt_bias, a_log, d_skip, ssm_norm, w_br_a, w_br_b, w_out, norm_mlp, w_up, w_down,
              norm_final):
    lbs = jnp.cumsum(jax.nn.softmax(lower_bounds.astype(jnp.float32), axis=0), axis=0)
    lbs = lbs - lbs[0:1]
    dt = x_prompt.dtype
    xp, xs = x_prompt, x_sample
    hg_p, ssm_p, cv_p, hg_s, ssm_s, cv_s = [], [], [], [], [], []
    for l in range(DEPTH):
        p = {'w_ada': w_ada[l], 'b_ada': b_ada[l], 'norm_mix': norm_mix[l], 'w_in': w_in[l],
             'b_merge': b_merge[l], 'hg_norm': hg_norm[l], 'conv_w': conv_w[l], 'conv_b': conv_b[l],
             'dt_bias': dt_bias[l], 'a_log': a_log[l], 'd_skip': d_skip[l], 'ssm_norm': ssm_norm[l],
             'w_br_a': w_br_a[l], 'w_br_b': w_br_b[l], 'w_out': w_out[l], 'norm_mlp': norm_mlp[l],
             'w_up': w_up[l], 'w_down': w_down[l]}
        first = l == 0
        xp, s1, s2, s3 = _layer(
            xp, c_prompt,
            jnp.zeros((xp.shape[0], HG_HEADS, HG_DK, HG_DV), dt),
            jnp.zeros((xp.shape[0], SSM_HEADS, SSM_HEADDIM, SSM_STATE), dt),
            jnp.zeros((xp.shape[0], SSM_CONV - 1, SSM_CONV_CH), dt),
            p, lbs[l], first)
        hg_p.append(s1); ssm_p.append(s2); cv_p.append(s3)
        xs, t1, t2, t3 = _layer(xs, c_sample, state_hgrn[l], state_ssm[l], state_conv[l],
                                p, lbs[l], first)
        hg_s.append(t1); ssm_s.append(t2); cv_s.append(t3)
    y_prompt = _rmsnorm(xp, norm_final)
    y_sample = _rmsnorm(xs, norm_final)
    return (y_prompt, y_sample, jnp.stack(hg_p), jnp.stack(ssm_p), jnp.stack(cv_p),
            jnp.stack(hg_s), jnp.stack(ssm_s), jnp.stack(cv_s))
```

```python
import os
import numpy as np
import concourse.bass as bass
import concourse.mybir as mybir
from concourse.bass_utils import run_bass_kernel_spmd

F32 = mybir.dt.float32
BF16 = mybir.dt.bfloat16
AF = mybir.ActivationFunctionType
ALU = mybir.AluOpType

ENGS = ["pe", "act", "dve", "pool", "sp"]
NSLOT = 28


class Sched:
    def __init__(self, nc):
        self.nc = nc
        self.ops = []
        self.by_eng = {e: [] for e in ENGS}
        self.last_w = {}
        self.readers = {}
        self.slot_last = [None] * NSLOT
        self.next_slot = 0
        self.ndom = 4 + NSLOT
        self.dma_ops = []

    def add(self, eng, fn, R=(), W=(), dma=False):
        op = dict(id=len(self.ops), eng=eng, fn=fn, dma=dma, deps=set(), flag=False, raw=set())
        if dma:
            s = self.next_slot
            self.next_slot = (s + 1) % NSLOT
            op["dom"] = 4 + s
            prev = self.slot_last[s]
            if prev is not None:
                op["deps"].add(prev)
            self.slot_last[s] = op["id"]
            self.dma_ops.append(op["id"])
        else:
            op["dom"] = ENGS.index(eng)
        deps = op["deps"]
        for r in R:
            lw = self.last_w.get(r)
            if lw is not None:
                deps.add(lw)
                op["raw"].add(lw)
            rd = self.readers.setdefault(r, {})
            rd[("dma", op["id"]) if dma else eng] = op["id"]
        for w in W:
            lw = self.last_w.get(w)
            if lw is not None:
                deps.add(lw)
            for rd in self.readers.get(w, {}).values():
                if rd != op["id"]:
                    deps.add(rd)
            self.last_w[w] = op["id"]
            self.readers[w] = {}
        self.ops.append(op)
        self.by_eng[eng].append(op)
        return op["id"]

    def final_wait(self, eng="sp"):
        op = dict(id=len(self.ops), eng=eng, fn=None, dma=False, deps=set(self.dma_ops), flag=False,
                  dom=ENGS.index(eng), raw=set())
        self.ops.append(op)
        self.by_eng[eng].append(op)

    def emit(self):
        nc = self.nc
        ops = self.ops
        def relevant(op, d):
            dop = ops[d]
            if dop["dma"] or dop["eng"] != op["eng"]:
                return True
            return op["eng"] != "pe" and d in op["raw"]
        for op in ops:
            for d in op["deps"]:
                if relevant(op, d):
                    ops[d]["flag"] = True
        cnt = [0] * self.ndom
        for op in ops:
            if op["dma"] or op["flag"]:
                cnt[op["dom"]] += 1
                op["seq"] = cnt[op["dom"]]
        seen = {e: np.zeros(self.ndom, np.int64) for e in ENGS}
        for op in ops:
            e = op["eng"]
            sv = seen[e]
            need = {}
            for d in sorted(op["deps"]):
                dop = ops[d]
                if not relevant(op, d):
                    continue
                dm = dop["dom"]
                if dop["seq"] > sv[dm]:
                    need[dm] = max(need.get(dm, 0), dop["seq"])
                np.maximum(sv, dop["clk"], out=sv)
            op["waits"] = sorted(need.items())
            if op["flag"] or op["dma"]:
                clk = sv.copy()
                clk[op["dom"]] = max(clk[op["dom"]], op["seq"]) if not op["dma"] else op["seq"]
                op["clk"] = clk
        sems = [nc.alloc_semaphore(name=f"s_{i}") for i in range(self.ndom)]

        def run_eng(ename):
            def body(e):
                for op in self.by_eng[ename]:
                    for dm, s in op["waits"]:
                        e.wait_ge(sems[dm], s * 16 if dm >= 4 else s)
                    if op["fn"] is None:
                        continue
                    ins = op["fn"](e)
                    if op["dma"]:
                        ins.then_inc(sems[op["dom"]], 16)
                    elif op["flag"]:
                        ins.then_inc(sems[op["dom"]], 1)
            return body

        with nc.Block() as block:
            block.tensor(run_eng("pe"))
            block.scalar(run_eng("act"))
            block.vector(run_eng("dve"))
            block.gpsimd(run_eng("pool"))
            block.sync(run_eng("sp"))


D = 1024
KC = 8
NS = 16
EPS = 1e-6
EL_HEAD, EL_GRP, EL_DT, EL_C, EL_WO, EL_UP, EL_DN, EL_ADA = 4096, 6144, 256, 5120, 4096, 4096, 4096, 4096
OFF_HEAD = 0
OFF_GRP = OFF_HEAD + 8 * EL_HEAD
OFF_DT = OFF_GRP + 8 * EL_GRP
OFF_C = OFF_DT + EL_DT
OFF_WO = OFF_C + 8 * EL_C
OFF_UP = OFF_WO + 2 * EL_WO
OFF_DN = OFF_UP + 8 * EL_UP
OFF_ADA = OFF_DN + 8 * EL_DN
OFF_DTX = OFF_ADA + 12 * EL_ADA
EL_DTX = 4096
EL_TOT = OFF_DTX + 4 * EL_DTX
WSLOT = 6144
NWS = 2
V_NMIX, V_NMLP, V_BADA, V_CW, V_CB, V_SN, V_BM, V_HGN, V_DSK, V_LB, V_NF = 0, 8, 16, 64, 192, 224, 240, 256, 257, 273, 305
V_DTB, V_ALG = 320, 336
NV = 352


def build(L, NTOK, TH, do_sample=True, dbg=False, stage=99, skipB=False):
    nc = bass.Bass("TRN2", target_bir_lowering=False)
    S = Sched(nc)
    NHALF = NTOK // TH
    NB = TH // 512
    NT = NTOK // 128
    dram = lambda n, s, k, dt=F32: nc.dram_tensor(n, s, dt, kind=k).ap()
    d_xT = dram("xT", [128, KC, NTOK], "ExternalInput")
    d_xsT = dram("xsT", [128, KC, NS], "ExternalInput")
    d_cT = dram("cT", [128, KC, 1 + NS], "ExternalInput")
    d_w = dram("wpk", [L, 128, EL_TOT], "ExternalInput")
    d_vec = dram("vec", [128, L, NV], "ExternalInput")
    d_tmv = dram("tmv", [128, L, 64], "ExternalInput")
    d_shg = dram("s_hg", [L, NS, 8, 128, 128], "ExternalInput")
    d_ssm = dram("s_ssm", [L, NS, 32, 64, 128], "ExternalInput")
    d_scv = dram("s_cvT", [L, 128, 32, 3, NS], "ExternalInput")
    o_yT = dram("o_yT", [128, KC, NTOK], "ExternalOutput")
    o_ysT = dram("o_ysT", [128, KC, NS], "ExternalOutput")
    o_hgp = dram("o_hgp", [L, 8, 128, 128], "ExternalOutput")
    o_ssp = dram("o_ssp", [L, 128, 2048], "ExternalOutput")
    o_cvp = dram("o_cvp", [L, 128, 32, 3], "ExternalOutput")
    o_hgs = dram("o_hgs", [L, NS, 8, 128, 128], "ExternalOutput")
    o_sss = dram("o_sss", [L, NS, 32, 64, 128], "ExternalOutput")
    o_cvs = dram("o_cvs", [L, 128, 32, 3, NS], "ExternalOutput")

    sb = lambda n, s, dt=F32: nc.alloc_sbuf_tensor(n, s, dt).ap()
    xT = sb("xTs", [128, KC, NTOK])
    xsT = sb("xsTs", [128, KC, NS])
    hT = sb("hTs", [128, KC, TH], BF16)
    hsT = sb("hsTs", [128, KC, NS], BF16)
    BIG = sb("BIG", [128, 32, TH], BF16)
    ohT = BIG[:, 0:8, :]
    yT = BIG[:, 8:24, :]
    uT = BIG[:, 24:32, :]
    aT = BIG
    BIGs = sb("BIGs", [128, 32, NS], BF16)
    ohsT, ysT, usT, asT = BIGs[:, 0:8, :], BIGs[:, 8:24, :], BIGs[:, 24:32, :], BIGs
    Shg = sb("Shg", [128, 8, 128])
    HT = sb("HTs", [128, 2048])
    hist = sb("hist", [128, 32, 3])
    wring = sb("wring", [128, NWS, WSLOT], BF16)
    vec = sb("vecs", [128, L, NV])
    tmv = sb("tmvs", [128, L, 64])
    modT = sb("modT", [128, 48, 1 + NS])
    scT = sb("scT", [128, KC, 1 + NS], BF16)
    lbs = sb("lbs", [128, L, 8])
    ident_b = sb("ident_b", [128, 128], BF16)
    ident_f = sb("ident_f", [128, 128])
    ones_f = sb("ones_f", [128, 128])
    U_f = sb("U_f", [128, 128])
    U_b = sb("U_b", [128, 128], BF16)
    V_b = sb("V_b", [128, 128], BF16)
    ones_b = sb("ones_b", [128, 128], BF16)
    PB = [nc.alloc_psum_tensor(f"pb{i}", [128, 512], F32).ap() for i in range(8)]

    ALIAS = {"ysb": "nbuf", "sqy": "nbuf", "sqo": "nbuf", "osb": "nbuf", "tmo": "logf", "Ub1": "Ub0", "Ub2": "Ub0", "Ub3": "Ub0", "thA": "qs", "thB": "th", "t1": "sog", "t2": "logf", "rl0": "kk", "rl1": "bcs",
             "lnvn": "th", "rsn": "logf", "lnvo": "th", "rso": "logf", "sqn": "nbuf", "ntmp": "nbuf",
             "Lm": "qs", "EC": "th", "cacc": "sog", "lnvy": "th", "rsy": "logf", "E1": "th", "E2": "logf",
             "Rh": "qtT", "Rl": "ktT", "Mm": "vTb", "zs": "kk",
             "yys": "qs", "sqys": "th", "lnvs": "sog", "osbs": "logf", "tmos": "kk", "XCb": "bcs", "sqos": "bcs",
             "lnvos": "sog", "rsos": "qs", "nstmp": "bcs"}
    canon = lambda ks: [ALIAS.get(k, k) if isinstance(k, str) else k for k in ks]
    _sadd = S.add
    S.add = lambda eng, fn, R=(), W=(), dma=False: _sadd(eng, fn, canon(R), canon(W), dma)

    def add(eng, fn, R=(), W=()):
        return S.add(eng, fn, R, W)

    def dma(eng, out, in_, R=(), W=()):
        return S.add(eng, lambda e: e.dma_start(out=out, in_=in_), R, W, dma=True)

    def mm(out, lhsT, rhs, start, stop, R, W):
        return S.add("pe", lambda e: e.matmul(out, lhsT=lhsT, rhs=rhs, start=start, stop=stop,
                                             skip_group_check=True), R, W)

    def tp(out, in_, ident, R, W):
        return S.add("pe", lambda e: e.transpose(out, in_, ident), R, W)

    def act(out, in_, func, R, W, scale=1.0, bias=None, eng="act"):
        if bias is None:
            return S.add(eng, lambda e: e.activation(out=out, in_=in_, func=func, scale=scale), R, W)
        return S.add(eng, lambda e: e.activation(out=out, in_=in_, func=func, scale=scale, bias=bias), R, W)

    def tt(eng, out, in0, in1, op, R, W):
        return S.add(eng, lambda e: e.tensor_tensor(out=out, in0=in0, in1=in1, op=op), R, W)

    def ts(eng, out, in0, s1, op0, R, W, s2=None, op1=None):
        if op1 is None:
            return S.add(eng, lambda e: e.tensor_scalar(out=out, in0=in0, scalar1=s1, scalar2=None, op0=op0), R, W)
        return S.add(eng, lambda e: e.tensor_scalar(out=out, in0=in0, scalar1=s1, scalar2=s2, op0=op0, op1=op1), R, W)

    def stt(out, in0, scalar, in1, op0, op1, R, W):
        return S.add("dve", lambda e: e.scalar_tensor_tensor(out=out, in0=in0, scalar=scalar, in1=in1,
                                                            op0=op0, op1=op1), R, W)

    def cp(eng, out, in_, R, W):
        if eng == "act":
            return S.add("act", lambda e: e.activation(out=out, in_=in_, func=AF.Identity), R, W)
        return S.add(eng, lambda e: e.tensor_copy(out=out, in_=in_), R, W)

    def memset(eng, ap, val, W):
        return S.add(eng, lambda e: e.memset(ap, val), (), W)

    wstate = dict(n=0)

    def wload(l, off, el):
        s = wstate["n"] % NWS
        wstate["n"] += 1
        key = ("wr", s)
        dst = wring[:, s, 0:el]
        dma("pool", dst, d_w[l, :, off:off + el], W=[key])
        return wring[:, s, :], key

    dma("sp", xT, d_xT, W=[("xT", t) for t in range(NT)])
    dma("sp", vec, d_vec, W=["vec"])
    dma("sp", tmv, d_tmv, W=["tmv"])
    dma("sp", xsT, d_xsT, W=["xsT"])
    cT = sb("cTs", [128, KC, 1 + NS])
    dma("sp", cT, d_cT, W=["cT"])
    memset("pool", ones_f, 1.0, ["ones_f"])
    memset("pool", ones_b, 1.0, ["ones_b"])
    add("pool", lambda e: e.affine_select(out=ident_f, in_=ones_f, pattern=[[-1, 128]], compare_op=ALU.is_equal,
                                          fill=0.0, base=0, channel_multiplier=1), ["ones_f"], ["ident_f"])
    add("pool", lambda e: e.affine_select(out=U_f, in_=ones_f, pattern=[[1, 128]], compare_op=ALU.is_ge,
                                          fill=0.0, base=0, channel_multiplier=-1), ["ones_f"], ["U_f"])
    Vf = sb("V_f", [128, 128])
    add("pool", lambda e: e.affine_select(out=Vf, in_=ones_f, pattern=[[-1, 128]], compare_op=ALU.is_gt,
                                          fill=0.0, base=0, channel_multiplier=1), ["ones_f"], ["V_f"])
    cp("dve", ident_b, ident_f, ["ident_f"], ["ident_b"])
    cp("dve", U_b, U_f, ["U_f"], ["U_b"])
    cp("dve", V_b, Vf, ["V_f"], ["V_b"])
    memset("dve", Shg, 0.0, [("Shg", h) for h in range(8)])
    memset("dve", HT, 0.0, [("HT", g) for g in range(8)])
    memset("dve", hist, 0.0, [("hist", c) for c in range(32)])
    act(scT, cT, AF.Silu, ["cT"], ["scT"])
    lbw = sb("lbw", [128, L, 8])
    lbsum = sb("lbsum", [128, 8])
    lbraw = vec[:, 0, V_LB:V_LB + 8 * L].rearrange("p (l h) -> p l h", h=8)
    act(lbw, lbraw, AF.Exp, ["vec"], ["lbw"])
    cp("dve", lbsum, lbw[:, 0, :], ["lbw"], ["lbsum"])
    for l in range(1, L):
        tt("dve", lbsum, lbsum, lbw[:, l, :], ALU.add, ["lbw", "lbsum"], ["lbsum"])
    add("dve", lambda e: e.reciprocal(out=lbsum, in_=lbsum), ["lbsum"], ["lbsum"])
    memset("dve", lbs[:, 0, :], 0.0, ["lbs"])
    for l in range(1, L):
        tt("dve", lbw[:, l, :], lbw[:, l, :], lbsum, ALU.mult, ["lbw", "lbsum"], ["lbw"])
        tt("dve", lbs[:, l, :], lbs[:, l - 1, :], lbw[:, l, :], ALU.add, ["lbw", "lbs"], ["lbs"])

    WK = {}

    def wk(name, shape, dt=F32):
        sub = None
        SUBS = {"ysb": (0, 2), "sqy": (2, 4), "sqo": (0, 1), "osb": (1, 2)}
        if name in SUBS:
            sub = SUBS[name]
            name = "nbuf"
        else:
            name = ALIAS.get(name, name)
        if name == "nbuf":
            shape = [128, 4, 512]
        if name not in WK:
            WK[name] = sb("wk_" + name, shape, dt)
        if sub is not None:
            return WK[name][:, sub[0]:sub[1], :]
        r = WK[name]
        if name != "nbuf":
            need = 1
            for d_ in shape[1:]:
                need *= d_
            if len(r.shape) == 3:
                r = r.rearrange("p a b -> p (a b)")
            if r.dtype != dt:
                r = r.bitcast(dt)
            if r.shape[1] > need:
                r = r[:, 0:need]
            if len(shape) == 3:
                r = r.rearrange("p (a b) -> p a b", a=shape[1])
        return r

    for nm_ in ("qs", "th", "sog", "logf", "kk", "bcs"):
        wk(nm_, [128, 512])
    for nm_ in ("qtT", "ktT", "vTb"):
        wk(nm_, [128, 512], BF16)
    def rms_block(src_ap, nk, ncols, srckeys, tag, ps, pskey, inv_n):
        if tag == "n":
            sq = wk("sqn", [128, 4, 512])
            for hh in range(2):
                act(sq, src_ap[:, hh * 4:(hh + 1) * 4, :], AF.Square, srckeys, ["sqn"])
                for k in range(4):
                    mm(ps[:, 0:ncols], ones_f, sq[:, k, :], hh == 0 and k == 0, hh == 1 and k == 3, ["sqn", "ones_f"], [pskey])
        else:
            sq = wk("sq" + tag, [128, nk, ncols])
            act(sq, src_ap, AF.Square, srckeys, ["sq" + tag])
            for k in range(nk):
                mm(ps[:, 0:ncols], ones_f, sq[:, k, :], k == 0, k == nk - 1, ["sq" + tag, "ones_f"], [pskey])
        lnv = wk("lnv" + tag, [128, ncols])
        act(lnv, ps[:, 0:ncols], AF.Ln, [pskey], ["lnv" + tag], scale=inv_n, bias=epsT)
        rs = wk("rs" + tag, [128, ncols])
        act(rs, lnv, AF.Exp, ["lnv" + tag], ["rs" + tag], scale=-0.5)
        return rs, "rs" + tag

    epsT = sb("epsT", [128, 1])
    memset("dve", epsT, EPS, ["epsT"])

    def phase_B(l, hf, t0, smp, hkeys):
        vl = lambda off, n=1: vec[:, l, off:off + n]
        NTT = TH // 128
        wsl, wkey = wload(l, OFF_DT, EL_DT)
        wdt = wsl[:, 0:EL_DT].rearrange("p (k j) -> p k j", k=8)
        la_all = wk("la_all", [128, NTT, 32])
        dt_all = wk("dt_all", [128, NTT, 32])
        lah = wk("lah", [128, NTT, 32], BF16)
        lal = wk("lal", [128, NTT, 32], BF16)
        wend = wk("wend", [128, NTT, 32])
        etot = wk("etot", [128, NTT, 32])
        dtr = wk("dtr", [128, NTT, 32])
        cums = wk("cums", [128, NTT, 32])
        v3 = lambda ap_: ap_[:, 0:NTT * 32].rearrange("p (t h) -> p t h", h=32)
        for t in range(NTT):
            for k in range(8):
                mm(PB[0][:, t * 32:(t + 1) * 32], hT[:, k, t * 128:(t + 1) * 128], wdt[:, k, :], k == 0, k == 7,
                   [wkey] + hkeys(t // 4), [("pb", 0)])
        tt("dve", dtr, v3(PB[0]), tmv[:, l, 0:32].unsqueeze(1).broadcast_to([128, NTT, 32]), ALU.add,
           [("pb", 0), "tmv"], ["dtr"])
        act(dtr, dtr, AF.Exp, ["dtr"], ["dtr"])
        act(dt_all, dtr, AF.Ln, ["dtr", "ones_f"], ["dt_all"], bias=ones_f[:, 0:1])
        tt("dve", la_all, dt_all, Abc.unsqueeze(1).broadcast_to([128, NTT, 32]), ALU.mult, ["dt_all", "Abc"], ["la_all"])
        cp("dve", lah, la_all, ["la_all"], ["lah"])
        tt("dve", lal, la_all, lah, ALU.subtract, ["la_all", "lah"], ["lal"])
        for t in range(NTT):
            mm(PB[1][:, t * 32:(t + 1) * 32], U_f, la_all[:, t, :], True, True, ["U_f", "la_all"], [("pb", 1)])
        for t in range(NTT):
            mm(PB[2][:, t * 32:(t + 1) * 32], ones_f, la_all[:, t, :], True, True, ["ones_f", "la_all"], [("pb", 2)])
        act(etot, v3(PB[2]), AF.Exp, [("pb", 2)], ["etot"])
        cp("act", cums, v3(PB[1]), [("pb", 1)], ["cums"])
        tt("dve", cums, v3(PB[2]), cums, ALU.subtract, [("pb", 2), "cums"], ["cums"])
        act(wend, cums, AF.Exp, ["cums"], ["wend"])
        KB = int(os.environ.get('KB', '9'))
        for g in range(8):
            if KB < 1:
                break
            wsl, wkey = wload(l, OFF_GRP + g * EL_GRP, EL_GRP)
            wv = wsl[:, 0:EL_GRP].rearrange("p (k j c) -> p k j c", k=8, j=6)
            gs = slice(g * 256, (g + 1) * 256)
            hs = slice(4 * g, 4 * g + 4)
            for nb in range(NB):
                bs = slice(nb * 512, (nb + 1) * 512)
                zs = wk("zs", [128, 2, 512], BF16)
                xc = wk("xc", [128, 4, 512], BF16)
                for j in range(6):
                    pbi = j % 2
                    for k in range(8):
                        mm(PB[pbi], wv[:, k, j, :], hT[:, k, bs], k == 0, k == 7, [wkey] + hkeys(nb), [("pb", pbi)])
                    if j < 2:
                        act(zs[:, j, :], PB[pbi], AF.Silu, [("pb", pbi)], ["zs"])
                        continue
                    ci = j - 2
                    ch = (2 * g + ci) if ci < 2 else (16 + g if ci == 2 else 24 + g)
                    ubk = "Ub%d" % ci
                    Ub = wk(ubk, [128, 3 + 512])
                    cp("pool", Ub[:, 0:3], hist[:, ch, :], [("hist", ch)], [ubk])
                    act(Ub[:, 3:515], PB[pbi], AF.Identity, [("pb", pbi)], [ubk])
                    cp("pool", hist[:, ch, :], Ub[:, 512:515], [ubk], [("hist", ch)])
                    acc = wk("cacc", [128, 512])
                    ts("dve", acc, Ub[:, 0:512], vl(V_CW + ch), ALU.mult, [ubk, "vec"], ["cacc"], vl(V_CB + ch), ALU.add)
                    for jj in range(1, 4):
                        stt(acc, Ub[:, jj:jj + 512], vl(V_CW + jj * 32 + ch), acc, ALU.mult, ALU.add, [ubk, "vec", "cacc"], ["cacc"])
                    act(xc[:, ci, :], acc, AF.Silu, ["cacc"], ["xc"])
                for t in range(4):
                    if KB < 2:
                        break
                    tg = nb * 4 + t
                    ct = slice(t * 128, (t + 1) * 128)
                    tpb = PB[7].bitcast(BF16)
                    for i3 in range(3):
                        tp(tpb[:, i3 * 128:(i3 + 1) * 128], xc[:, i3, ct], ident_b, ["xc", "ident_b"], [("pb", 7)])
                    xbt = wk("xbt", [128, 384], BF16)
                    cp("act", xbt, tpb[:, 0:384], [("pb", 7)], ["xbt"])
                    dtx = wk("dtx", [128, 256], BF16)
                    tt("dve", dtx.rearrange("p (k q) -> p k q", q=64), xbt[:, 0:256].rearrange("p (k q) -> p k q", q=64),
                       dt_all[:, tg, hs].unsqueeze(2).broadcast_to([128, 4, 64]), ALU.mult, ["xbt", "dt_all"], ["dtx"])
                    Btok = xbt[:, 256:384]
                    dtxh = wk("dtxh", [128, 256], BF16)
                    tt("dve", dtxh.rearrange("p (k q) -> p k q", q=64), dtx.rearrange("p (k q) -> p k q", q=64),
                       wend[:, tg, hs].unsqueeze(2).broadcast_to([128, 4, 64]), ALU.mult, ["dtx", "wend"], ["dtxh"])
                    if KB < 3:
                        continue
                    Rh = wk("Rh", [128, 4, 128], BF16)
                    Rl = wk("Rl", [128, 4, 128], BF16)
                    Ubc = U_b.unsqueeze(1).broadcast_to([128, 4, 128])
                    tt("dve", Rh, Ubc, lah[:, tg, hs].unsqueeze(2).broadcast_to([128, 4, 128]), ALU.mult, ["U_b", "lah"], ["Rh"])
                    tt("dve", Rl, Ubc, lal[:, tg, hs].unsqueeze(2).broadcast_to([128, 4, 128]), ALU.mult, ["U_b", "lal"], ["Rl"])
                    Rhf = Rh.rearrange("p k i -> p (k i)")
                    Rlf = Rl.rearrange("p k i -> p (k i)")
                    mm(PB[2], V_b, Rhf, True, False, ["V_b", "Rh"], [("pb", 2)])
                    mm(PB[2], V_b, Rlf, False, True, ["V_b", "Rl"], [("pb", 2)])
                    mm(PB[3], ones_b, Rhf, True, False, ["ones_b", "Rh"], [("pb", 3)])
                    mm(PB[3], ones_b, Rlf, False, True, ["ones_b", "Rl"], [("pb", 3)])
                    Lm = wk("Lm", [128, 512])
                    EC = wk("EC", [128, 512])
                    act(Lm, PB[2], AF.Exp, [("pb", 2)], ["Lm"])
                    act(EC, PB[3], AF.Exp, [("pb", 3)], ["EC"])
                    if KB < 4:
                        continue
                    mm(PB[6][:, 0:128], xc[:, 2, ct], xc[:, 3, ct], True, True, ["xc"], [("pb", 6)])
                    cbm = wk("cbm", [128, 128])
                    tt("dve", cbm, PB[6][:, 0:128], U_f, ALU.mult, [("pb", 6), "U_f"], ["cbm"])
                    Mm = wk("Mm", [128, 4, 128], BF16)
                    tt("dve", Mm, Lm.rearrange("p (k i) -> p k i", k=4), cbm.unsqueeze(1).broadcast_to([128, 4, 128]), ALU.mult,
                       ["Lm", "cbm"], ["Mm"])
                    Ct = wk("Ct", [128, 4, 128], BF16)
                    tt("dve", Ct, EC.rearrange("p (k i) -> p k i", k=4), xc[:, 3, ct].unsqueeze(1).broadcast_to([128, 4, 128]),
                       ALU.mult, ["EC", "xc"], ["Ct"])
                    if KB < 5:
                        continue
                    HTb = wk("HTb", [128, 256], BF16)
                    cp("act", HTb, HT[:, gs], [("HT", g)], ["HTb"])
                    for k in range(4):
                        cc, po = k // 2, (k % 2) * 64
                        outp = PB[4 + cc][po:po + 64, ct]
                        mm(outp, dtx[:, k * 64:(k + 1) * 64], Mm[:, k, :], True, False, ["dtx", "Mm"], [("pb", 4 + cc)])
                        mm(outp, HTb[:, k * 64:(k + 1) * 64], Ct[:, k, :], False, True, ["HTb", "Ct"], [("pb", 4 + cc)])
                    if KB < 6:
                        continue
                    mm(PB[6][:, 128:384], Btok, dtxh, True, True, ["xbt", "dtxh"], [("pb", 6)])
                    HTg = HT[:, gs].rearrange("p (k q) -> p k q", q=64)
                    tt("dve", HTg, HTg, etot[:, tg, hs].unsqueeze(2).broadcast_to([128, 4, 64]), ALU.mult, [("HT", g), "etot"], [("HT", g)])
                    tt("dve", HT[:, gs], HT[:, gs], PB[6][:, 128:384], ALU.add, [("HT", g), ("pb", 6)], [("HT", g)])
                if KB < 7:
                    continue
                ysb = wk("ysb", [128, 2, 512])
                for cc in range(2):
                    c16 = 2 * g + cc
                    stt(ysb[:, cc, :], xc[:, cc, :], vl(V_DSK + c16), PB[4 + cc], ALU.mult, ALU.add, ["xc", "vec", ("pb", 4 + cc)], ["ysb"])
                    tt("pool", ysb[:, cc, :], ysb[:, cc, :], zs[:, cc, :], ALU.mult, ["ysb", "zs"], ["ysb"])
                rs, rskey = rms_block(ysb, 2, 512, ["ysb"], "y", PB[2], ("pb", 2), 1.0 / 256)
                for cc in range(2):
                    c16 = 2 * g + cc
                    stt(yT[:, c16, bs], ysb[:, cc, :], vl(V_SN + c16), rs, ALU.mult, ALU.mult, ["ysb", "vec", rskey], [("BIG", 8 + c16)])
        if hf == NHALF - 1:
            dma("sp", o_ssp[l], HT, R=[("HT", g) for g in range(8)])
            dma("sp", o_cvp[l], hist, R=[("hist", c) for c in range(32)])
            if l + 1 < L:
                memset("pool", HT, 0.0, [("HT", g) for g in range(8)])
                memset("pool", hist, 0.0, [("hist", c) for c in range(32)])

    BIGf = BIG.rearrange("p a b -> p (a b)").bitcast(F32)
    CH_EL = TH // 2

    def carve(off, n, shape=None, dt=F32, parts=128):
        ap_ = BIGf[0:parts, off:off + n]
        if dt == BF16:
            ap_ = ap_.bitcast(BF16)
        if shape is not None:
            if len(shape) == 2:
                ap_ = ap_.rearrange("p (a b) -> p a b", a=shape[0])
            else:
                ap_ = ap_.rearrange("p (a b c) -> p a b c", a=shape[0], b=shape[1])
        keys = [("BIG", c) for c in range(off // CH_EL, (off + n - 1) // CH_EL + 1)]
        return ap_, keys

    def sample_pass(l):
        vl = lambda off, n=1: vec[:, l, off:off + n]
        R_ = lambda k: k * 1024
        rs, rskey = rms_block(xsT, 8, NS, ["xsT"], "ns", PB[0], ("pb", 0), 1.0 / D)
        tmp = wk("nstmp", [128, 8, NS])
        tt("dve", tmp, xsT, rs.unsqueeze(1).broadcast_to([128, 8, NS]), ALU.mult, ["xsT", rskey], ["nstmp"])
        tt("dve", tmp, tmp, mods[:, 0, :, 1:], ALU.mult, ["nstmp", "mods"], ["nstmp"])
        tt("dve", hsT, tmp, mods[:, 1, :, 1:], ALU.add, ["nstmp", "mods"], ["hsT"])
        Sel, kSel = carve(R_(5), 1024, [16, 128], BF16, parts=16)
        memset("pool", Sel, 1.0, kSel)
        add("pool", lambda e: e.affine_select(out=Sel, in_=Sel, pattern=[[-1, 16], [0, 128]], compare_op=ALU.is_equal,
                                              fill=0.0, base=0, channel_multiplier=1), kSel, kSel)
        vtok, kvtok = carve(R_(6), 512, None, BF16, parts=16)
        sm, ksm = carve(R_(7), 512, [4, 8, 16])
        SQ, SFG, SKK, SOG = sm[:, 0], sm[:, 1], sm[:, 2], sm[:, 3]
        for h in range(8):
            wsl, wkey = wload(l, OFF_HEAD + h * EL_HEAD, EL_HEAD)
            wv = wsl[:, 0:EL_HEAD].rearrange("p (k j c) -> p k j c", k=8, j=4)
            for j in range(4):
                for k in range(8):
                    mm(PB[0][:, j * 16:(j + 1) * 16], wv[:, k, j, :], hsT[:, k, :], k == 0, k == 7, [wkey, "hsT"], [("pb", 0)])
            for k in range(8):
                mm(PB[1][0:16, 0:128], hsT[:, k, :], wv[:, k, 2, :], k == 0, k == 7, [wkey, "hsT"], [("pb", 1)])
            qss = wk("qss", [128, 16])
            act(qss, PB[0][:, 0:16], AF.Silu, [("pb", 0)], ["qss"])
            ts("dve", SQ[:, h, :], qss, 128.0 ** -0.5, ALU.mult, ["qss"], ksm)
            ths = wk("ths", [128, 16])
            act(ths, PB[0][:, 16:32], AF.Tanh, [("pb", 0)], ["ths"], scale=0.5)
            ts("dve", SFG[:, h, :], ths, lbv[:, 1, h:h + 1], ALU.mult, ["ths", "lbv"], ksm, lbv[:, 2, h:h + 1], ALU.add)
            ts("dve", SKK[:, h, :], ths, lbv[:, 3, h:h + 1], ALU.mult, ["ths", "lbv"], ksm, lbv[:, 1, h:h + 1], ALU.add)
            act(SOG[:, h, :], PB[0][:, 48:64], AF.Silu, [("pb", 0)], ksm)
            cp("act", vtok[:, h * 128:(h + 1) * 128], PB[1][0:16, 0:128], [("pb", 1)], kvtok)
        bufs = [carve(R_(i), 1024, [8, 128]) for i in range(5)]
        for s in range(NS):
            Sin, kin = bufs[s % 2]
            Snw, knw = bufs[2 + s % 2]
            kvt, kkv = bufs[4]
            dma("sp", Sin, d_shg[l, s].rearrange("h d e -> d h e"), W=kin)
            for half in range(2):
                mm(PB[2 + half], Sel[:, s, :], vtok[:, half * 512:(half + 1) * 512], True, True, kSel + kvtok, [("pb", 2 + half)])
            tt("dve", Snw, Sin, SFG[:, :, s:s + 1].broadcast_to([128, 8, 128]), ALU.mult, kin + ksm, knw)
            for half in range(2):
                tt("dve", kvt[:, half * 4:(half + 1) * 4, :], PB[2 + half].rearrange("p (h e) -> p h e", h=4),
                   SKK[:, half * 4:(half + 1) * 4, s:s + 1].broadcast_to([128, 4, 128]), ALU.mult, [("pb", 2 + half)] + ksm, kkv)
            tt("pool", Snw, Snw, kvt, ALU.add, knw + kkv, knw)
            for h in range(8):
                mm(PB[4][:, h * 16 + s:h * 16 + s + 1], Snw[:, h, :], SQ[:, h, s:s + 1], True, True, knw + ksm, [("pb", 4)])
            dma("sp", o_hgs[l, s].rearrange("h d e -> d h e"), Snw, R=knw)
        osb = wk("osbs", [128, 1, 128])
        cp("act", osb[:, 0, :], PB[4][:, 0:128], [("pb", 4)], ["osbs"])
        rs, rskey = rms_block(osb, 1, 128, ["osbs"], "os", PB[5], ("pb", 5), 1.0 / 128)
        tmo = wk("tmos", [128, 128])
        tt("dve", tmo, osb[:, 0, :], rs, ALU.mult, ["osbs", rskey], ["tmos"])
        stt(ohsT.rearrange("p h s -> p (h s)"), tmo, vl(V_HGN), SOG.rearrange("p h s -> p (h s)"), ALU.mult, ALU.mult,
            ["tmos", "vec"] + ksm, ["ohsT"])
        sm2, ksm2 = carve(R_(3), 1024, [4, 16, 16])
        DT, DEC, DTX, ZS = sm2[:, 0], sm2[:, 1], sm2[:, 2], sm2[:, 3]
        YS, kYS = carve(R_(7) + 512, 256, [16, 16])
        XC, kXC = carve(R_(7), 512, [32, 16])
        for b4 in range(4):
            wsl, wkey = wload(l, OFF_DTX + b4 * EL_DTX, EL_DTX)
            wv = wsl[:, 0:EL_DTX].rearrange("p (c k j) -> p c k j", c=4, k=8)
            for c4 in range(4):
                c = b4 * 4 + c4
                for k in range(8):
                    mm(PB[0][:, c * 16:(c + 1) * 16], wv[:, c4, k, :], hsT[:, k, :], k == 0, k == 7, [wkey, "hsT"], [("pb", 0)])
        tt("dve", DT, PB[0][:, 0:256].rearrange("p (c s) -> p c s", s=16), vl(V_DTB, 16).unsqueeze(2).broadcast_to([128, 16, 16]),
           ALU.add, [("pb", 0), "vec"], ksm2)
        act(DT, DT, AF.Exp, ksm2, ksm2)
        act(DT, DT, AF.Ln, ksm2 + ["ones_f"], ksm2, bias=ones_f[:, 0:1])
        Aex = wk("Aex", [128, 16])
        act(Aex, vl(V_ALG, 16), AF.Exp, ["vec"], ["Aex"])
        tt("dve", DEC, DT, Aex.unsqueeze(2).broadcast_to([128, 16, 16]), ALU.mult, ksm2 + ["Aex"], ksm2)
        act(DEC, DEC, AF.Exp, ksm2, ksm2, scale=-1.0)
        cvs, kcv = carve(R_(0), 2048, [32, 4, 16])
        dma("sp", cvs[:, :, 0:3, :], d_scv[l], W=kcv)
        for g in range(8):
            wsl, wkey = wload(l, OFF_GRP + g * EL_GRP, EL_GRP)
            wv = wsl[:, 0:EL_GRP].rearrange("p (k j c) -> p k j c", k=8, j=6)
            for j in range(6):
                for k in range(8):
                    mm(PB[1][:, j * 16:(j + 1) * 16], wv[:, k, j, :], hsT[:, k, :], k == 0, k == 7, [wkey, "hsT"], [("pb", 1)])
            for j in range(2):
                act(ZS[:, 2 * g + j, :], PB[1][:, j * 16:(j + 1) * 16], AF.Silu, [("pb", 1)], ksm2)
            for ci in range(4):
                ch = (2 * g + ci) if ci < 2 else (16 + g if ci == 2 else 24 + g)
                cp("act", cvs[:, ch, 3, :], PB[1][:, (2 + ci) * 16:(3 + ci) * 16], [("pb", 1)], kcv)
                acc = wk("caccs", [128, 16])
                ts("dve", acc, cvs[:, ch, 0, :], vl(V_CW + ch), ALU.mult, kcv + ["vec"], ["caccs"], vl(V_CB + ch), ALU.add)
                for jj in range(1, 4):
                    stt(acc, cvs[:, ch, jj, :], vl(V_CW + jj * 32 + ch), acc, ALU.mult, ALU.add, kcv + ["vec", "caccs"], ["caccs"])
                act(XC[:, ch, :], acc, AF.Silu, ["caccs"], kXC)
        dma("sp", o_cvs[l], cvs[:, :, 1:4, :], R=kcv)
        tt("dve", DTX, XC[:, 0:16, :], DT, ALU.mult, kXC + ksm2, ksm2)
        BCt, kBC = carve(R_(6), 1024, None, BF16, parts=16)
        XCb = wk("XCb", [128, 16, 16], BF16)
        cp("dve", XCb, XC[:, 16:32, :], kXC, ["XCb"])
        tpb = PB[6].bitcast(BF16)
        tpb2 = PB[7].bitcast(BF16)
        for c in range(16):
            dst = (tpb if c < 8 else tpb2)[0:16, (c % 8) * 128:(c % 8 + 1) * 128]
            tp(dst, XCb[:, c, :], ident_b, ["XCb", "ident_b"], [("pb", 6 + c // 8)])
        cp("act", BCt[:, 0:1024], tpb[0:16, 0:1024], [("pb", 6)], kBC)
        cp("act", BCt[:, 1024:2048], tpb2[0:16, 0:1024], [("pb", 7)], kBC)
        hb = [carve(R_(0) + i * 512, 512, [4, 128]) for i in range(2)] + [carve(R_(1) + i * 512, 512, [4, 128]) for i in range(2)]
        tb, ktb = carve(R_(2), 512, [4, 128])
        it = 0
        for s in range(NS):
            for q in range(4):
                Hin, kin = hb[it % 2]
                Hnw, knw = hb[2 + it % 2]
                it += 1
                dma("sp", Hin, d_ssm[l, s, 8 * q:8 * q + 8].rearrange("(c hh) p n -> (hh p) c n", hh=2), W=kin)
                mm(PB[2][:, 0:256], Sel[:, s, :], BCt[:, (2 * q) * 128:(2 * q + 2) * 128], True, True, kSel + kBC, [("pb", 2)])
                mm(PB[3][:, 0:256], Sel[:, s, :], BCt[:, 1024 + (2 * q) * 128:1024 + (2 * q + 2) * 128], True, True, kSel + kBC, [("pb", 3)])
                cs = slice(4 * q, 4 * q + 4)
                tt("dve", Hnw, Hin, DEC[:, cs, s:s + 1].broadcast_to([128, 4, 128]), ALU.mult, kin + ksm2, knw)
                Bbc = PB[2][:, 0:256].rearrange("p (g n) -> p g n", g=2).unsqueeze(2).broadcast_to([128, 2, 2, 128])
                Cbc = PB[3][:, 0:256].rearrange("p (g n) -> p g n", g=2).unsqueeze(2).broadcast_to([128, 2, 2, 128])
                tb4 = tb.rearrange("p (g c) n -> p g c n", g=2)
                tt("dve", tb4, Bbc, DTX[:, cs, s:s + 1].rearrange("p (g c) o -> p g c o", g=2).broadcast_to([128, 2, 2, 128]), ALU.mult,
                   [("pb", 2)] + ksm2, ktb)
                tt("pool", Hnw, Hnw, tb, ALU.add, knw + ktb, knw)
                tt("dve", tb4, Cbc, Hnw.rearrange("p (g c) n -> p g c n", g=2), ALU.mult, [("pb", 3)] + knw, ktb)
                add("dve", lambda e, s=s, cs=cs: e.tensor_reduce(out=YS[:, cs, s], in_=tb, axis=mybir.AxisListType.X, op=ALU.add),
                    ktb, kYS)
                dma("sp", o_sss[l, s, 8 * q:8 * q + 8].rearrange("(c hh) p n -> (hh p) c n", hh=2), Hnw, R=knw)
        yy = wk("yys", [128, 16, 16])
        tt("dve", yy, XC[:, 0:16, :], vl(V_DSK, 16).unsqueeze(2).broadcast_to([128, 16, 16]), ALU.mult, kXC + ["vec"], ["yys"])
        tt("dve", yy, yy, YS, ALU.add, ["yys"] + kYS, ["yys"])
        tt("dve", yy, yy, ZS, ALU.mult, ["yys"] + ksm2, ["yys"])
        sqs = wk("sqys", [128, 16, 16])
        act(sqs, yy, AF.Square, ["yys"], ["sqys"])
        for g in range(8):
            for j in range(2):
                mm(PB[5][:, g * 16:(g + 1) * 16], ones_f, sqs[:, 2 * g + j, :], j == 0, j == 1, ["sqys", "ones_f"], [("pb", 5)])
        lnv = wk("lnvs", [128, 128])
        act(lnv, PB[5][:, 0:128], AF.Ln, [("pb", 5)], ["lnvs"], scale=1.0 / 256, bias=epsT)
        act(lnv, lnv, AF.Exp, ["lnvs"], ["lnvs"], scale=-0.5)
        tt("dve", yy.rearrange("p (g j) s -> p g j s", j=2), yy.rearrange("p (g j) s -> p g j s", j=2),
           lnv.rearrange("p (g s) -> p g s", s=16).unsqueeze(2).broadcast_to([128, 8, 2, 16]), ALU.mult, ["yys", "lnvs"], ["yys"])
        tt("dve", ysT, yy, vl(V_SN, 16).unsqueeze(2).broadcast_to([128, 16, 16]), ALU.mult, ["yys", "vec"], ["ysT"])


    for l in range(L):
        vl = lambda off, n=1: vec[:, l, off:off + n]
        if stage < 1:
            break
        mod_ps = [PB[6], PB[7]]
        for blk in range(12):
            wsl, wkey = wload(l, OFF_ADA + blk * EL_ADA, EL_ADA)
            wv = wsl[:, 0:EL_ADA].rearrange("p (m k c) -> p m k c", m=4, k=8)
            for m in range(4):
                ch = blk * 4 + m
                ps = mod_ps[ch // 24][:, (ch % 24) * 17:(ch % 24) * 17 + 17]
                for k in range(8):
                    mm(ps, wv[:, m, k, :], scT[:, k, :], k == 0, k == 7, [wkey, "scT"], [("pb", 6 + ch // 24)])
        for hh in range(2):
            tt("dve", modT[:, hh * 24:(hh + 1) * 24, :],
               mod_ps[hh][:, 0:24 * 17].rearrange("p (c t) -> p c t", t=17),
               vl(V_BADA + hh * 24, 24).unsqueeze(2).broadcast_to([128, 24, 17]), ALU.add,
               [("pb", 6 + hh), "vec"], ["modT"])
        mods = wk("mods", [128, 6, 8, 1 + NS])
        for (dst, sc_off, nrm) in ((0, 8, V_NMIX), (3, 32, V_NMLP)):
            ts("dve", mods[:, dst], modT[:, sc_off:sc_off + 8, :], 1.0, ALU.add, ["modT"], ["mods"])
            tt("dve", mods[:, dst], mods[:, dst], vl(nrm, 8).unsqueeze(2).broadcast_to([128, 8, 1 + NS]), ALU.mult,
               ["mods", "vec"], ["mods"])
        cp("dve", mods[:, 1], modT[:, 0:8, :], ["modT"], ["mods"])
        cp("dve", mods[:, 4], modT[:, 24:32, :], ["modT"], ["mods"])
        ts("dve", mods[:, 2], modT[:, 16:24, :], 0.5, ALU.mult, ["modT"], ["mods"])
        cp("dve", mods[:, 5], modT[:, 40:48, :], ["modT"], ["mods"])
        lbv = wk("lbv", [128, 4, 8])
        cp("dve", lbv[:, 0], lbs[:, l, :], ["lbs"], ["lbv"])
        ts("dve", lbv[:, 1], lbs[:, l, :], -0.5, ALU.mult, ["lbs"], ["lbv"], 0.5, ALU.add)
        tt("dve", lbv[:, 2], lbv[:, 1], lbs[:, l, :], ALU.add, ["lbv", "lbs"], ["lbv"])
        ts("dve", lbv[:, 3], lbv[:, 1], -1.0, ALU.mult, ["lbv"], ["lbv"])
        Abc = wk("Abc", [128, 32])
        act(Abc, tmv[:, l, 32:64], AF.Exp, ["tmv"], ["Abc"])
        ts("dve", Abc, Abc, -1.0, ALU.mult, ["Abc"], ["Abc"])
        hbm = wk("hbm", [128, 16])
        ts("dve", hbm, vl(V_BM, 16), 0.5, ALU.mult, ["vec"], ["hbm"])

        if do_sample and stage >= 2:
            sample_pass(l)
        for hf in range(NHALF):
            if stage < 2:
                break
            t0 = hf * TH
            smp = do_sample and hf == 0
            xkeys = lambda nb: [("xT", (t0 + nb * 512) // 128 + i) for i in range(4)]

            def norm_mod(ai, bi, dstkey):
                for nb in range(NB):
                    cs = slice(t0 + nb * 512, t0 + nb * 512 + 512)
                    rs, rskey = rms_block(xT[:, :, cs], 8, 512, xkeys(nb), "n", PB[0], ("pb", 0), 1.0 / D)
                    tmp = wk("ntmp", [128, 4, 512])
                    for hh in range(2):
                        tt("dve", tmp, xT[:, hh * 4:(hh + 1) * 4, cs], rs.unsqueeze(1).broadcast_to([128, 4, 512]), ALU.mult,
                           xkeys(nb) + [rskey], ["ntmp"])
                        for k4 in range(4):
                            k = hh * 4 + k4
                            act(hT[:, k, nb * 512:(nb + 1) * 512], tmp[:, k4, :], AF.Identity, ["ntmp", "mods"],
                                [(dstkey, k, nb)], scale=mods[:, ai, k, 0:1], bias=mods[:, bi, k, 0:1])
                if smp and ai == 3:
                    rs, rskey = rms_block(xsT, 8, NS, ["xsT"], "ns", PB[0], ("pb", 0), 1.0 / D)
                    tmp = wk("nstmp", [128, 8, NS])
                    tt("dve", tmp, xsT, rs.unsqueeze(1).broadcast_to([128, 8, NS]), ALU.mult, ["xsT", rskey], ["nstmp"])
                    tt("dve", tmp, tmp, mods[:, ai, :, 1:], ALU.mult, ["nstmp", "mods"], ["nstmp"])
                    tt("dve", hsT, tmp, mods[:, bi, :, 1:], ALU.add, ["nstmp", "mods"], ["hsT"])

            norm_mod(0, 1, "hT")
            hkeys = lambda nb: [("hT", k, nb) for k in range(8)]

            for h in range(8):
                if stage < 3:
                    break
                wsl, wkey = wload(l, OFF_HEAD + h * EL_HEAD, EL_HEAD)
                wv = wsl[:, 0:EL_HEAD].rearrange("p (k j c) -> p k j c", k=8, j=4)
                for nb in range(NB):
                    bs = slice(nb * 512, (nb + 1) * 512)
                    for j in range(4):
                        for k in range(8):
                            mm(PB[j], wv[:, k, j, :], hT[:, k, bs], k == 0, k == 7, [wkey] + hkeys(nb), [("pb", j)])
                    qs = wk("qs", [128, 512])
                    act(qs, PB[0], AF.Silu, [("pb", 0)], ["qs"])
                    th = wk("th", [128, 512])
                    act(th, PB[1], AF.Tanh, [("pb", 1)], ["th"], scale=0.5)
                    vTb = wk("vTb", [128, 512], BF16)
                    cp("act", vTb, PB[2], [("pb", 2)], ["vTb"])
                    sog = wk("sog", [128, 512])
                    act(sog, PB[3], AF.Silu, [("pb", 3)], ["sog"])
                    logf = wk("logf", [128, 512])
                    act(logf, th, AF.Ln, ["th", "lbv"], ["logf"], scale=lbv[:, 1, h:h + 1], bias=lbv[:, 2, h:h + 1])
                    kk = wk("kk", [128, 512])
                    ts("dve", kk, th, lbv[:, 3, h:h + 1], ALU.mult, ["th", "lbv"], ["kk"], lbv[:, 1, h:h + 1], ALU.add)
                    bcs = wk("bcs", [128, 512])
                    for t in range(8):
                        add("dve", lambda e, t=t: e.tensor_tensor_scan(out=bcs[:, t * 64:(t + 1) * 64],
                                                                        data0=ones_f[:, 0:64], data1=logf[:, t * 64:(t + 1) * 64],
                                                                        initial=0.0, op0=ALU.mult, op1=ALU.add),
                            ["logf", "ones_f"], ["bcs"])
                    sc4 = wk("sc4", [128, 6, 8])
                    bmv = bcs[:, 31:512:64]
                    bev = bcs[:, 63:512:64]
                    ts("dve", sc4[:, 0], bmv, -1.0, ALU.mult, ["bcs"], ["sc4"])
                    cp("dve", sc4[:, 1], bmv, ["bcs"], ["sc4"])
                    tt("dve", sc4[:, 2], bev, bmv, ALU.subtract, ["bcs"], ["sc4"])
                    act(sc4[:, 3], bmv, AF.Exp, ["bcs"], ["sc4"])
                    act(sc4[:, 4], bev, AF.Exp, ["bcs"], ["sc4"])
                    act(sc4[:, 5], sc4[:, 2], AF.Exp, ["sc4"], ["sc4"])
                    E1 = wk("E1", [128, 512])
                    E2 = wk("E2", [128, 512])
                    qtT = wk("qtT", [128, 512], BF16)
                    ktT = wk("ktT", [128, 512], BF16)
                    for t in range(8):
                        c4 = slice(t * 64, (t + 1) * 64)
                        act(E1[:, c4], bcs[:, c4], AF.Exp, ["bcs", "sc4"], ["E1"], bias=sc4[:, 0, t:t + 1])
                        act(E2[:, c4], bcs[:, c4], AF.Exp, ["bcs", "sc4"], ["E2"], scale=-1.0, bias=sc4[:, 1, t:t + 1])
                    stt(qtT, qs, 128.0 ** -0.5, E1, ALU.mult, ALU.mult, ["qs", "E1"], ["qtT"])
                    tt("dve", ktT, kk, E2, ALU.mult, ["kk", "E2"], ["ktT"])
                    for t in range(8):
                        if stage < 4:
                            break
                        c0 = t * 64
                        c4 = slice(c0, c0 + 64)
                        tpb = PB[7].bitcast(BF16)
                        tp(tpb[0:64, 0:128], ktT[:, c4], ident_b, ["ktT", "ident_b"], [("pb", 7)])
                        tp(tpb[0:64, 128:256], vTb[:, c4], ident_b, ["vTb", "ident_b"], [("pb", 7)])
                        kv = wk("kv_tok", [64, 256], BF16)
                        cp("act", kv, tpb[0:64, 0:256], [("pb", 7)], ["kv_tok"])
                        scp = PB[6][0:64, 0:64]
                        mm(scp[0:32, :], ktT[:, c0:c0 + 32], qtT[:, c4], True, True, ["ktT", "qtT"], [("pb", 6)])
                        mm(scp[32:64, 32:64], ktT[:, c0 + 32:c0 + 64], qtT[:, c0 + 32:c0 + 64],
                           True, True, ["ktT", "qtT"], [("pb", 6)])
                        sk = "scb%d" % (t % 2)
                        scb = wk(sk, [64, 64], BF16)
                        if l == 0 and hf == 0 and h == 0 and nb == 0 and t < 2:
                            memset("pool", scb, 0.0, [sk])
                        tt("dve", scb[0:32, :], scp[0:32, :], U_f[0:32, 0:64], ALU.mult, [("pb", 6), "U_f"], [sk])
                        tt("dve", scb[32:64, 32:64], scp[32:64, 32:64], U_f[32:64, 32:64], ALU.mult,
                           [("pb", 6), "U_f"], [sk])
                        Sb = wk("Sb", [128, 128], BF16)
                        ts("dve", Sb, Shg[:, h, :], sc4[:, 3, t:t + 1], ALU.mult, [("Shg", h), "sc4"], ["Sb"])
                        mm(PB[4][:, c4], kv[:, 128:256], scb, True, False, ["kv_tok", sk], [("pb", 4)])
                        mm(PB[4][:, c4], Sb, qtT[:, c4], False, True, ["Sb", "qtT"], [("pb", 4)])
                        kvp = PB[6][:, 128:256]
                        mm(kvp, kv[:, 0:128], kv[:, 128:256], True, True, ["kv_tok"], [("pb", 6)])
                        ts("dve", Shg[:, h, :], Shg[:, h, :], sc4[:, 4, t:t + 1], ALU.mult, [("Shg", h), "sc4"], [("Shg", h)])
                        stt(Shg[:, h, :], kvp, sc4[:, 5, t:t + 1], Shg[:, h, :], ALU.mult, ALU.add,
                            [("pb", 6), "sc4", ("Shg", h)], [("Shg", h)])
                    if stage < 5:
                        continue
                    osb = wk("osb", [128, 1, 512])
                    cp("act", osb[:, 0, :], PB[4], [("pb", 4)], ["osb"])
                    rs, rskey = rms_block(osb, 1, 512, ["osb"], "o", PB[5], ("pb", 5), 1.0 / 128)
                    tmo = wk("tmo", [128, 512])
                    tt("dve", tmo, osb[:, 0, :], rs, ALU.mult, ["osb", rskey], ["tmo"])
                    stt(ohT[:, h, bs], tmo, vl(V_HGN), sog, ALU.mult, ALU.mult, ["tmo", "vec", "sog"], [("BIG", h)])
                if hf == NHALF - 1:
                    dma("sp", o_hgp[l, h], Shg[:, h, :], R=[("Shg", h)])
                    if l + 1 < L:
                        memset("pool", Shg[:, h, :], 0.0, [("Shg", h)])

            if stage >= 6 and not skipB:
                phase_B(l, hf, t0, smp, hkeys)
            else:
                for c in range(16):
                    memset("pool", yT[:, c, :], 0.0, [("BIG", 8 + c)])
                if smp:
                    memset("pool", ysT, 0.0, ["ysT"])
            if stage < 7:
                continue
            blocks = [(nb, 512) for nb in range(NB)] + ([("s", NS)] if smp else [])
            for c in range(8):
                wsl, wkey = wload(l, OFF_C + c * EL_C, EL_C)
                wv = wsl[:, 0:EL_C].rearrange("p (k j) -> p k j", k=40)
                for (nb, n) in blocks:
                    if nb == "s":
                        oh_, y_, h_, u_ = ohsT, ysT, hsT, usT
                        bs = slice(0, NS)
                        kr = lambda k: ["ohsT"]
                        ky = lambda k: ["ysT"]
                        kh = ["hsT"]
                        ku = ["usT"]
                    else:
                        oh_, y_, h_, u_ = ohT, yT, hT, uT
                        bs = slice(nb * 512, (nb + 1) * 512)
                        kr = lambda k: [("BIG", k)]
                        ky = lambda k: [("BIG", 8 + k)]
                        kh = hkeys(nb)
                        ku = [("BIG", 24 + c)]
                    for k in range(8):
                        mm(PB[0][:, 0:n], wv[:, k, :], oh_[:, k, bs], k == 0, k == 7, [wkey] + kr(k), [("pb", 0)])
                    for k in range(16):
                        mm(PB[1][:, 0:n], wv[:, 8 + k, :], y_[:, k, bs], k == 0, k == 15, [wkey] + ky(k), [("pb", 1)])
                    for k in range(8):
                        mm(PB[2][:, 0:n], wv[:, 24 + k, :], h_[:, k, bs], k == 0, k == 7, [wkey] + kh, [("pb", 2)])
                    for k in range(8):
                        mm(PB[3][:, 0:n], wv[:, 32 + k, :], h_[:, k, bs], k == 0, k == 7, [wkey] + kh, [("pb", 3)])
                    thA = wk("thA", [128, 512])
                    thB = wk("thB", [128, 512])
                    act(thA[:, 0:n], PB[2][:, 0:n], AF.Tanh, [("pb", 2), "hbm"], ["thA"], scale=0.5, bias=hbm[:, c:c + 1])
                    act(thB[:, 0:n], PB[3][:, 0:n], AF.Tanh, [("pb", 3), "hbm"], ["thB"], scale=0.5, bias=hbm[:, 8 + c:9 + c])
                    t1 = wk("t1", [128, 512])
                    t2 = wk("t2", [128, 512])
                    stt(t1[:, 0:n], thA[:, 0:n], 1.0, PB[0][:, 0:n], ALU.add, ALU.mult, ["thA", ("pb", 0)], ["t1"])
                    stt(t2[:, 0:n], thB[:, 0:n], 1.0, PB[1][:, 0:n], ALU.add, ALU.mult, ["thB", ("pb", 1)], ["t2"])
                    tt("pool", u_[:, c, bs], t1[:, 0:n], t2[:, 0:n], ALU.add, ["t1", "t2"], ku)
            rot = 0
            for b2 in range(2):
                wsl, wkey = wload(l, OFF_WO + b2 * EL_WO, EL_WO)
                wv = wsl[:, 0:EL_WO].rearrange("p (c k j) -> p c k j", c=4, k=8)
                for cc in range(4):
                    c2 = b2 * 4 + cc
                    for (nb, n) in blocks:
                        pbi = 4 + rot % 2
                        rot += 1
                        if nb == "s":
                            for k in range(8):
                                mm(PB[pbi][:, 0:n], wv[:, cc, k, :], usT[:, k, :], k == 0, k == 7, [wkey, "usT"], [("pb", pbi)])
                            t3 = wk("t3s", [128, NS])
                            tt("dve", t3, PB[pbi][:, 0:n], mods[:, 2, c2, 1:], ALU.mult, [("pb", pbi), "mods"], ["t3s"])
                            tt("dve", xsT[:, c2, :], xsT[:, c2, :], t3, ALU.add, ["t3s", "xsT"], ["xsT"])
                        else:
                            bs = slice(nb * 512, (nb + 1) * 512)
                            xs_ = slice(t0 + nb * 512, t0 + (nb + 1) * 512)
                            for k in range(8):
                                mm(PB[pbi], wv[:, cc, k, :], uT[:, k, bs], k == 0, k == 7, [wkey, ("BIG", 24 + k)], [("pb", pbi)])
                            stt(xT[:, c2, xs_], PB[pbi], mods[:, 2, c2, 0:1], xT[:, c2, xs_], ALU.mult, ALU.add,
                                [("pb", pbi), "mods"] + xkeys(nb), xkeys(nb))
            if stage < 8:
                continue
            norm_mod(3, 4, "hT")
            rot = 0
            for b8 in range(8):
                wsl, wkey = wload(l, OFF_UP + b8 * EL_UP, EL_UP)
                wv = wsl[:, 0:EL_UP].rearrange("p (m k j) -> p m k j", m=4, k=8)
                for mi in range(4):
                    m = b8 * 4 + mi
                    for (nb, n) in blocks:
                        pbi = rot % 4
                        rot += 1
                        if nb == "s":
                            h_, a_, bs, kh, ka = hsT, asT, slice(0, NS), ["hsT"], ["asT"]
                        else:
                            h_, a_, bs, kh, ka = hT, aT, slice(nb * 512, (nb + 1) * 512), hkeys(nb), [("BIG", m)]
                        for k in range(8):
                            mm(PB[pbi][:, 0:n], wv[:, mi, k, :], h_[:, k, bs], k == 0, k == 7, [wkey] + kh, [("pb", pbi)])
                        rl = wk("rl%d" % (rot % 2), [128, 512])
                        act(rl[:, 0:n], PB[pbi][:, 0:n], AF.Relu, [("pb", pbi)], ["rl%d" % (rot % 2)])
                        tt("pool", a_[:, m, bs], rl[:, 0:n], rl[:, 0:n], ALU.mult, ["rl%d" % (rot % 2)], ka)
            for c in range(8):
                wsl, wkey = wload(l, OFF_DN + c * EL_DN, EL_DN)
                wv = wsl[:, 0:EL_DN].rearrange("p (m j) -> p m j", m=32)
                for (nb, n) in blocks:
                    pbi = 4 + rot % 4
                    rot += 1
                    if nb == "s":
                        for m in range(32):
                            mm(PB[pbi][:, 0:n], wv[:, m, :], asT[:, m, :], m == 0, m == 31, [wkey, "asT"], [("pb", pbi)])
                        t3 = wk("t3s", [128, NS])
                        tt("dve", t3, PB[pbi][:, 0:n], mods[:, 5, c, 1:], ALU.mult, [("pb", pbi), "mods"], ["t3s"])
                        tt("dve", xsT[:, c, :], xsT[:, c, :], t3, ALU.add, ["t3s", "xsT"], ["xsT"])
                    else:
                        bs = slice(nb * 512, (nb + 1) * 512)
                        xs_ = slice(t0 + nb * 512, t0 + (nb + 1) * 512)
                        for m in range(32):
                            mm(PB[pbi], wv[:, m, :], aT[:, m, bs], m == 0, m == 31, [wkey, ("BIG", m)], [("pb", pbi)])
                        stt(xT[:, c, xs_], PB[pbi], mods[:, 5, c, 0:1], xT[:, c, xs_], ALU.mult, ALU.add,
                            [("pb", pbi), "mods"] + xkeys(nb), xkeys(nb))
    if stage >= 9:
        for nb in range(NTOK // 512):
            cs = slice(nb * 512, (nb + 1) * 512)
            xk = [("xT", nb * 4 + i) for i in range(4)]
            rs, rskey = rms_block(xT[:, :, cs], 8, 512, xk, "n", PB[0], ("pb", 0), 1.0 / D)
            tmp = wk("ntmp", [128, 4, 512])
            for hh in range(2):
                tt("dve", tmp, xT[:, hh * 4:(hh + 1) * 4, cs], rs.unsqueeze(1).broadcast_to([128, 4, 512]), ALU.mult, xk + [rskey], ["ntmp"])
                tt("pool", tmp, tmp, vec[:, 0, V_NF + hh * 4:V_NF + hh * 4 + 4].unsqueeze(2).broadcast_to([128, 4, 512]), ALU.mult,
                   ["ntmp", "vec"], ["ntmp"])
                dma("sp", o_yT[:, hh * 4:(hh + 1) * 4, cs], tmp, R=["ntmp"])
        if do_sample:
            rs, rskey = rms_block(xsT, 8, NS, ["xsT"], "ns", PB[0], ("pb", 0), 1.0 / D)
            tmp = wk("nstmp", [128, 8, NS])
            tt("dve", tmp, xsT, rs.unsqueeze(1).broadcast_to([128, 8, NS]), ALU.mult, ["xsT", rskey], ["nstmp"])
            tt("dve", tmp, tmp, vec[:, 0, V_NF:V_NF + 8].unsqueeze(2).broadcast_to([128, 8, NS]), ALU.mult,
               ["nstmp", "vec"], ["nstmp"])
            dma("sp", o_ysT, tmp, R=["nstmp"])
    if dbg:
        for nm, ap_, shp, dt_ in (("BIG", BIG, [128, 32, TH], BF16), ("hT", hT, [128, KC, TH], BF16),
                                  ("modT", modT, [128, 48, 1 + NS], F32), ("xTd", xT, [128, KC, NTOK], F32),
                                  ("Shg", Shg, [128, 8, 128], F32), ("HT", HT, [128, 2048], F32)):
            dd = nc.dram_tensor("dbg_" + nm, shp, dt_, kind="ExternalOutput").ap()
            dma("sp", dd, ap_, R=list(S.last_w.keys()))
        for nm, ap_ in WK.items():
            dd = nc.dram_tensor("dbgw_" + nm, list(ap_.shape), ap_.dtype, kind="ExternalOutput").ap()
            dma("sp", dd, ap_, R=list(S.last_w.keys()))
    S.final_wait()
    S.emit()
    return nc


def _fm(v):
    return np.ascontiguousarray(v.reshape(-1, 128).T)


def pack_weights(inp, L):
    w_in, w_ada = inp["w_in"], inp["w_ada"]
    out = np.empty((L, 128, EL_TOT), np.float32)
    for l in range(L):
        wi = w_in[l].reshape(8, 128, -1)
        o = out[l]
        for h in range(8):
            blk = np.stack([wi[:, :, j * 1024 + h * 128: j * 1024 + (h + 1) * 128] for j in range(4)], axis=2)
            o[:, OFF_HEAD + h * EL_HEAD: OFF_HEAD + (h + 1) * EL_HEAD] = blk.transpose(1, 0, 2, 3).reshape(128, -1)
        for g in range(8):
            cols = [4096 + 256 * g, 4096 + 256 * g + 128, 6144 + 256 * g, 6144 + 256 * g + 128,
                    6144 + 2048 + 128 * g, 6144 + 3072 + 128 * g]
            blk = np.stack([wi[:, :, c:c + 128] for c in cols], axis=2)
            o[:, OFF_GRP + g * EL_GRP: OFF_GRP + (g + 1) * EL_GRP] = blk.transpose(1, 0, 2, 3).reshape(128, -1)
        o[:, OFF_DT:OFF_DT + EL_DT] = wi[:, :, 10240:10272].transpose(1, 0, 2).reshape(128, -1)
        wa = inp["w_br_a"][l].reshape(8, 128, 1024)
        wb = inp["w_br_b"][l].reshape(16, 128, 1024)
        for c in range(8):
            cs = slice(c * 128, (c + 1) * 128)
            blk = np.concatenate([wa[:, :, cs], wb[:, :, cs], wi[:, :, 10272 + c * 128:10272 + (c + 1) * 128],
                                  wi[:, :, 11296 + c * 128:11296 + (c + 1) * 128]], axis=0)
            o[:, OFF_C + c * EL_C: OFF_C + (c + 1) * EL_C] = blk.transpose(1, 0, 2).reshape(128, -1)
        wo = inp["w_out"][l].reshape(8, 128, 8, 128)
        for b in range(2):
            blk = wo[:, :, b * 4:(b + 1) * 4, :].transpose(1, 2, 0, 3)
            o[:, OFF_WO + b * EL_WO: OFF_WO + (b + 1) * EL_WO] = blk.reshape(128, -1)
        wu = inp["w_up"][l].reshape(8, 128, 32, 128)
        for b in range(8):
            blk = wu[:, :, b * 4:(b + 1) * 4, :].transpose(1, 2, 0, 3)
            o[:, OFF_UP + b * EL_UP: OFF_UP + (b + 1) * EL_UP] = blk.reshape(128, -1)
        wd = inp["w_down"][l].reshape(32, 128, 8, 128)
        for c in range(8):
            blk = wd[:, :, c, :].transpose(1, 0, 2)
            o[:, OFF_DN + c * EL_DN: OFF_DN + (c + 1) * EL_DN] = blk.reshape(128, -1)
        wad = w_ada[l].reshape(8, 128, 48, 128)
        for b in range(12):
            blk = wad[:, :, b * 4:(b + 1) * 4, :].transpose(1, 2, 0, 3)
            o[:, OFF_ADA + b * EL_ADA: OFF_ADA + (b + 1) * EL_ADA] = blk.reshape(128, -1)
        wdt = wi[:, :, 10240:10272]
        wexp = np.repeat(wdt, 64, axis=2).reshape(8, 128, 16, 128)
        for b in range(4):
            blk = wexp[:, :, b * 4:(b + 1) * 4, :].transpose(1, 2, 0, 3)
            o[:, OFF_DTX + b * EL_DTX: OFF_DTX + (b + 1) * EL_DTX] = blk.reshape(128, -1)
    return out


def pack_vecs(inp, L):
    vec = np.zeros((128, L, NV), np.float32)
    tmv = np.zeros((128, L, 64), np.float32)
    for l in range(L):
        vec[:, l, V_NMIX:V_NMIX + 8] = _fm(inp["norm_mix"][l])
        vec[:, l, V_NMLP:V_NMLP + 8] = _fm(inp["norm_mlp"][l])
        vec[:, l, V_BADA:V_BADA + 48] = _fm(inp["b_ada"][l])
        for j in range(4):
            vec[:, l, V_CW + j * 32:V_CW + (j + 1) * 32] = _fm(inp["conv_w"][l, j])
        vec[:, l, V_CB:V_CB + 32] = _fm(inp["conv_b"][l])
        vec[:, l, V_SN:V_SN + 16] = _fm(inp["ssm_norm"][l])
        vec[:, l, V_BM:V_BM + 16] = _fm(inp["b_merge"][l])
        vec[:, l, V_HGN] = inp["hg_norm"][l]
        vec[:, l, V_DSK:V_DSK + 16] = _fm(np.repeat(inp["d_skip"][l], 64))
        lb = inp["lower_bounds"][:L].reshape(L, 8, 128)
        vec[:, l, V_LB:V_LB + 8 * L] = lb.transpose(2, 0, 1).reshape(128, -1)
        vec[:, l, V_NF:V_NF + 8] = _fm(inp["norm_final"])
        vec[:, l, V_DTB:V_DTB + 16] = _fm(np.repeat(inp["dt_bias"][l], 64))
        vec[:, l, V_ALG:V_ALG + 16] = _fm(np.repeat(inp["a_log"][l], 64))
        tmv[:, l, 0:32] = inp["dt_bias"][l][None, :]
        tmv[:, l, 32:64] = inp["a_log"][l][None, :]
    return vec, tmv


_L, _NTOK, _TH = 4, 2048, 512


def kernel(**inp):
    inp = {k: np.asarray(v) for k, v in inp.items()}
    L = _L
    wpk = pack_weights(inp, L)
    vec, tmv = pack_vecs(inp, L)
    in_maps = []
    for b in range(8):
        m = {}
        xp = inp["x_prompt"][b]
        m["xT"] = np.ascontiguousarray(xp.T.reshape(8, 128, _NTOK).transpose(1, 0, 2))
        xs = inp["x_sample"][16 * b:16 * b + 16, 0]
        m["xsT"] = np.ascontiguousarray(xs.T.reshape(8, 128, 16).transpose(1, 0, 2))
        c = np.concatenate([inp["c_prompt"][b:b + 1], inp["c_sample"][16 * b:16 * b + 16]], 0)
        m["cT"] = np.ascontiguousarray(c.T.reshape(8, 128, 17).transpose(1, 0, 2))
        m["wpk"] = wpk
        m["vec"] = vec
        m["tmv"] = tmv
        m["s_hg"] = np.ascontiguousarray(inp["state_hgrn"][:, 16 * b:16 * b + 16])
        m["s_ssm"] = np.ascontiguousarray(inp["state_ssm"][:, 16 * b:16 * b + 16])
        cv = inp["state_conv"][:, 16 * b:16 * b + 16]
        m["s_cvT"] = np.ascontiguousarray(cv.reshape(L, 16, 3, 32, 128).transpose(0, 4, 3, 2, 1))
        in_maps.append(m)
    nc = build(L, _NTOK, _TH, do_sample=True)
    res = run_bass_kernel_spmd(nc, in_maps, core_ids=list(range(8)))
    R = res.results
    y_prompt = np.stack([r["o_yT"].transpose(2, 1, 0).reshape(_NTOK, 1024) for r in R], 0)
    y_sample = np.concatenate([r["o_ysT"].transpose(2, 1, 0).reshape(16, 1, 1024) for r in R], 0)
    hg_p = np.stack([r["o_hgp"] for r in R], 1)
    ss_p = np.stack([r["o_ssp"].reshape(L, 128, 32, 64).transpose(0, 2, 3, 1) for r in R], 1)
    cv_p = np.stack([r["o_cvp"].transpose(0, 3, 2, 1).reshape(L, 3, 4096) for r in R], 1)
    hg_s = np.concatenate([r["o_hgs"] for r in R], 1)
    ss_s = np.concatenate([r["o_sss"] for r in R], 1)
    cv_s = np.concatenate([r["o_cvs"].transpose(0, 4, 3, 2, 1).reshape(L, 16, 3, 4096) for r in R], 1)
    f = lambda a: np.ascontiguousarray(a, dtype=np.float32)
    return (f(y_prompt), f(y_sample), f(hg_p), f(ss_p), f(cv_p), f(hg_s), f(ss_s), f(cv_s))
```

```python
import os
import numpy as np
import concourse.bass as bass
import concourse.mybir as mybir
from concourse.bass_utils import run_bass_kernel_spmd

F32 = mybir.dt.float32
BF16 = mybir.dt.bfloat16
AF = mybir.ActivationFunctionType
ALU = mybir.AluOpType

ENGS = ["pe", "act", "dve", "pool", "sp"]
NSLOT = 28


class Sched:
    def __init__(self, nc):
        self.nc = nc
        self.ops = []
        self.by_eng = {e: [] for e in ENGS}
        self.last_w = {}
        self.readers = {}
        self.slot_last = [None] * NSLOT
        self.next_slot = 0
        self.ndom = 4 + NSLOT
        self.dma_ops = []

    def add(self, eng, fn, R=(), W=(), dma=False):
        op = dict(id=len(self.ops), eng=eng, fn=fn, dma=dma, deps=set(), flag=False, raw=set())
        if dma:
            s = self.next_slot
            self.next_slot = (s + 1) % NSLOT
            op["dom"] = 4 + s
            prev = self.slot_last[s]
            if prev is not None:
                op["deps"].add(prev)
            self.slot_last[s] = op["id"]
            self.dma_ops.append(op["id"])
        else:
            op["dom"] = ENGS.index(eng)
        deps = op["deps"]
        for r in R:
            lw = self.last_w.get(r)
            if lw is not None:
                deps.add(lw)
                op["raw"].add(lw)
            rd = self.readers.setdefault(r, {})
            rd[("dma", op["id"]) if dma else eng] = op["id"]
        for w in W:
            lw = self.last_w.get(w)
            if lw is not None:
                deps.add(lw)
            for rd in self.readers.get(w, {}).values():
                if rd != op["id"]:
                    deps.add(rd)
            self.last_w[w] = op["id"]
            self.readers[w] = {}
        self.ops.append(op)
        self.by_eng[eng].append(op)
        return op["id"]

    def final_wait(self, eng="sp"):
        op = dict(id=len(self.ops), eng=eng, fn=None, dma=False, deps=set(self.dma_ops), flag=False,
                  dom=ENGS.index(eng), raw=set())
        self.ops.append(op)
        self.by_eng[eng].append(op)

    def emit(self):
        nc = self.nc
        ops = self.ops
        def relevant(op, d):
            dop = ops[d]
            if dop["dma"] or dop["eng"] != op["eng"]:
                return True
            return op["eng"] != "pe" and d in op["raw"]
        for op in ops:
            for d in op["deps"]:
                if relevant(op, d):
                    ops[d]["flag"] = True
        cnt = [0] * self.ndom
        for op in ops:
            if op["dma"] or op["flag"]:
                cnt[op["dom"]] += 1
                op["seq"] = cnt[op["dom"]]
        seen = {e: np.zeros(self.ndom, np.int64) for e in ENGS}
        for op in ops:
            e = op["eng"]
            sv = seen[e]
            need = {}
            for d in sorted(op["deps"]):
                dop = ops[d]
                if not relevant(op, d):
                    continue
                dm = dop["dom"]
                if dop["seq"] > sv[dm]:
                    need[dm] = max(need.get(dm, 0), dop["seq"])
                np.maximum(sv, dop["clk"], out=sv)
            op["waits"] = sorted(need.items())
            if op["flag"] or op["dma"]:
                clk = sv.copy()
                clk[op["dom"]] = max(clk[op["dom"]], op["seq"]) if not op["dma"] else op["seq"]
                op["clk"] = clk
        sems = [nc.alloc_semaphore(name=f"s_{i}") for i in range(self.ndom)]

        def run_eng(ename):
            def body(e):
                for op in self.by_eng[ename]:
                    for dm, s in op["waits"]:
                        e.wait_ge(sems[dm], s * 16 if dm >= 4 else s)
                    if op["fn"] is None:
                        continue
                    ins = op["fn"](e)
                    if op["dma"]:
                        ins.then_inc(sems[op["dom"]], 16)
                    elif op["flag"]:
                        ins.then_inc(sems[op["dom"]], 1)
            return body

        with nc.Block() as block:
            block.tensor(run_eng("pe"))
            block.scalar(run_eng("act"))
            block.vector(run_eng("dve"))
            block.gpsimd(run_eng("pool"))
            block.sync(run_eng("sp"))


D = 1024
KC = 8
NS = 16
EPS = 1e-6
EL_HEAD, EL_GRP, EL_DT, EL_C, EL_WO, EL_UP, EL_DN, EL_ADA = 4096, 6144, 256, 5120, 4096, 4096, 4096, 4096
OFF_HEAD = 0
OFF_GRP = OFF_HEAD + 8 * EL_HEAD
OFF_DT = OFF_GRP + 8 * EL_GRP
OFF_C = OFF_DT + EL_DT
OFF_WO = OFF_C + 8 * EL_C
OFF_UP = OFF_WO + 2 * EL_WO
OFF_DN = OFF_UP + 8 * EL_UP
OFF_ADA = OFF_DN + 8 * EL_DN
OFF_DTX = OFF_ADA + 12 * EL_ADA
EL_DTX = 4096
EL_TOT = OFF_DTX + 4 * EL_DTX
WSLOT = 6144
NWS = 2
V_NMIX, V_NMLP, V_BADA, V_CW, V_CB, V_SN, V_BM, V_HGN, V_DSK, V_LB, V_NF = 0, 8, 16, 64, 192, 224, 240, 256, 257, 273, 305
V_DTB, V_ALG = 320, 336
NV = 352


def build(L, NTOK, TH, do_sample=True, dbg=False, stage=99, skipB=False):
    nc = bass.Bass("TRN2", target_bir_lowering=False)
    S = Sched(nc)
    NHALF = NTOK // TH
    NB = TH // 512
    NT = NTOK // 128
    dram = lambda n, s, k, dt=F32: nc.dram_tensor(n, s, dt, kind=k).ap()
    d_xT = dram("xT", [128, KC, NTOK], "ExternalInput")
    d_xsT = dram("xsT", [128, KC, NS], "ExternalInput")
    d_cT = dram("cT", [128, KC, 1 + NS], "ExternalInput")
    d_w = dram("wpk", [L, 128, EL_TOT], "ExternalInput")
    d_vec = dram("vec", [128, L, NV], "ExternalInput")
    d_tmv = dram("tmv", [128, L, 64], "ExternalInput")
    d_shg = dram("s_hg", [L, NS, 8, 128, 128], "ExternalInput")
    d_ssm = dram("s_ssm", [L, NS, 32, 64, 128], "ExternalInput")
    d_scv = dram("s_cvT", [L, 128, 32, 3, NS], "ExternalInput")
    o_yT = dram("o_yT", [128, KC, NTOK], "ExternalOutput")
    o_ysT = dram("o_ysT", [128, KC, NS], "ExternalOutput")
    o_hgp = dram("o_hgp", [L, 8, 128, 128], "ExternalOutput")
    o_ssp = dram("o_ssp", [L, 128, 2048], "ExternalOutput")
    o_cvp = dram("o_cvp", [L, 128, 32, 3], "ExternalOutput")
    o_hgs = dram("o_hgs", [L, NS, 8, 128, 128], "ExternalOutput")
    o_sss = dram("o_sss", [L, NS, 32, 64, 128], "ExternalOutput")
    o_cvs = dram("o_cvs", [L, 128, 32, 3, NS], "ExternalOutput")

    sb = lambda n, s, dt=F32: nc.alloc_sbuf_tensor(n, s, dt).ap()
    xT = sb("xTs", [128, KC, NTOK])
    xsT = sb("xsTs", [128, KC, NS])
    hT = sb("hTs", [128, KC, TH], BF16)
    hsT = sb("hsTs", [128, KC, NS], BF16)
    BIG = sb("BIG", [128, 32, TH], BF16)
    ohT = BIG[:, 0:8, :]
    yT = BIG[:, 8:24, :]
    uT = BIG[:, 24:32, :]
    aT = BIG
    BIGs = sb("BIGs", [128, 32, NS], BF16)
    ohsT, ysT, usT, asT = BIGs[:, 0:8, :], BIGs[:, 8:24, :], BIGs[:, 24:32, :], BIGs
    Shg = sb("Shg", [128, 8, 128])
    HT = sb("HTs", [128, 2048])
    hist = sb("hist", [128, 32, 3])
    wring = sb("wring", [128, NWS, WSLOT], BF16)
    vec = sb("vecs", [128, L, NV])
    tmv = sb("tmvs", [128, L, 64])
    modT = sb("modT", [128, 48, 1 + NS])
    scT = sb("scT", [128, KC, 1 + NS], BF16)
    lbs = sb("lbs", [128, L, 8])
    ident_b = sb("ident_b", [128, 128], BF16)
    ident_f = sb("ident_f", [128, 128])
    ones_f = sb("ones_f", [128, 128])
    U_f = sb("U_f", [128, 128])
    U_b = sb("U_b", [128, 128], BF16)
    V_b = sb("V_b", [128, 128], BF16)
    ones_b = sb("ones_b", [128, 128], BF16)
    PB = [nc.alloc_psum_tensor(f"pb{i}", [128, 512], F32).ap() for i in range(8)]

    ALIAS = {"ysb": "nbuf", "sqy": "nbuf", "sqo": "nbuf", "osb": "nbuf", "tmo": "logf", "Ub1": "Ub0", "Ub2": "Ub0", "Ub3": "Ub0", "thA": "qs", "thB": "th", "t1": "sog", "t2": "logf", "rl0": "kk", "rl1": "bcs",
             "lnvn": "th", "rsn": "logf", "lnvo": "th", "rso": "logf", "sqn": "nbuf", "ntmp": "nbuf",
             "Lm": "qs", "EC": "th", "cacc": "sog", "lnvy": "th", "rsy": "logf", "E1": "th", "E2": "logf",
             "Rh": "qtT", "Rl": "ktT", "Mm": "vTb", "zs": "kk",
             "yys": "qs", "sqys": "th", "lnvs": "sog", "osbs": "logf", "tmos": "kk", "XCb": "bcs", "sqos": "bcs",
             "lnvos": "sog", "rsos": "qs", "nstmp": "bcs"}
    canon = lambda ks: [ALIAS.get(k, k) if isinstance(k, str) else k for k in ks]
    _sadd = S.add
    S.add = lambda eng, fn, R=(), W=(), dma=False: _sadd(eng, fn, canon(R), canon(W), dma)

    def add(eng, fn, R=(), W=()):
        return S.add(eng, fn, R, W)

    def dma(eng, out, in_, R=(), W=()):
        return S.add(eng, lambda e: e.dma_start(out=out, in_=in_), R, W, dma=True)

    def mm(out, lhsT, rhs, start, stop, R, W):
        return S.add("pe", lambda e: e.matmul(out, lhsT=lhsT, rhs=rhs, start=start, stop=stop,
                                             skip_group_check=True), R, W)

    def tp(out, in_, ident, R, W):
        return S.add("pe", lambda e: e.transpose(out, in_, ident), R, W)

    def act(out, in_, func, R, W, scale=1.0, bias=None, eng="act"):
        if bias is None:
            return S.add(eng, lambda e: e.activation(out=out, in_=in_, func=func, scale=scale), R, W)
        return S.add(eng, lambda e: e.activation(out=out, in_=in_, func=func, scale=scale, bias=bias), R, W)

    def tt(eng, out, in0, in1, op, R, W):
        return S.add(eng, lambda e: e.tensor_tensor(out=out, in0=in0, in1=in1, op=op), R, W)

    def ts(eng, out, in0, s1, op0, R, W, s2=None, op1=None):
        if op1 is None:
            return S.add(eng, lambda e: e.tensor_scalar(out=out, in0=in0, scalar1=s1, scalar2=None, op0=op0), R, W)
        return S.add(eng, lambda e: e.tensor_scalar(out=out, in0=in0, scalar1=s1, scalar2=s2, op0=op0, op1=op1), R, W)

    def stt(out, in0, scalar, in1, op0, op1, R, W):
        return S.add("dve", lambda e: e.scalar_tensor_tensor(out=out, in0=in0, scalar=scalar, in1=in1,
                                                            op0=op0, op1=op1), R, W)

    def cp(eng, out, in_, R, W):
        if eng == "act":
            return S.add("act", lambda e: e.activation(out=out, in_=in_, func=AF.Identity), R, W)
        return S.add(eng, lambda e: e.tensor_copy(out=out, in_=in_), R, W)

    def memset(eng, ap, val, W):
        return S.add(eng, lambda e: e.memset(ap, val), (), W)

    wstate = dict(n=0)

    def wload(l, off, el):
        s = wstate["n"] % NWS
        wstate["n"] += 1
        key = ("wr", s)
        dst = wring[:, s, 0:el]
        dma("pool", dst, d_w[l, :, off:off + el], W=[key])
        return wring[:, s, :], key

    dma("sp", xT, d_xT, W=[("xT", t) for t in range(NT)])
    dma("sp", vec, d_vec, W=["vec"])
    dma("sp", tmv, d_tmv, W=["tmv"])
    dma("sp", xsT, d_xsT, W=["xsT"])
    cT = sb("cTs", [128, KC, 1 + NS])
    dma("sp", cT, d_cT, W=["cT"])
    memset("pool", ones_f, 1.0, ["ones_f"])
    memset("pool", ones_b, 1.0, ["ones_b"])
    add("pool", lambda e: e.affine_select(out=ident_f, in_=ones_f, pattern=[[-1, 128]], compare_op=ALU.is_equal,
                                          fill=0.0, base=0, channel_multiplier=1), ["ones_f"], ["ident_f"])
    add("pool", lambda e: e.affine_select(out=U_f, in_=ones_f, pattern=[[1, 128]], compare_op=ALU.is_ge,
                                          fill=0.0, base=0, channel_multiplier=-1), ["ones_f"], ["U_f"])
    Vf = sb("V_f", [128, 128])
    add("pool", lambda e: e.affine_select(out=Vf, in_=ones_f, pattern=[[-1, 128]], compare_op=ALU.is_gt,
                                          fill=0.0, base=0, channel_multiplier=1), ["ones_f"], ["V_f"])
    cp("dve", ident_b, ident_f, ["ident_f"], ["ident_b"])
    cp("dve", U_b, U_f, ["U_f"], ["U_b"])
    cp("dve", V_b, Vf, ["V_f"], ["V_b"])
    memset("dve", Shg, 0.0, [("Shg", h) for h in range(8)])
    memset("dve", HT, 0.0, [("HT", g) for g in range(8)])
    memset("dve", hist, 0.0, [("hist", c) for c in range(32)])
    act(scT, cT, AF.Silu, ["cT"], ["scT"])
    lbw = sb("lbw", [128, L, 8])
    lbsum = sb("lbsum", [128, 8])
    lbraw = vec[:, 0, V_LB:V_LB + 8 * L].rearrange("p (l h) -> p l h", h=8)
    act(lbw, lbraw, AF.Exp, ["vec"], ["lbw"])
    cp("dve", lbsum, lbw[:, 0, :], ["lbw"], ["lbsum"])
    for l in range(1, L):
        tt("dve", lbsum, lbsum, lbw[:, l, :], ALU.add, ["lbw", "lbsum"], ["lbsum"])
    add("dve", lambda e: e.reciprocal(out=lbsum, in_=lbsum), ["lbsum"], ["lbsum"])
    memset("dve", lbs[:, 0, :], 0.0, ["lbs"])
    for l in range(1, L):
        tt("dve", lbw[:, l, :], lbw[:, l, :], lbsum, ALU.mult, ["lbw", "lbsum"], ["lbw"])
        tt("dve", lbs[:, l, :], lbs[:, l - 1, :], lbw[:, l, :], ALU.add, ["lbw", "lbs"], ["lbs"])

    WK = {}

    def wk(name, shape, dt=F32):
        sub = None
        SUBS = {"ysb": (0, 2), "sqy": (2, 4), "sqo": (0, 1), "osb": (1, 2)}
        if name in SUBS:
            sub = SUBS[name]
            name = "nbuf"
        else:
            name = ALIAS.get(name, name)
        if name == "nbuf":
            shape = [128, 4, 512]
        if name not in WK:
            WK[name] = sb("wk_" + name, shape, dt)
        if sub is not None:
            return WK[name][:, sub[0]:sub[1], :]
        r = WK[name]
        if name != "nbuf":
            need = 1
            for d_ in shape[1:]:
                need *= d_
            if len(r.shape) == 3:
                r = r.rearrange("p a b -> p (a b)")
            if r.dtype != dt:
                r = r.bitcast(dt)
            if r.shape[1] > need:
                r = r[:, 0:need]
            if len(shape) == 3:
                r = r.rearrange("p (a b) -> p a b", a=shape[1])
        return r

    for nm_ in ("qs", "th", "sog", "logf", "kk", "bcs"):
        wk(nm_, [128, 512])
    for nm_ in ("qtT", "ktT", "vTb"):
        wk(nm_, [128, 512], BF16)
    def rms_block(src_ap, nk, ncols, srckeys, tag, ps, pskey, inv_n):
        if tag == "n":
            sq = wk("sqn", [128, 4, 512])
            for hh in range(2):
                act(sq, src_ap[:, hh * 4:(hh + 1) * 4, :], AF.Square, srckeys, ["sqn"])
                for k in range(4):
                    mm(ps[:, 0:ncols], ones_f, sq[:, k, :], hh == 0 and k == 0, hh == 1 and k == 3, ["sqn", "ones_f"], [pskey])
        else:
            sq = wk("sq" + tag, [128, nk, ncols])
            act(sq, src_ap, AF.Square, srckeys, ["sq" + tag])
            for k in range(nk):
                mm(ps[:, 0:ncols], ones_f, sq[:, k, :], k == 0, k == nk - 1, ["sq" + tag, "ones_f"], [pskey])
        lnv = wk("lnv" + tag, [128, ncols])
        act(lnv, ps[:, 0:ncols], AF.Ln, [pskey], ["lnv" + tag], scale=inv_n, bias=epsT)
        rs = wk("rs" + tag, [128, ncols])
        act(rs, lnv, AF.Exp, ["lnv" + tag], ["rs" + tag], scale=-0.5)
        return rs, "rs" + tag

    epsT = sb("epsT", [128, 1])
    memset("dve", epsT, EPS, ["epsT"])

    def phase_B(l, hf, t0, smp, hkeys):
        vl = lambda off, n=1: vec[:, l, off:off + n]
        NTT = TH // 128
        wsl, wkey = wload(l, OFF_DT, EL_DT)
        wdt = wsl[:, 0:EL_DT].rearrange("p (k j) -> p k j", k=8)
        la_all = wk("la_all", [128, NTT, 32])
        dt_all = wk("dt_all", [128, NTT, 32])
        lah = wk("lah", [128, NTT, 32], BF16)
        lal = wk("lal", [128, NTT, 32], BF16)
        wend = wk("wend", [128, NTT, 32])
        etot = wk("etot", [128, NTT, 32])
        dtr = wk("dtr", [128, NTT, 32])
        cums = wk("cums", [128, NTT, 32])
        v3 = lambda ap_: ap_[:, 0:NTT * 32].rearrange("p (t h) -> p t h", h=32)
        for t in range(NTT):
            for k in range(8):
                mm(PB[0][:, t * 32:(t + 1) * 32], hT[:, k, t * 128:(t + 1) * 128], wdt[:, k, :], k == 0, k == 7,
                   [wkey] + hkeys(t // 4), [("pb", 0)])
        tt("dve", dtr, v3(PB[0]), tmv[:, l, 0:32].unsqueeze(1).broadcast_to([128, NTT, 32]), ALU.add,
           [("pb", 0), "tmv"], ["dtr"])
        act(dtr, dtr, AF.Exp, ["dtr"], ["dtr"])
        act(dt_all, dtr, AF.Ln, ["dtr", "ones_f"], ["dt_all"], bias=ones_f[:, 0:1])
        tt("dve", la_all, dt_all, Abc.unsqueeze(1).broadcast_to([128, NTT, 32]), ALU.mult, ["dt_all", "Abc"], ["la_all"])
        cp("dve", lah, la_all, ["la_all"], ["lah"])
        tt("dve", lal, la_all, lah, ALU.subtract, ["la_all", "lah"], ["lal"])
        for t in range(NTT):
            mm(PB[1][:, t * 32:(t + 1) * 32], U_f, la_all[:, t, :], True, True, ["U_f", "la_all"], [("pb", 1)])
        for t in range(NTT):
            mm(PB[2][:, t * 32:(t + 1) * 32], ones_f, la_all[:, t, :], True, True, ["ones_f", "la_all"], [("pb", 2)])
        act(etot, v3(PB[2]), AF.Exp, [("pb", 2)], ["etot"])
        cp("act", cums, v3(PB[1]), [("pb", 1)], ["cums"])
        tt("dve", cums, v3(PB[2]), cums, ALU.subtract, [("pb", 2), "cums"], ["cums"])
        act(wend, cums, AF.Exp, ["cums"], ["wend"])
        KB = int(os.environ.get('KB', '9'))
        for g in range(8):
            if KB < 1:
                break
            wsl, wkey = wload(l, OFF_GRP + g * EL_GRP, EL_GRP)
            wv = wsl[:, 0:EL_GRP].rearrange("p (k j c) -> p k j c", k=8, j=6)
            gs = slice(g * 256, (g + 1) * 256)
            hs = slice(4 * g, 4 * g + 4)
            for nb in range(NB):
                bs = slice(nb * 512, (nb + 1) * 512)
                zs = wk("zs", [128, 2, 512], BF16)
                xc = wk("xc", [128, 4, 512], BF16)
                for j in range(6):
                    pbi = j % 2
                    for k in range(8):
                        mm(PB[pbi], wv[:, k, j, :], hT[:, k, bs], k == 0, k == 7, [wkey] + hkeys(nb), [("pb", pbi)])
                    if j < 2:
                        act(zs[:, j, :], PB[pbi], AF.Silu, [("pb", pbi)], ["zs"])
                        continue
                    ci = j - 2
                    ch = (2 * g + ci) if ci < 2 else (16 + g if ci == 2 else 24 + g)
                    ubk = "Ub%d" % ci
                    Ub = wk(ubk, [128, 3 + 512])
                    cp("dve", Ub[:, 0:3], hist[:, ch, :], [("hist", ch)], [ubk])
                    act(Ub[:, 3:515], PB[pbi], AF.Identity, [("pb", pbi)], [ubk])
                    cp("dve", hist[:, ch, :], Ub[:, 512:515], [ubk], [("hist", ch)])
                    acc = wk("cacc", [128, 512])
                    ts("dve", acc, Ub[:, 0:512], vl(V_CW + ch), ALU.mult, [ubk, "vec"], ["cacc"], vl(V_CB + ch), ALU.add)
                    for jj in range(1, 4):
                        stt(acc, Ub[:, jj:jj + 512], vl(V_CW + jj * 32 + ch), acc, ALU.mult, ALU.add, [ubk, "vec", "cacc"], ["cacc"])
                    act(xc[:, ci, :], acc, AF.Silu, ["cacc"], ["xc"])
                for t in range(4):
                    if KB < 2:
                        break
                    tg = nb * 4 + t
                    ct = slice(t * 128, (t + 1) * 128)
                    tpb = PB[7].bitcast(BF16)
                    for i3 in range(3):
                        tp(tpb[:, i3 * 128:(i3 + 1) * 128], xc[:, i3, ct], ident_b, ["xc", "ident_b"], [("pb", 7)])
                    xbt = wk("xbt", [128, 384], BF16)
                    cp("act", xbt, tpb[:, 0:384], [("pb", 7)], ["xbt"])
                    dtx = wk("dtx", [128, 256], BF16)
                    tt("dve", dtx.rearrange("p (k q) -> p k q", q=64), xbt[:, 0:256].rearrange("p (k q) -> p k q", q=64),
                       dt_all[:, tg, hs].unsqueeze(2).broadcast_to([128, 4, 64]), ALU.mult, ["xbt", "dt_all"], ["dtx"])
                    Btok = xbt[:, 256:384]
                    dtxh = wk("dtxh", [128, 256], BF16)
                    tt("dve", dtxh.rearrange("p (k q) -> p k q", q=64), dtx.rearrange("p (k q) -> p k q", q=64),
                       wend[:, tg, hs].unsqueeze(2).broadcast_to([128, 4, 64]), ALU.mult, ["dtx", "wend"], ["dtxh"])
                    if KB < 3:
                        continue
                    Rh = wk("Rh", [128, 4, 128], BF16)
                    Rl = wk("Rl", [128, 4, 128], BF16)
                    Ubc = U_b.unsqueeze(1).broadcast_to([128, 4, 128])
                    tt("dve", Rh, Ubc, lah[:, tg, hs].unsqueeze(2).broadcast_to([128, 4, 128]), ALU.mult, ["U_b", "lah"], ["Rh"])
                    tt("dve", Rl, Ubc, lal[:, tg, hs].unsqueeze(2).broadcast_to([128, 4, 128]), ALU.mult, ["U_b", "lal"], ["Rl"])
                    Rhf = Rh.rearrange("p k i -> p (k i)")
                    Rlf = Rl.rearrange("p k i -> p (k i)")
                    mm(PB[2], V_b, Rhf, True, False, ["V_b", "Rh"], [("pb", 2)])
                    mm(PB[2], V_b, Rlf, False, True, ["V_b", "Rl"], [("pb", 2)])
                    mm(PB[3], ones_b, Rhf, True, False, ["ones_b", "Rh"], [("pb", 3)])
                    mm(PB[3], ones_b, Rlf, False, True, ["ones_b", "Rl"], [("pb", 3)])
                    Lm = wk("Lm", [128, 512])
                    EC = wk("EC", [128, 512])
                    act(Lm, PB[2], AF.Exp, [("pb", 2)], ["Lm"])
                    act(EC, PB[3], AF.Exp, [("pb", 3)], ["EC"])
                    if KB < 4:
                        continue
                    mm(PB[6][:, 0:128], xc[:, 2, ct], xc[:, 3, ct], True, True, ["xc"], [("pb", 6)])
                    cbm = wk("cbm", [128, 128])
                    tt("dve", cbm, PB[6][:, 0:128], U_f, ALU.mult, [("pb", 6), "U_f"], ["cbm"])
                    Mm = wk("Mm", [128, 4, 128], BF16)
                    tt("dve", Mm, Lm.rearrange("p (k i) -> p k i", k=4), cbm.unsqueeze(1).broadcast_to([128, 4, 128]), ALU.mult,
                       ["Lm", "cbm"], ["Mm"])
                    Ct = wk("Ct", [128, 4, 128], BF16)
                    tt("dve", Ct, EC.rearrange("p (k i) -> p k i", k=4), xc[:, 3, ct].unsqueeze(1).broadcast_to([128, 4, 128]),
                       ALU.mult, ["EC", "xc"], ["Ct"])
                    if KB < 5:
                        continue
                    HTb = wk("HTb", [128, 256], BF16)
                    cp("act", HTb, HT[:, gs], [("HT", g)], ["HTb"])
                    for k in range(4):
                        cc, po = k // 2, (k % 2) * 64
                        outp = PB[4 + cc][po:po + 64, ct]
                        mm(outp, dtx[:, k * 64:(k + 1) * 64], Mm[:, k, :], True, False, ["dtx", "Mm"], [("pb", 4 + cc)])
                        mm(outp, HTb[:, k * 64:(k + 1) * 64], Ct[:, k, :], False, True, ["HTb", "Ct"], [("pb", 4 + cc)])
                    if KB < 6:
                        continue
                    mm(PB[6][:, 128:384], Btok, dtxh, True, True, ["xbt", "dtxh"], [("pb", 6)])
                    HTg = HT[:, gs].rearrange("p (k q) -> p k q", q=64)
                    tt("dve", HTg, HTg, etot[:, tg, hs].unsqueeze(2).broadcast_to([128, 4, 64]), ALU.mult, [("HT", g), "etot"], [("HT", g)])
                    tt("dve", HT[:, gs], HT[:, gs], PB[6][:, 128:384], ALU.add, [("HT", g), ("pb", 6)], [("HT", g)])
                if KB < 7:
                    continue
                ysb = wk("ysb", [128, 2, 512])
                for cc in range(2):
                    c16 = 2 * g + cc
                    stt(ysb[:, cc, :], xc[:, cc, :], vl(V_DSK + c16), PB[4 + cc], ALU.mult, ALU.add, ["xc", "vec", ("pb", 4 + cc)], ["ysb"])
                    tt("dve", ysb[:, cc, :], ysb[:, cc, :], zs[:, cc, :], ALU.mult, ["ysb", "zs"], ["ysb"])
                rs, rskey = rms_block(ysb, 2, 512, ["ysb"], "y", PB[2], ("pb", 2), 1.0 / 256)
                for cc in range(2):
                    c16 = 2 * g + cc
                    stt(yT[:, c16, bs], ysb[:, cc, :], vl(V_SN + c16), rs, ALU.mult, ALU.mult, ["ysb", "vec", rskey], [("BIG", 8 + c16)])
        if hf == NHALF - 1:
            dma("sp", o_ssp[l], HT, R=[("HT", g) for g in range(8)])
            dma("sp", o_cvp[l], hist, R=[("hist", c) for c in range(32)])
            if l + 1 < L:
                memset("dve", HT, 0.0, [("HT", g) for g in range(8)])
                memset("dve", hist, 0.0, [("hist", c) for c in range(32)])

    BIGf = BIG.rearrange("p a b -> p (a b)").bitcast(F32)
    CH_EL = TH // 2

    def carve(off, n, shape=None, dt=F32, parts=128):
        ap_ = BIGf[0:parts, off:off + n]
        if dt == BF16:
            ap_ = ap_.bitcast(BF16)
        if shape is not None:
            if len(shape) == 2:
                ap_ = ap_.rearrange("p (a b) -> p a b", a=shape[0])
            else:
                ap_ = ap_.rearrange("p (a b c) -> p a b c", a=shape[0], b=shape[1])
        keys = [("BIG", c) for c in range(off // CH_EL, (off + n - 1) // CH_EL + 1)]
        return ap_, keys

    def sample_pass(l):
        vl = lambda off, n=1: vec[:, l, off:off + n]
        R_ = lambda k: k * 1024
        rs, rskey = rms_block(xsT, 8, NS, ["xsT"], "ns", PB[0], ("pb", 0), 1.0 / D)
        tmp = wk("nstmp", [128, 8, NS])
        tt("dve", tmp, xsT, rs.unsqueeze(1).broadcast_to([128, 8, NS]), ALU.mult, ["xsT", rskey], ["nstmp"])
        tt("dve", tmp, tmp, mods[:, 0, :, 1:], ALU.mult, ["nstmp", "mods"], ["nstmp"])
        tt("dve", hsT, tmp, mods[:, 1, :, 1:], ALU.add, ["nstmp", "mods"], ["hsT"])
        Sel, kSel = carve(R_(5), 1024, [16, 128], BF16, parts=16)
        memset("pool", Sel, 1.0, kSel)
        add("pool", lambda e: e.affine_select(out=Sel, in_=Sel, pattern=[[-1, 16], [0, 128]], compare_op=ALU.is_equal,
                                              fill=0.0, base=0, channel_multiplier=1), kSel, kSel)
        vtok, kvtok = carve(R_(6), 512, None, BF16, parts=16)
        sm, ksm = carve(R_(7), 512, [4, 8, 16])
        SQ, SFG, SKK, SOG = sm[:, 0], sm[:, 1], sm[:, 2], sm[:, 3]
        for h in range(8):
            wsl, wkey = wload(l, OFF_HEAD + h * EL_HEAD, EL_HEAD)
            wv = wsl[:, 0:EL_HEAD].rearrange("p (k j c) -> p k j c", k=8, j=4)
            for j in range(4):
                for k in range(8):
                    mm(PB[0][:, j * 16:(j + 1) * 16], wv[:, k, j, :], hsT[:, k, :], k == 0, k == 7, [wkey, "hsT"], [("pb", 0)])
            for k in range(8):
                mm(PB[1][0:16, 0:128], hsT[:, k, :], wv[:, k, 2, :], k == 0, k == 7, [wkey, "hsT"], [("pb", 1)])
            qss = wk("qss", [128, 16])
            act(qss, PB[0][:, 0:16], AF.Silu, [("pb", 0)], ["qss"])
            ts("dve", SQ[:, h, :], qss, 128.0 ** -0.5, ALU.mult, ["qss"], ksm)
            ths = wk("ths", [128, 16])
            act(ths, PB[0][:, 16:32], AF.Tanh, [("pb", 0)], ["ths"], scale=0.5)
            ts("dve", SFG[:, h, :], ths, lbv[:, 1, h:h + 1], ALU.mult, ["ths", "lbv"], ksm, lbv[:, 2, h:h + 1], ALU.add)
            ts("dve", SKK[:, h, :], ths, lbv[:, 3, h:h + 1], ALU.mult, ["ths", "lbv"], ksm, lbv[:, 1, h:h + 1], ALU.add)
            act(SOG[:, h, :], PB[0][:, 48:64], AF.Silu, [("pb", 0)], ksm)
            cp("act", vtok[:, h * 128:(h + 1) * 128], PB[1][0:16, 0:128], [("pb", 1)], kvtok)
        bufs = [carve(R_(i), 1024, [8, 128]) for i in range(5)]
        for s in range(NS):
            Sin, kin = bufs[s % 2]
            Snw, knw = bufs[2 + s % 2]
            kvt, kkv = bufs[4]
            dma("sp", Sin, d_shg[l, s].rearrange("h d e -> d h e"), W=kin)
            for half in range(2):
                mm(PB[2 + half], Sel[:, s, :], vtok[:, half * 512:(half + 1) * 512], True, True, kSel + kvtok, [("pb", 2 + half)])
            tt("dve", Snw, Sin, SFG[:, :, s:s + 1].broadcast_to([128, 8, 128]), ALU.mult, kin + ksm, knw)
            for half in range(2):
                tt("dve", kvt[:, half * 4:(half + 1) * 4, :], PB[2 + half].rearrange("p (h e) -> p h e", h=4),
                   SKK[:, half * 4:(half + 1) * 4, s:s + 1].broadcast_to([128, 4, 128]), ALU.mult, [("pb", 2 + half)] + ksm, kkv)
            tt("dve", Snw, Snw, kvt, ALU.add, knw + kkv, knw)
            for h in range(8):
                mm(PB[4][:, h * 16 + s:h * 16 + s + 1], Snw[:, h, :], SQ[:, h, s:s + 1], True, True, knw + ksm, [("pb", 4)])
            dma("sp", o_hgs[l, s].rearrange("h d e -> d h e"), Snw, R=knw)
        osb = wk("osbs", [128, 1, 128])
        cp("act", osb[:, 0, :], PB[4][:, 0:128], [("pb", 4)], ["osbs"])
        rs, rskey = rms_block(osb, 1, 128, ["osbs"], "os", PB[5], ("pb", 5), 1.0 / 128)
        tmo = wk("tmos", [128, 128])
        tt("dve", tmo, osb[:, 0, :], rs, ALU.mult, ["osbs", rskey], ["tmos"])
        stt(ohsT.rearrange("p h s -> p (h s)"), tmo, vl(V_HGN), SOG.rearrange("p h s -> p (h s)"), ALU.mult, ALU.mult,
            ["tmos", "vec"] + ksm, ["ohsT"])
        sm2, ksm2 = carve(R_(3), 1024, [4, 16, 16])
        DT, DEC, DTX, ZS = sm2[:, 0], sm2[:, 1], sm2[:, 2], sm2[:, 3]
        YS, kYS = carve(R_(7) + 512, 256, [16, 16])
        XC, kXC = carve(R_(7), 512, [32, 16])
        for b4 in range(4):
            wsl, wkey = wload(l, OFF_DTX + b4 * EL_DTX, EL_DTX)
            wv = wsl[:, 0:EL_DTX].rearrange("p (c k j) -> p c k j", c=4, k=8)
            for c4 in range(4):
                c = b4 * 4 + c4
                for k in range(8):
                    mm(PB[0][:, c * 16:(c + 1) * 16], wv[:, c4, k, :], hsT[:, k, :], k == 0, k == 7, [wkey, "hsT"], [("pb", 0)])
        tt("dve", DT, PB[0][:, 0:256].rearrange("p (c s) -> p c s", s=16), vl(V_DTB, 16).unsqueeze(2).broadcast_to([128, 16, 16]),
           ALU.add, [("pb", 0), "vec"], ksm2)
        act(DT, DT, AF.Exp, ksm2, ksm2)
        act(DT, DT, AF.Ln, ksm2 + ["ones_f"], ksm2, bias=ones_f[:, 0:1])
        Aex = wk("Aex", [128, 16])
        act(Aex, vl(V_ALG, 16), AF.Exp, ["vec"], ["Aex"])
        tt("dve", DEC, DT, Aex.unsqueeze(2).broadcast_to([128, 16, 16]), ALU.mult, ksm2 + ["Aex"], ksm2)
        act(DEC, DEC, AF.Exp, ksm2, ksm2, scale=-1.0)
        cvs, kcv = carve(R_(0), 2048, [32, 4, 16])
        dma("sp", cvs[:, :, 0:3, :], d_scv[l], W=kcv)
        for g in range(8):
            wsl, wkey = wload(l, OFF_GRP + g * EL_GRP, EL_GRP)
            wv = wsl[:, 0:EL_GRP].rearrange("p (k j c) -> p k j c", k=8, j=6)
            for j in range(6):
                for k in range(8):
                    mm(PB[1][:, j * 16:(j + 1) * 16], wv[:, k, j, :], hsT[:, k, :], k == 0, k == 7, [wkey, "hsT"], [("pb", 1)])
            for j in range(2):
                act(ZS[:, 2 * g + j, :], PB[1][:, j * 16:(j + 1) * 16], AF.Silu, [("pb", 1)], ksm2)
            for ci in range(4):
                ch = (2 * g + ci) if ci < 2 else (16 + g if ci == 2 else 24 + g)
                cp("act", cvs[:, ch, 3, :], PB[1][:, (2 + ci) * 16:(3 + ci) * 16], [("pb", 1)], kcv)
                acc = wk("caccs", [128, 16])
                ts("dve", acc, cvs[:, ch, 0, :], vl(V_CW + ch), ALU.mult, kcv + ["vec"], ["caccs"], vl(V_CB + ch), ALU.add)
                for jj in range(1, 4):
                    stt(acc, cvs[:, ch, jj, :], vl(V_CW + jj * 32 + ch), acc, ALU.mult, ALU.add, kcv + ["vec", "caccs"], ["caccs"])
                act(XC[:, ch, :], acc, AF.Silu, ["caccs"], kXC)
        dma("sp", o_cvs[l], cvs[:, :, 1:4, :], R=kcv)
        tt("dve", DTX, XC[:, 0:16, :], DT, ALU.mult, kXC + ksm2, ksm2)
        BCt, kBC = carve(R_(6), 1024, None, BF16, parts=16)
        XCb = wk("XCb", [128, 16, 16], BF16)
        cp("dve", XCb, XC[:, 16:32, :], kXC, ["XCb"])
        tpb = PB[6].bitcast(BF16)
        tpb2 = PB[7].bitcast(BF16)
        for c in range(16):
            dst = (tpb if c < 8 else tpb2)[0:16, (c % 8) * 128:(c % 8 + 1) * 128]
            tp(dst, XCb[:, c, :], ident_b, ["XCb", "ident_b"], [("pb", 6 + c // 8)])
        cp("act", BCt[:, 0:1024], tpb[0:16, 0:1024], [("pb", 6)], kBC)
        cp("act", BCt[:, 1024:2048], tpb2[0:16, 0:1024], [("pb", 7)], kBC)
        hb = [carve(R_(0) + i * 512, 512, [4, 128]) for i in range(2)] + [carve(R_(1) + i * 512, 512, [4, 128]) for i in range(2)]
        tb, ktb = carve(R_(2), 512, [4, 128])
        it = 0
        for s in range(NS):
            for q in range(4):
                Hin, kin = hb[it % 2]
                Hnw, knw = hb[2 + it % 2]
                it += 1
                dma("sp", Hin, d_ssm[l, s, 8 * q:8 * q + 8].rearrange("(c hh) p n -> (hh p) c n", hh=2), W=kin)
                mm(PB[2][:, 0:256], Sel[:, s, :], BCt[:, (2 * q) * 128:(2 * q + 2) * 128], True, True, kSel + kBC, [("pb", 2)])
                mm(PB[3][:, 0:256], Sel[:, s, :], BCt[:, 1024 + (2 * q) * 128:1024 + (2 * q + 2) * 128], True, True, kSel + kBC, [("pb", 3)])
                cs = slice(4 * q, 4 * q + 4)
                tt("dve", Hnw, Hin, DEC[:, cs, s:s + 1].broadcast_to([128, 4, 128]), ALU.mult, kin + ksm2, knw)
                Bbc = PB[2][:, 0:256].rearrange("p (g n) -> p g n", g=2).unsqueeze(2).broadcast_to([128, 2, 2, 128])
                Cbc = PB[3][:, 0:256].rearrange("p (g n) -> p g n", g=2).unsqueeze(2).broadcast_to([128, 2, 2, 128])
                tb4 = tb.rearrange("p (g c) n -> p g c n", g=2)
                tt("dve", tb4, Bbc, DTX[:, cs, s:s + 1].rearrange("p (g c) o -> p g c o", g=2).broadcast_to([128, 2, 2, 128]), ALU.mult,
                   [("pb", 2)] + ksm2, ktb)
                tt("dve", Hnw, Hnw, tb, ALU.add, knw + ktb, knw)
                tt("dve", tb4, Cbc, Hnw.rearrange("p (g c) n -> p g c n", g=2), ALU.mult, [("pb", 3)] + knw, ktb)
                add("dve", lambda e, s=s, cs=cs: e.tensor_reduce(out=YS[:, cs, s], in_=tb, axis=mybir.AxisListType.X, op=ALU.add),
                    ktb, kYS)
                dma("sp", o_sss[l, s, 8 * q:8 * q + 8].rearrange("(c hh) p n -> (hh p) c n", hh=2), Hnw, R=knw)
        yy = wk("yys", [128, 16, 16])
        tt("dve", yy, XC[:, 0:16, :], vl(V_DSK, 16).unsqueeze(2).broadcast_to([128, 16, 16]), ALU.mult, kXC + ["vec"], ["yys"])
        tt("dve", yy, yy, YS, ALU.add, ["yys"] + kYS, ["yys"])
        tt("dve", yy, yy, ZS, ALU.mult, ["yys"] + ksm2, ["yys"])
        sqs = wk("sqys", [128, 16, 16])
        act(sqs, yy, AF.Square, ["yys"], ["sqys"])
        for g in range(8):
            for j in range(2):
                mm(PB[5][:, g * 16:(g + 1) * 16], ones_f, sqs[:, 2 * g + j, :], j == 0, j == 1, ["sqys", "ones_f"], [("pb", 5)])
        lnv = wk("lnvs", [128, 128])
        act(lnv, PB[5][:, 0:128], AF.Ln, [("pb", 5)], ["lnvs"], scale=1.0 / 256, bias=epsT)
        act(lnv, lnv, AF.Exp, ["lnvs"], ["lnvs"], scale=-0.5)
        tt("dve", yy.rearrange("p (g j) s -> p g j s", j=2), yy.rearrange("p (g j) s -> p g j s", j=2),
           lnv.rearrange("p (g s) -> p g s", s=16).unsqueeze(2).broadcast_to([128, 8, 2, 16]), ALU.mult, ["yys", "lnvs"], ["yys"])
        tt("dve", ysT, yy, vl(V_SN, 16).unsqueeze(2).broadcast_to([128, 16, 16]), ALU.mult, ["yys", "vec"], ["ysT"])


    for l in range(L):
        vl = lambda off, n=1: vec[:, l, off:off + n]
        if stage < 1:
            break
        mod_ps = [PB[6], PB[7]]
        for blk in range(12):
            wsl, wkey = wload(l, OFF_ADA + blk * EL_ADA, EL_ADA)
            wv = wsl[:, 0:EL_ADA].rearrange("p (m k c) -> p m k c", m=4, k=8)
            for m in range(4):
                ch = blk * 4 + m
                ps = mod_ps[ch // 24][:, (ch % 24) * 17:(ch % 24) * 17 + 17]
                for k in range(8):
                    mm(ps, wv[:, m, k, :], scT[:, k, :], k == 0, k == 7, [wkey, "scT"], [("pb", 6 + ch // 24)])
        for hh in range(2):
            tt("dve", modT[:, hh * 24:(hh + 1) * 24, :],
               mod_ps[hh][:, 0:24 * 17].rearrange("p (c t) -> p c t", t=17),
               vl(V_BADA + hh * 24, 24).unsqueeze(2).broadcast_to([128, 24, 17]), ALU.add,
               [("pb", 6 + hh), "vec"], ["modT"])
        mods = wk("mods", [128, 6, 8, 1 + NS])
        for (dst, sc_off, nrm) in ((0, 8, V_NMIX), (3, 32, V_NMLP)):
            ts("dve", mods[:, dst], modT[:, sc_off:sc_off + 8, :], 1.0, ALU.add, ["modT"], ["mods"])
            tt("dve", mods[:, dst], mods[:, dst], vl(nrm, 8).unsqueeze(2).broadcast_to([128, 8, 1 + NS]), ALU.mult,
               ["mods", "vec"], ["mods"])
        cp("dve", mods[:, 1], modT[:, 0:8, :], ["modT"], ["mods"])
        cp("dve", mods[:, 4], modT[:, 24:32, :], ["modT"], ["mods"])
        ts("dve", mods[:, 2], modT[:, 16:24, :], 0.5, ALU.mult, ["modT"], ["mods"])
        cp("dve", mods[:, 5], modT[:, 40:48, :], ["modT"], ["mods"])
        lbv = wk("lbv", [128, 4, 8])
        cp("dve", lbv[:, 0], lbs[:, l, :], ["lbs"], ["lbv"])
        ts("dve", lbv[:, 1], lbs[:, l, :], -0.5, ALU.mult, ["lbs"], ["lbv"], 0.5, ALU.add)
        tt("dve", lbv[:, 2], lbv[:, 1], lbs[:, l, :], ALU.add, ["lbv", "lbs"], ["lbv"])
        ts("dve", lbv[:, 3], lbv[:, 1], -1.0, ALU.mult, ["lbv"], ["lbv"])
        Abc = wk("Abc", [128, 32])
        act(Abc, tmv[:, l, 32:64], AF.Exp, ["tmv"], ["Abc"])
        ts("dve", Abc, Abc, -1.0, ALU.mult, ["Abc"], ["Abc"])
        hbm = wk("hbm", [128, 16])
        ts("dve", hbm, vl(V_BM, 16), 0.5, ALU.mult, ["vec"], ["hbm"])

        if do_sample and stage >= 2:
            sample_pass(l)
        for hf in range(NHALF):
            if stage < 2:
                break
            t0 = hf * TH
            smp = do_sample and hf == 0
            xkeys = lambda nb: [("xT", (t0 + nb * 512) // 128 + i) for i in range(4)]

            def norm_mod(ai, bi, dstkey):
                for nb in range(NB):
                    cs = slice(t0 + nb * 512, t0 + nb * 512 + 512)
                    rs, rskey = rms_block(xT[:, :, cs], 8, 512, xkeys(nb), "n", PB[0], ("pb", 0), 1.0 / D)
                    tmp = wk("ntmp", [128, 4, 512])
                    for hh in range(2):
                        tt("dve", tmp, xT[:, hh * 4:(hh + 1) * 4, cs], rs.unsqueeze(1).broadcast_to([128, 4, 512]), ALU.mult,
                           xkeys(nb) + [rskey], ["ntmp"])
                        for k4 in range(4):
                            k = hh * 4 + k4
                            act(hT[:, k, nb * 512:(nb + 1) * 512], tmp[:, k4, :], AF.Identity, ["ntmp", "mods"],
                                [(dstkey, k, nb)], scale=mods[:, ai, k, 0:1], bias=mods[:, bi, k, 0:1])
                if smp and ai == 3:
                    rs, rskey = rms_block(xsT, 8, NS, ["xsT"], "ns", PB[0], ("pb", 0), 1.0 / D)
                    tmp = wk("nstmp", [128, 8, NS])
                    tt("dve", tmp, xsT, rs.unsqueeze(1).broadcast_to([128, 8, NS]), ALU.mult, ["xsT", rskey], ["nstmp"])
                    tt("dve", tmp, tmp, mods[:, ai, :, 1:], ALU.mult, ["nstmp", "mods"], ["nstmp"])
                    tt("dve", hsT, tmp, mods[:, bi, :, 1:], ALU.add, ["nstmp", "mods"], ["hsT"])

            norm_mod(0, 1, "hT")
            hkeys = lambda nb: [("hT", k, nb) for k in range(8)]

            for h in range(8):
                if stage < 3:
                    break
                wsl, wkey = wload(l, OFF_HEAD + h * EL_HEAD, EL_HEAD)
                wv = wsl[:, 0:EL_HEAD].rearrange("p (k j c) -> p k j c", k=8, j=4)
                for nb in range(NB):
                    bs = slice(nb * 512, (nb + 1) * 512)
                    for j in range(4):
                        for k in range(8):
                            mm(PB[j], wv[:, k, j, :], hT[:, k, bs], k == 0, k == 7, [wkey] + hkeys(nb), [("pb", j)])
                    qs = wk("qs", [128, 512])
                    act(qs, PB[0], AF.Silu, [("pb", 0)], ["qs"])
                    th = wk("th", [128, 512])
                    act(th, PB[1], AF.Tanh, [("pb", 1)], ["th"], scale=0.5)
                    vTb = wk("vTb", [128, 512], BF16)
                    cp("act", vTb, PB[2], [("pb", 2)], ["vTb"])
                    sog = wk("sog", [128, 512])
                    act(sog, PB[3], AF.Silu, [("pb", 3)], ["sog"])
                    logf = wk("logf", [128, 512])
                    act(logf, th, AF.Ln, ["th", "lbv"], ["logf"], scale=lbv[:, 1, h:h + 1], bias=lbv[:, 2, h:h + 1])
                    kk = wk("kk", [128, 512])
                    ts("dve", kk, th, lbv[:, 3, h:h + 1], ALU.mult, ["th", "lbv"], ["kk"], lbv[:, 1, h:h + 1], ALU.add)
                    bcs = wk("bcs", [128, 512])
                    for t in range(8):
                        add("dve", lambda e, t=t: e.tensor_tensor_scan(out=bcs[:, t * 64:(t + 1) * 64],
                                                                        data0=ones_f[:, 0:64], data1=logf[:, t * 64:(t + 1) * 64],
                                                                        initial=0.0, op0=ALU.mult, op1=ALU.add),
                            ["logf", "ones_f"], ["bcs"])
                    sc4 = wk("sc4", [128, 6, 8])
                    bmv = bcs[:, 31:512:64]
                    bev = bcs[:, 63:512:64]
                    ts("dve", sc4[:, 0], bmv, -1.0, ALU.mult, ["bcs"], ["sc4"])
                    cp("dve", sc4[:, 1], bmv, ["bcs"], ["sc4"])
                    tt("dve", sc4[:, 2], bev, bmv, ALU.subtract, ["bcs"], ["sc4"])
                    act(sc4[:, 3], bmv, AF.Exp, ["bcs"], ["sc4"])
                    act(sc4[:, 4], bev, AF.Exp, ["bcs"], ["sc4"])
                    act(sc4[:, 5], sc4[:, 2], AF.Exp, ["sc4"], ["sc4"])
                    E1 = wk("E1", [128, 512])
                    E2 = wk("E2", [128, 512])
                    qtT = wk("qtT", [128, 512], BF16)
                    ktT = wk("ktT", [128, 512], BF16)
                    for t in range(8):
                        c4 = slice(t * 64, (t + 1) * 64)
                        act(E1[:, c4], bcs[:, c4], AF.Exp, ["bcs", "sc4"], ["E1"], bias=sc4[:, 0, t:t + 1])
                        act(E2[:, c4], bcs[:, c4], AF.Exp, ["bcs", "sc4"], ["E2"], scale=-1.0, bias=sc4[:, 1, t:t + 1])
                    stt(qtT, qs, 128.0 ** -0.5, E1, ALU.mult, ALU.mult, ["qs", "E1"], ["qtT"])
                    tt("dve", ktT, kk, E2, ALU.mult, ["kk", "E2"], ["ktT"])
                    for t in range(8):
                        if stage < 4:
                            break
                        c0 = t * 64
                        c4 = slice(c0, c0 + 64)
                        tpb = PB[7].bitcast(BF16)
                        tp(tpb[0:64, 0:128], ktT[:, c4], ident_b, ["ktT", "ident_b"], [("pb", 7)])
                        tp(tpb[0:64, 128:256], vTb[:, c4], ident_b, ["vTb", "ident_b"], [("pb", 7)])
                        kv = wk("kv_tok", [64, 256], BF16)
                        cp("act", kv, tpb[0:64, 0:256], [("pb", 7)], ["kv_tok"])
                        scp = PB[6][0:64, 0:64]
                        mm(scp[0:32, :], ktT[:, c0:c0 + 32], qtT[:, c4], True, True, ["ktT", "qtT"], [("pb", 6)])
                        mm(scp[32:64, 32:64], ktT[:, c0 + 32:c0 + 64], qtT[:, c0 + 32:c0 + 64],
                           True, True, ["ktT", "qtT"], [("pb", 6)])
                        sk = "scb%d" % (t % 2)
                        scb = wk(sk, [64, 64], BF16)
                        if l == 0 and hf == 0 and h == 0 and nb == 0 and t < 2:
                            memset("dve", scb, 0.0, [sk])
                        tt("dve", scb[0:32, :], scp[0:32, :], U_f[0:32, 0:64], ALU.mult, [("pb", 6), "U_f"], [sk])
                        tt("dve", scb[32:64, 32:64], scp[32:64, 32:64], U_f[32:64, 32:64], ALU.mult,
                           [("pb", 6), "U_f"], [sk])
                        Sb = wk("Sb", [128, 128], BF16)
                        ts("dve", Sb, Shg[:, h, :], sc4[:, 3, t:t + 1], ALU.mult, [("Shg", h), "sc4"], ["Sb"])
                        mm(PB[4][:, c4], kv[:, 128:256], scb, True, False, ["kv_tok", sk], [("pb", 4)])
                        mm(PB[4][:, c4], Sb, qtT[:, c4], False, True, ["Sb", "qtT"], [("pb", 4)])
                        kvp = PB[6][:, 128:256]
                        mm(kvp, kv[:, 0:128], kv[:, 128:256], True, True, ["kv_tok"], [("pb", 6)])
                        ts("dve", Shg[:, h, :], Shg[:, h, :], sc4[:, 4, t:t + 1], ALU.mult, [("Shg", h), "sc4"], [("Shg", h)])
                        stt(Shg[:, h, :], kvp, sc4[:, 5, t:t + 1], Shg[:, h, :], ALU.mult, ALU.add,
                            [("pb", 6), "sc4", ("Shg", h)], [("Shg", h)])
                    if stage < 5:
                        continue
                    osb = wk("osb", [128, 1, 512])
                    cp("act", osb[:, 0, :], PB[4], [("pb", 4)], ["osb"])
                    rs, rskey = rms_block(osb, 1, 512, ["osb"], "o", PB[5], ("pb", 5), 1.0 / 128)
                    tmo = wk("tmo", [128, 512])
                    tt("dve", tmo, osb[:, 0, :], rs, ALU.mult, ["osb", rskey], ["tmo"])
                    stt(ohT[:, h, bs], tmo, vl(V_HGN), sog, ALU.mult, ALU.mult, ["tmo", "vec", "sog"], [("BIG", h)])
                if hf == NHALF - 1:
                    dma("sp", o_hgp[l, h], Shg[:, h, :], R=[("Shg", h)])
                    if l + 1 < L:
                        memset("dve", Shg[:, h, :], 0.0, [("Shg", h)])

            if stage >= 6 and not skipB:
                phase_B(l, hf, t0, smp, hkeys)
            else:
                for c in range(16):
                    memset("pool", yT[:, c, :], 0.0, [("BIG", 8 + c)])
                if smp:
                    memset("pool", ysT, 0.0, ["ysT"])
            if stage < 7:
                continue
            blocks = [(nb, 512) for nb in range(NB)] + ([("s", NS)] if smp else [])
            for c in range(8):
                wsl, wkey = wload(l, OFF_C + c * EL_C, EL_C)
                wv = wsl[:, 0:EL_C].rearrange("p (k j) -> p k j", k=40)
                for (nb, n) in blocks:
                    if nb == "s":
                        oh_, y_, h_, u_ = ohsT, ysT, hsT, usT
                        bs = slice(0, NS)
                        kr = lambda k: ["ohsT"]
                        ky = lambda k: ["ysT"]
                        kh = ["hsT"]
                        ku = ["usT"]
                    else:
                        oh_, y_, h_, u_ = ohT, yT, hT, uT
                        bs = slice(nb * 512, (nb + 1) * 512)
                        kr = lambda k: [("BIG", k)]
                        ky = lambda k: [("BIG", 8 + k)]
                        kh = hkeys(nb)
                        ku = [("BIG", 24 + c)]
                    for k in range(8):
                        mm(PB[0][:, 0:n], wv[:, k, :], oh_[:, k, bs], k == 0, k == 7, [wkey] + kr(k), [("pb", 0)])
                    for k in range(16):
                        mm(PB[1][:, 0:n], wv[:, 8 + k, :], y_[:, k, bs], k == 0, k == 15, [wkey] + ky(k), [("pb", 1)])
                    for k in range(8):
                        mm(PB[2][:, 0:n], wv[:, 24 + k, :], h_[:, k, bs], k == 0, k == 7, [wkey] + kh, [("pb", 2)])
                    for k in range(8):
                        mm(PB[3][:, 0:n], wv[:, 32 + k, :], h_[:, k, bs], k == 0, k == 7, [wkey] + kh, [("pb", 3)])
                    thA = wk("thA", [128, 512])
                    thB = wk("thB", [128, 512])
                    act(thA[:, 0:n], PB[2][:, 0:n], AF.Tanh, [("pb", 2), "hbm"], ["thA"], scale=0.5, bias=hbm[:, c:c + 1])
                    act(thB[:, 0:n], PB[3][:, 0:n], AF.Tanh, [("pb", 3), "hbm"], ["thB"], scale=0.5, bias=hbm[:, 8 + c:9 + c])
                    t1 = wk("t1", [128, 512])
                    t2 = wk("t2", [128, 512])
                    stt(t1[:, 0:n], thA[:, 0:n], 1.0, PB[0][:, 0:n], ALU.add, ALU.mult, ["thA", ("pb", 0)], ["t1"])
                    stt(t2[:, 0:n], thB[:, 0:n], 1.0, PB[1][:, 0:n], ALU.add, ALU.mult, ["thB", ("pb", 1)], ["t2"])
                    tt("dve", u_[:, c, bs], t1[:, 0:n], t2[:, 0:n], ALU.add, ["t1", "t2"], ku)
            rot = 0
            for b2 in range(2):
                wsl, wkey = wload(l, OFF_WO + b2 * EL_WO, EL_WO)
                wv = wsl[:, 0:EL_WO].rearrange("p (c k j) -> p c k j", c=4, k=8)
                for cc in range(4):
                    c2 = b2 * 4 + cc
                    for (nb, n) in blocks:
                        pbi = 4 + rot % 2
                        rot += 1
                        if nb == "s":
                            for k in range(8):
                                mm(PB[pbi][:, 0:n], wv[:, cc, k, :], usT[:, k, :], k == 0, k == 7, [wkey, "usT"], [("pb", pbi)])
                            t3 = wk("t3s", [128, NS])
                            tt("dve", t3, PB[pbi][:, 0:n], mods[:, 2, c2, 1:], ALU.mult, [("pb", pbi), "mods"], ["t3s"])
                            tt("dve", xsT[:, c2, :], xsT[:, c2, :], t3, ALU.add, ["t3s", "xsT"], ["xsT"])
                        else:
                            bs = slice(nb * 512, (nb + 1) * 512)
                            xs_ = slice(t0 + nb * 512, t0 + (nb + 1) * 512)
                            for k in range(8):
                                mm(PB[pbi], wv[:, cc, k, :], uT[:, k, bs], k == 0, k == 7, [wkey, ("BIG", 24 + k)], [("pb", pbi)])
                            stt(xT[:, c2, xs_], PB[pbi], mods[:, 2, c2, 0:1], xT[:, c2, xs_], ALU.mult, ALU.add,
                                [("pb", pbi), "mods"] + xkeys(nb), xkeys(nb))
            if stage < 8:
                continue
            norm_mod(3, 4, "hT")
            rot = 0
            for b8 in range(8):
                wsl, wkey = wload(l, OFF_UP + b8 * EL_UP, EL_UP)
                wv = wsl[:, 0:EL_UP].rearrange("p (m k j) -> p m k j", m=4, k=8)
                for mi in range(4):
                    m = b8 * 4 + mi
                    for (nb, n) in blocks:
                        pbi = rot % 4
                        rot += 1
                        if nb == "s":
                            h_, a_, bs, kh, ka = hsT, asT, slice(0, NS), ["hsT"], ["asT"]
                        else:
                            h_, a_, bs, kh, ka = hT, aT, slice(nb * 512, (nb + 1) * 512), hkeys(nb), [("BIG", m)]
                        for k in range(8):
                            mm(PB[pbi][:, 0:n], wv[:, mi, k, :], h_[:, k, bs], k == 0, k == 7, [wkey] + kh, [("pb", pbi)])
                        rl = wk("rl%d" % (rot % 2), [128, 512])
                        act(rl[:, 0:n], PB[pbi][:, 0:n], AF.Relu, [("pb", pbi)], ["rl%d" % (rot % 2)])
                        tt("dve", a_[:, m, bs], rl[:, 0:n], rl[:, 0:n], ALU.mult, ["rl%d" % (rot % 2)], ka)
            for c in range(8):
                wsl, wkey = wload(l, OFF_DN + c * EL_DN, EL_DN)
                wv = wsl[:, 0:EL_DN].rearrange("p (m j) -> p m j", m=32)
                for (nb, n) in blocks:
                    pbi = 4 + rot % 4
                    rot += 1
                    if nb == "s":
                        for m in range(32):
                            mm(PB[pbi][:, 0:n], wv[:, m, :], asT[:, m, :], m == 0, m == 31, [wkey, "asT"], [("pb", pbi)])
                        t3 = wk("t3s", [128, NS])
                        tt("dve", t3, PB[pbi][:, 0:n], mods[:, 5, c, 1:], ALU.mult, [("pb", pbi), "mods"], ["t3s"])
                        tt("dve", xsT[:, c, :], xsT[:, c, :], t3, ALU.add, ["t3s", "xsT"], ["xsT"])
                    else:
                        bs = slice(nb * 512, (nb + 1) * 512)
                        xs_ = slice(t0 + nb * 512, t0 + (nb + 1) * 512)
                        for m in range(32):
                            mm(PB[pbi], wv[:, m, :], aT[:, m, bs], m == 0, m == 31, [wkey, ("BIG", m)], [("pb", pbi)])
                        stt(xT[:, c, xs_], PB[pbi], mods[:, 5, c, 0:1], xT[:, c, xs_], ALU.mult, ALU.add,
                            [("pb", pbi), "mods"] + xkeys(nb), xkeys(nb))
    if stage >= 9:
        for nb in range(NTOK // 512):
            cs = slice(nb * 512, (nb + 1) * 512)
            xk = [("xT", nb * 4 + i) for i in range(4)]
            rs, rskey = rms_block(xT[:, :, cs], 8, 512, xk, "n", PB[0], ("pb", 0), 1.0 / D)
            tmp = wk("ntmp", [128, 4, 512])
            for hh in range(2):
                tt("dve", tmp, xT[:, hh * 4:(hh + 1) * 4, cs], rs.unsqueeze(1).broadcast_to([128, 4, 512]), ALU.mult, xk + [rskey], ["ntmp"])
                tt("dve", tmp, tmp, vec[:, 0, V_NF + hh * 4:V_NF + hh * 4 + 4].unsqueeze(2).broadcast_to([128, 4, 512]), ALU.mult,
                   ["ntmp", "vec"], ["ntmp"])
                dma("sp", o_yT[:, hh * 4:(hh + 1) * 4, cs], tmp, R=["ntmp"])
        if do_sample:
            rs, rskey = rms_block(xsT, 8, NS, ["xsT"], "ns", PB[0], ("pb", 0), 1.0 / D)
            tmp = wk("nstmp", [128, 8, NS])
            tt("dve", tmp, xsT, rs.unsqueeze(1).broadcast_to([128, 8, NS]), ALU.mult, ["xsT", rskey], ["nstmp"])
            tt("dve", tmp, tmp, vec[:, 0, V_NF:V_NF + 8].unsqueeze(2).broadcast_to([128, 8, NS]), ALU.mult,
               ["nstmp", "vec"], ["nstmp"])
            dma("sp", o_ysT, tmp, R=["nstmp"])
    if dbg:
        for nm, ap_, shp, dt_ in (("BIG", BIG, [128, 32, TH], BF16), ("hT", hT, [128, KC, TH], BF16),
                                  ("modT", modT, [128, 48, 1 + NS], F32), ("xTd", xT, [128, KC, NTOK], F32),
                                  ("Shg", Shg, [128, 8, 128], F32), ("HT", HT, [128, 2048], F32)):
            dd = nc.dram_tensor("dbg_" + nm, shp, dt_, kind="ExternalOutput").ap()
            dma("sp", dd, ap_, R=list(S.last_w.keys()))
        for nm, ap_ in WK.items():
            dd = nc.dram_tensor("dbgw_" + nm, list(ap_.shape), ap_.dtype, kind="ExternalOutput").ap()
            dma("sp", dd, ap_, R=list(S.last_w.keys()))
    S.final_wait()
    S.emit()
    return nc


def _fm(v):
    return np.ascontiguousarray(v.reshape(-1, 128).T)


def pack_weights(inp, L):
    w_in, w_ada = inp["w_in"], inp["w_ada"]
    out = np.empty((L, 128, EL_TOT), np.float32)
    for l in range(L):
        wi = w_in[l].reshape(8, 128, -1)
        o = out[l]
        for h in range(8):
            blk = np.stack([wi[:, :, j * 1024 + h * 128: j * 1024 + (h + 1) * 128] for j in range(4)], axis=2)
            o[:, OFF_HEAD + h * EL_HEAD: OFF_HEAD + (h + 1) * EL_HEAD] = blk.transpose(1, 0, 2, 3).reshape(128, -1)
        for g in range(8):
            cols = [4096 + 256 * g, 4096 + 256 * g + 128, 6144 + 256 * g, 6144 + 256 * g + 128,
                    6144 + 2048 + 128 * g, 6144 + 3072 + 128 * g]
            blk = np.stack([wi[:, :, c:c + 128] for c in cols], axis=2)
            o[:, OFF_GRP + g * EL_GRP: OFF_GRP + (g + 1) * EL_GRP] = blk.transpose(1, 0, 2, 3).reshape(128, -1)
        o[:, OFF_DT:OFF_DT + EL_DT] = wi[:, :, 10240:10272].transpose(1, 0, 2).reshape(128, -1)
        wa = inp["w_br_a"][l].reshape(8, 128, 1024)
        wb = inp["w_br_b"][l].reshape(16, 128, 1024)
        for c in range(8):
            cs = slice(c * 128, (c + 1) * 128)
            blk = np.concatenate([wa[:, :, cs], wb[:, :, cs], wi[:, :, 10272 + c * 128:10272 + (c + 1) * 128],
                                  wi[:, :, 11296 + c * 128:11296 + (c + 1) * 128]], axis=0)
            o[:, OFF_C + c * EL_C: OFF_C + (c + 1) * EL_C] = blk.transpose(1, 0, 2).reshape(128, -1)
        wo = inp["w_out"][l].reshape(8, 128, 8, 128)
        for b in range(2):
            blk = wo[:, :, b * 4:(b + 1) * 4, :].transpose(1, 2, 0, 3)
            o[:, OFF_WO + b * EL_WO: OFF_WO + (b + 1) * EL_WO] = blk.reshape(128, -1)
        wu = inp["w_up"][l].reshape(8, 128, 32, 128)
        for b in range(8):
            blk = wu[:, :, b * 4:(b + 1) * 4, :].transpose(1, 2, 0, 3)
            o[:, OFF_UP + b * EL_UP: OFF_UP + (b + 1) * EL_UP] = blk.reshape(128, -1)
        wd = inp["w_down"][l].reshape(32, 128, 8, 128)
        for c in range(8):
            blk = wd[:, :, c, :].transpose(1, 0, 2)
            o[:, OFF_DN + c * EL_DN: OFF_DN + (c + 1) * EL_DN] = blk.reshape(128, -1)
        wad = w_ada[l].reshape(8, 128, 48, 128)
        for b in range(12):
            blk = wad[:, :, b * 4:(b + 1) * 4, :].transpose(1, 2, 0, 3)
            o[:, OFF_ADA + b * EL_ADA: OFF_ADA + (b + 1) * EL_ADA] = blk.reshape(128, -1)
        wdt = wi[:, :, 10240:10272]
        wexp = np.repeat(wdt, 64, axis=2).reshape(8, 128, 16, 128)
        for b in range(4):
            blk = wexp[:, :, b * 4:(b + 1) * 4, :].transpose(1, 2, 0, 3)
            o[:, OFF_DTX + b * EL_DTX: OFF_DTX + (b + 1) * EL_DTX] = blk.reshape(128, -1)
    return out


def pack_vecs(inp, L):
    vec = np.zeros((128, L, NV), np.float32)
    tmv = np.zeros((128, L, 64), np.float32)
    for l in range(L):
        vec[:, l, V_NMIX:V_NMIX + 8] = _fm(inp["norm_mix"][l])
        vec[:, l, V_NMLP:V_NMLP + 8] = _fm(inp["norm_mlp"][l])
        vec[:, l, V_BADA:V_BADA + 48] = _fm(inp["b_ada"][l])
        for j in range(4):
            vec[:, l, V_CW + j * 32:V_CW + (j + 1) * 32] = _fm(inp["conv_w"][l, j])
        vec[:, l, V_CB:V_CB + 32] = _fm(inp["conv_b"][l])
        vec[:, l, V_SN:V_SN + 16] = _fm(inp["ssm_norm"][l])
        vec[:, l, V_BM:V_BM + 16] = _fm(inp["b_merge"][l])
        vec[:, l, V_HGN] = inp["hg_norm"][l]
        vec[:, l, V_DSK:V_DSK + 16] = _fm(np.repeat(inp["d_skip"][l], 64))
        lb = inp["lower_bounds"][:L].reshape(L, 8, 128)
        vec[:, l, V_LB:V_LB + 8 * L] = lb.transpose(2, 0, 1).reshape(128, -1)
        vec[:, l, V_NF:V_NF + 8] = _fm(inp["norm_final"])
        vec[:, l, V_DTB:V_DTB + 16] = _fm(np.repeat(inp["dt_bias"][l], 64))
        vec[:, l, V_ALG:V_ALG + 16] = _fm(np.repeat(inp["a_log"][l], 64))
        tmv[:, l, 0:32] = inp["dt_bias"][l][None, :]
        tmv[:, l, 32:64] = inp["a_log"][l][None, :]
    return vec, tmv


_L, _NTOK, _TH = 4, 2048, 512


def kernel(**inp):
    inp = {k: np.asarray(v) for k, v in inp.items()}
    L = _L
    wpk = pack_weights(inp, L)
    vec, tmv = pack_vecs(inp, L)
    in_maps = []
    for b in range(8):
        m = {}
        xp = inp["x_prompt"][b]
        m["xT"] = np.ascontiguousarray(xp.T.reshape(8, 128, _NTOK).transpose(1, 0, 2))
        xs = inp["x_sample"][16 * b:16 * b + 16, 0]
        m["xsT"] = np.ascontiguousarray(xs.T.reshape(8, 128, 16).transpose(1, 0, 2))
        c = np.concatenate([inp["c_prompt"][b:b + 1], inp["c_sample"][16 * b:16 * b + 16]], 0)
        m["cT"] = np.ascontiguousarray(c.T.reshape(8, 128, 17).transpose(1, 0, 2))
        m["wpk"] = wpk
        m["vec"] = vec
        m["tmv"] = tmv
        m["s_hg"] = np.ascontiguousarray(inp["state_hgrn"][:, 16 * b:16 * b + 16])
        m["s_ssm"] = np.ascontiguousarray(inp["state_ssm"][:, 16 * b:16 * b + 16])
        cv = inp["state_conv"][:, 16 * b:16 * b + 16]
        m["s_cvT"] = np.ascontiguousarray(cv.reshape(L, 16, 3, 32, 128).transpose(0, 4, 3, 2, 1))
        in_maps.append(m)
    nc = build(L, _NTOK, _TH, do_sample=True)
    res = run_bass_kernel_spmd(nc, in_maps, core_ids=list(range(8)))
    R = res.results
    y_prompt = np.stack([r["o_yT"].transpose(2, 1, 0).reshape(_NTOK, 1024) for r in R], 0)
    y_sample = np.concatenate([r["o_ysT"].transpose(2, 1, 0).reshape(16, 1, 1024) for r in R], 0)
    hg_p = np.stack([r["o_hgp"] for r in R], 1)
    ss_p = np.stack([r["o_ssp"].reshape(L, 128, 32, 64).transpose(0, 2, 3, 1) for r in R], 1)
    cv_p = np.stack([r["o_cvp"].transpose(0, 3, 2, 1).reshape(L, 3, 4096) for r in R], 1)
    hg_s = np.concatenate([r["o_hgs"] for r in R], 1)
    ss_s = np.concatenate([r["o_sss"] for r in R], 1)
    cv_s = np.concatenate([r["o_cvs"].transpose(0, 4, 3, 2, 1).reshape(L, 16, 3, 4096) for r in R], 1)
    f = lambda a: np.ascontiguousarray(a, dtype=np.float32)
    return (f(y_prompt), f(y_sample), f(hg_p), f(ss_p), f(cv_p), f(hg_s), f(ss_s), f(cv_s))
```

```python
import os
import numpy as np
import concourse.bass as bass
import concourse.mybir as mybir
from concourse.bass_utils import run_bass_kernel_spmd

F32 = mybir.dt.float32
BF16 = mybir.dt.bfloat16
AF = mybir.ActivationFunctionType
ALU = mybir.AluOpType

ENGS = ["pe", "act", "dve", "pool", "sp"]
NSLOT = 28


class Sched:
    def __init__(self, nc):
        self.nc = nc
        self.ops = []
        self.by_eng = {e: [] for e in ENGS}
        self.last_w = {}
        self.readers = {}
        self.slot_last = [None] * NSLOT
        self.next_slot = 0
        self.ndom = 4 + NSLOT
        self.dma_ops = []

    def add(self, eng, fn, R=(), W=(), dma=False):
        op = dict(id=len(self.ops), eng=eng, fn=fn, dma=dma, deps=set(), flag=False, raw=set())
        if dma:
            s = self.next_slot
            self.next_slot = (s + 1) % NSLOT
            op["dom"] = 4 + s
            prev = self.slot_last[s]
            if prev is not None:
                op["deps"].add(prev)
            self.slot_last[s] = op["id"]
            self.dma_ops.append(op["id"])
        else:
            op["dom"] = ENGS.index(eng)
        deps = op["deps"]
        for r in R:
            lw = self.last_w.get(r)
            if lw is not None:
                deps.add(lw)
                op["raw"].add(lw)
            rd = self.readers.setdefault(r, {})
            rd[("dma", op["id"]) if dma else eng] = op["id"]
        for w in W:
            lw = self.last_w.get(w)
            if lw is not None:
                deps.add(lw)
            for rd in self.readers.get(w, {}).values():
                if rd != op["id"]:
                    deps.add(rd)
            self.last_w[w] = op["id"]
            self.readers[w] = {}
        self.ops.append(op)
        self.by_eng[eng].append(op)
        return op["id"]

    def final_wait(self, eng="sp"):
        op = dict(id=len(self.ops), eng=eng, fn=None, dma=False, deps=set(self.dma_ops), flag=False,
                  dom=ENGS.index(eng), raw=set())
        self.ops.append(op)
        self.by_eng[eng].append(op)

    def emit(self):
        nc = self.nc
        ops = self.ops
        def relevant(op, d):
            dop = ops[d]
            if dop["dma"] or dop["eng"] != op["eng"]:
                return True
            return op["eng"] != "pe" and d in op["raw"]
        for op in ops:
            for d in op["deps"]:
                if relevant(op, d):
                    ops[d]["flag"] = True
        cnt = [0] * self.ndom
        for op in ops:
            if op["dma"] or op["flag"]:
                cnt[op["dom"]] += 1
                op["seq"] = cnt[op["dom"]]
        seen = {e: np.zeros(self.ndom, np.int64) for e in ENGS}
        for op in ops:
            e = op["eng"]
            sv = seen[e]
            need = {}
            for d in sorted(op["deps"]):
                dop = ops[d]
                if not relevant(op, d):
                    continue
                dm = dop["dom"]
                if dop["seq"] > sv[dm]:
                    need[dm] = max(need.get(dm, 0), dop["seq"])
                np.maximum(sv, dop["clk"], out=sv)
            op["waits"] = sorted(need.items())
            if op["flag"] or op["dma"]:
                clk = sv.copy()
                clk[op["dom"]] = max(clk[op["dom"]], op["seq"]) if not op["dma"] else op["seq"]
                op["clk"] = clk
        sems = [nc.alloc_semaphore(name=f"s_{i}") for i in range(self.ndom)]

        def run_eng(ename):
            def body(e):
                for op in self.by_eng[ename]:
                    for dm, s in op["waits"]:
                        e.wait_ge(sems[dm], s * 16 if dm >= 4 else s)
                    if op["fn"] is None:
                        continue
                    ins = op["fn"](e)
                    if op["dma"]:
                        ins.then_inc(sems[op["dom"]], 16)
                    elif op["flag"]:
                        ins.then_inc(sems[op["dom"]], 1)
            return body

        with nc.Block() as block:
            block.tensor(run_eng("pe"))
            block.scalar(run_eng("act"))
            block.vector(run_eng("dve"))
            block.gpsimd(run_eng("pool"))
            block.sync(run_eng("sp"))


D = 1024
KC = 8
NS = 16
EPS = 1e-6
EL_HEAD, EL_GRP, EL_DT, EL_C, EL_WO, EL_UP, EL_DN, EL_ADA = 4096, 6144, 256, 5120, 4096, 4096, 4096, 4096
OFF_HEAD = 0
OFF_GRP = OFF_HEAD + 8 * EL_HEAD
OFF_DT = OFF_GRP + 8 * EL_GRP
OFF_C = OFF_DT + EL_DT
OFF_WO = OFF_C + 8 * EL_C
OFF_UP = OFF_WO + 2 * EL_WO
OFF_DN = OFF_UP + 8 * EL_UP
OFF_ADA = OFF_DN + 8 * EL_DN
OFF_DTX = OFF_ADA + 12 * EL_ADA
EL_DTX = 4096
EL_TOT = OFF_DTX + 4 * EL_DTX
WSLOT = 6144
NWS = 2
V_NMIX, V_NMLP, V_BADA, V_CW, V_CB, V_SN, V_BM, V_HGN, V_DSK, V_LB, V_NF = 0, 8, 16, 64, 192, 224, 240, 256, 257, 273, 305
V_DTB, V_ALG = 320, 336
NV = 352


def build(L, NTOK, TH, do_sample=True, dbg=False, stage=99, skipB=False):
    nc = bass.Bass("TRN2", target_bir_lowering=False)
    S = Sched(nc)
    NHALF = NTOK // TH
    NB = TH // 512
    NT = NTOK // 128
    dram = lambda n, s, k, dt=F32: nc.dram_tensor(n, s, dt, kind=k).ap()
    d_xT = dram("xT", [128, KC, NTOK], "ExternalInput")
    d_xsT = dram("xsT", [128, KC, NS], "ExternalInput")
    d_cT = dram("cT", [128, KC, 1 + NS], "ExternalInput")
    d_w = dram("wpk", [L, 128, EL_TOT], "ExternalInput")
    d_vec = dram("vec", [128, L, NV], "ExternalInput")
    d_tmv = dram("tmv", [128, L, 64], "ExternalInput")
    d_shg = dram("s_hg", [L, NS, 8, 128, 128], "ExternalInput")
    d_ssm = dram("s_ssm", [L, NS, 32, 64, 128], "ExternalInput")
    d_scv = dram("s_cvT", [L, 128, 32, 3, NS], "ExternalInput")
    o_yT = dram("o_yT", [128, KC, NTOK], "ExternalOutput")
    o_ysT = dram("o_ysT", [128, KC, NS], "ExternalOutput")
    o_hgp = dram("o_hgp", [L, 8, 128, 128], "ExternalOutput")
    o_ssp = dram("o_ssp", [L, 128, 2048], "ExternalOutput")
    o_cvp = dram("o_cvp", [L, 128, 32, 3], "ExternalOutput")
    o_hgs = dram("o_hgs", [L, NS, 8, 128, 128], "ExternalOutput")
    o_sss = dram("o_sss", [L, NS, 32, 64, 128], "ExternalOutput")
    o_cvs = dram("o_cvs", [L, 128, 32, 3, NS], "ExternalOutput")

    sb = lambda n, s, dt=F32: nc.alloc_sbuf_tensor(n, s, dt).ap()
    xT = sb("xTs", [128, KC, NTOK])
    xsT = sb("xsTs", [128, KC, NS])
    hT = sb("hTs", [128, KC, TH], BF16)
    hsT = sb("hsTs", [128, KC, NS], BF16)
    BIG = sb("BIG", [128, 32, TH], BF16)
    ohT = BIG[:, 0:8, :]
    yT = BIG[:, 8:24, :]
    uT = BIG[:, 24:32, :]
    aT = BIG
    BIGs = sb("BIGs", [128, 32, NS], BF16)
    ohsT, ysT, usT, asT = BIGs[:, 0:8, :], BIGs[:, 8:24, :], BIGs[:, 24:32, :], BIGs
    Shg = sb("Shg", [128, 8, 128])
    HT = sb("HTs", [128, 2048])
    hist = sb("hist", [128, 32, 3])
    wring = sb("wring", [128, NWS, WSLOT], BF16)
    vec = sb("vecs", [128, L, NV])
    tmv = sb("tmvs", [128, L, 64])
    modT = sb("modT", [128, 48, 1 + NS])
    scT = sb("scT", [128, KC, 1 + NS], BF16)
    lbs = sb("lbs", [128, L, 8])
    ident_b = sb("ident_b", [128, 128], BF16)
    ident_f = sb("ident_f", [128, 128])
    ones_f = sb("ones_f", [128, 128])
    U_f = sb("U_f", [128, 128])
    U_b = sb("U_b", [128, 128], BF16)
    V_b = sb("V_b", [128, 128], BF16)
    ones_b = sb("ones_b", [128, 128], BF16)
    PB = [nc.alloc_psum_tensor(f"pb{i}", [128, 512], F32).ap() for i in range(8)]

    ALIAS = {"ysb": "nbuf", "sqy": "nbuf", "sqo": "nbuf", "osb": "nbuf", "tmo": "logf", "Ub1": "Ub0", "Ub2": "Ub0", "Ub3": "Ub0", "thA": "qs", "thB": "th", "t1": "sog", "t2": "logf", "rl0": "kk", "rl1": "bcs",
             "lnvn": "th", "rsn": "logf", "lnvo": "th", "rso": "logf", "sqn": "nbuf", "ntmp": "nbuf",
             "Lm": "qs", "EC": "th", "cacc": "sog", "lnvy": "th", "rsy": "logf", "E1": "th", "E2": "logf",
             "Rh": "qtT", "Rl": "ktT", "Mm": "vTb", "zs": "kk",
             "yys": "qs", "sqys": "th", "lnvs": "sog", "osbs": "logf", "tmos": "kk", "XCb": "bcs", "sqos": "bcs",
             "lnvos": "sog", "rsos": "qs", "nstmp": "bcs"}
    canon = lambda ks: [ALIAS.get(k, k) if isinstance(k, str) else k for k in ks]
    _sadd = S.add
    S.add = lambda eng, fn, R=(), W=(), dma=False: _sadd(eng, fn, canon(R), canon(W), dma)

    def add(eng, fn, R=(), W=()):
        return S.add(eng, fn, R, W)

    def dma(eng, out, in_, R=(), W=()):
        return S.add(eng, lambda e: e.dma_start(out=out, in_=in_), R, W, dma=True)

    def mm(out, lhsT, rhs, start, stop, R, W):
        return S.add("pe", lambda e: e.matmul(out, lhsT=lhsT, rhs=rhs, start=start, stop=stop,
                                             skip_group_check=True), R, W)

    def tp(out, in_, ident, R, W):
        return S.add("pe", lambda e: e.transpose(out, in_, ident), R, W)

    def act(out, in_, func, R, W, scale=1.0, bias=None, eng="act"):
        if bias is None:
            return S.add(eng, lambda e: e.activation(out=out, in_=in_, func=func, scale=scale), R, W)
        return S.add(eng, lambda e: e.activation(out=out, in_=in_, func=func, scale=scale, bias=bias), R, W)

    def tt(eng, out, in0, in1, op, R, W):
        return S.add(eng, lambda e: e.tensor_tensor(out=out, in0=in0, in1=in1, op=op), R, W)

    def ts(eng, out, in0, s1, op0, R, W, s2=None, op1=None):
        if op1 is None:
            return S.add(eng, lambda e: e.tensor_scalar(out=out, in0=in0, scalar1=s1, scalar2=None, op0=op0), R, W)
        return S.add(eng, lambda e: e.tensor_scalar(out=out, in0=in0, scalar1=s1, scalar2=s2, op0=op0, op1=op1), R, W)

    def stt(out, in0, scalar, in1, op0, op1, R, W):
        return S.add("dve", lambda e: e.scalar_tensor_tensor(out=out, in0=in0, scalar=scalar, in1=in1,
                                                            op0=op0, op1=op1), R, W)

    def cp(eng, out, in_, R, W):
        if eng == "act":
            return S.add("act", lambda e: e.activation(out=out, in_=in_, func=AF.Identity), R, W)
        return S.add(eng, lambda e: e.tensor_copy(out=out, in_=in_), R, W)

    def memset(eng, ap, val, W):
        return S.add(eng, lambda e: e.memset(ap, val), (), W)

    wstate = dict(n=0)

    def wload(l, off, el):
        s = wstate["n"] % NWS
        wstate["n"] += 1
        key = ("wr", s)
        dst = wring[:, s, 0:el]
        dma("pool", dst, d_w[l, :, off:off + el], W=[key])
        return wring[:, s, :], key

    dma("sp", xT, d_xT, W=[("xT", t) for t in range(NT)])
    dma("sp", vec, d_vec, W=["vec"])
    dma("sp", tmv, d_tmv, W=["tmv"])
    dma("sp", xsT, d_xsT, W=["xsT"])
    cT = sb("cTs", [128, KC, 1 + NS])
    dma("sp", cT, d_cT, W=["cT"])
    memset("pool", ones_f, 1.0, ["ones_f"])
    memset("pool", ones_b, 1.0, ["ones_b"])
    add("pool", lambda e: e.affine_select(out=ident_f, in_=ones_f, pattern=[[-1, 128]], compare_op=ALU.is_equal,
                                          fill=0.0, base=0, channel_multiplier=1), ["ones_f"], ["ident_f"])
    add("pool", lambda e: e.affine_select(out=U_f, in_=ones_f, pattern=[[1, 128]], compare_op=ALU.is_ge,
                                          fill=0.0, base=0, channel_multiplier=-1), ["ones_f"], ["U_f"])
    Vf = sb("V_f", [128, 128])
    add("pool", lambda e: e.affine_select(out=Vf, in_=ones_f, pattern=[[-1, 128]], compare_op=ALU.is_gt,
                                          fill=0.0, base=0, channel_multiplier=1), ["ones_f"], ["V_f"])
    cp("dve", ident_b, ident_f, ["ident_f"], ["ident_b"])
    cp("dve", U_b, U_f, ["U_f"], ["U_b"])
    cp("dve", V_b, Vf, ["V_f"], ["V_b"])
    memset("dve", Shg, 0.0, [("Shg", h) for h in range(8)])
    memset("dve", HT, 0.0, [("HT", g) for g in range(8)])
    memset("dve", hist, 0.0, [("hist", c) for c in range(32)])
    act(scT, cT, AF.Silu, ["cT"], ["scT"])
    lbw = sb("lbw", [128, L, 8])
    lbsum = sb("lbsum", [128, 8])
    lbraw = vec[:, 0, V_LB:V_LB + 8 * L].rearrange("p (l h) -> p l h", h=8)
    act(lbw, lbraw, AF.Exp, ["vec"], ["lbw"])
    cp("dve", lbsum, lbw[:, 0, :], ["lbw"], ["lbsum"])
    for l in range(1, L):
        tt("dve", lbsum, lbsum, lbw[:, l, :], ALU.add, ["lbw", "lbsum"], ["lbsum"])
    add("dve", lambda e: e.reciprocal(out=lbsum, in_=lbsum), ["lbsum"], ["lbsum"])
    memset("dve", lbs[:, 0, :], 0.0, ["lbs"])
    for l in range(1, L):
        tt("dve", lbw[:, l, :], lbw[:, l, :], lbsum, ALU.mult, ["lbw", "lbsum"], ["lbw"])
        tt("dve", lbs[:, l, :], lbs[:, l - 1, :], lbw[:, l, :], ALU.add, ["lbw", "lbs"], ["lbs"])

    WK = {}

    def wk(name, shape, dt=F32):
        sub = None
        SUBS = {"ysb": (0, 2), "sqy": (2, 4), "sqo": (0, 1), "osb": (1, 2)}
        if name in SUBS:
            sub = SUBS[name]
            name = "nbuf"
        else:
            name = ALIAS.get(name, name)
        if name == "nbuf":
            shape = [128, 4, 512]
        if name not in WK:
            WK[name] = sb("wk_" + name, shape, dt)
        if sub is not None:
            return WK[name][:, sub[0]:sub[1], :]
        r = WK[name]
        if name != "nbuf":
            need = 1
            for d_ in shape[1:]:
                need *= d_
            if len(r.shape) == 3:
                r = r.rearrange("p a b -> p (a b)")
            if r.dtype != dt:
                r = r.bitcast(dt)
            if r.shape[1] > need:
                r = r[:, 0:need]
            if len(shape) == 3:
                r = r.rearrange("p (a b) -> p a b", a=shape[1])
        return r

    for nm_ in ("qs", "th", "sog", "logf", "kk", "bcs"):
        wk(nm_, [128, 512])
    for nm_ in ("qtT", "ktT", "vTb"):
        wk(nm_, [128, 512], BF16)
    def rms_block(src_ap, nk, ncols, srckeys, tag, ps, pskey, inv_n):
        if tag == "n":
            sq = wk("sqn", [128, 4, 512])
            for hh in range(2):
                act(sq, src_ap[:, hh * 4:(hh + 1) * 4, :], AF.Square, srckeys, ["sqn"])
                for k in range(4):
                    mm(ps[:, 0:ncols], ones_f, sq[:, k, :], hh == 0 and k == 0, hh == 1 and k == 3, ["sqn", "ones_f"], [pskey])
        else:
            sq = wk("sq" + tag, [128, nk, ncols])
            act(sq, src_ap, AF.Square, srckeys, ["sq" + tag])
            for k in range(nk):
                mm(ps[:, 0:ncols], ones_f, sq[:, k, :], k == 0, k == nk - 1, ["sq" + tag, "ones_f"], [pskey])
        lnv = wk("lnv" + tag, [128, ncols])
        act(lnv, ps[:, 0:ncols], AF.Ln, [pskey], ["lnv" + tag], scale=inv_n, bias=epsT)
        rs = wk("rs" + tag, [128, ncols])
        act(rs, lnv, AF.Exp, ["lnv" + tag], ["rs" + tag], scale=-0.5)
        return rs, "rs" + tag

    epsT = sb("epsT", [128, 1])
    memset("dve", epsT, EPS, ["epsT"])

    def phase_B(l, hf, t0, smp, hkeys):
        vl = lambda off, n=1: vec[:, l, off:off + n]
        NTT = TH // 128
        wsl, wkey = wload(l, OFF_DT, EL_DT)
        wdt = wsl[:, 0:EL_DT].rearrange("p (k j) -> p k j", k=8)
        la_all = wk("la_all", [128, NTT, 32])
        dt_all = wk("dt_all", [128, NTT, 32])
        lah = wk("lah", [128, NTT, 32], BF16)
        lal = wk("lal", [128, NTT, 32], BF16)
        wend = wk("wend", [128, NTT, 32])
        etot = wk("etot", [128, NTT, 32])
        dtr = wk("dtr", [128, NTT, 32])
        cums = wk("cums", [128, NTT, 32])
        v3 = lambda ap_: ap_[:, 0:NTT * 32].rearrange("p (t h) -> p t h", h=32)
        for t in range(NTT):
            for k in range(8):
                mm(PB[0][:, t * 32:(t + 1) * 32], hT[:, k, t * 128:(t + 1) * 128], wdt[:, k, :], k == 0, k == 7,
                   [wkey] + hkeys(t // 4), [("pb", 0)])
        tt("dve", dtr, v3(PB[0]), tmv[:, l, 0:32].unsqueeze(1).broadcast_to([128, NTT, 32]), ALU.add,
           [("pb", 0), "tmv"], ["dtr"])
        act(dtr, dtr, AF.Exp, ["dtr"], ["dtr"])
        act(dt_all, dtr, AF.Ln, ["dtr", "ones_f"], ["dt_all"], bias=ones_f[:, 0:1])
        tt("dve", la_all, dt_all, Abc.unsqueeze(1).broadcast_to([128, NTT, 32]), ALU.mult, ["dt_all", "Abc"], ["la_all"])
        cp("dve", lah, la_all, ["la_all"], ["lah"])
        tt("dve", lal, la_all, lah, ALU.subtract, ["la_all", "lah"], ["lal"])
        for t in range(NTT):
            mm(PB[1][:, t * 32:(t + 1) * 32], U_f, la_all[:, t, :], True, True, ["U_f", "la_all"], [("pb", 1)])
        for t in range(NTT):
            mm(PB[2][:, t * 32:(t + 1) * 32], ones_f, la_all[:, t, :], True, True, ["ones_f", "la_all"], [("pb", 2)])
        act(etot, v3(PB[2]), AF.Exp, [("pb", 2)], ["etot"])
        cp("act", cums, v3(PB[1]), [("pb", 1)], ["cums"])
        tt("dve", cums, v3(PB[2]), cums, ALU.subtract, [("pb", 2), "cums"], ["cums"])
        act(wend, cums, AF.Exp, ["cums"], ["wend"])
        KB = int(os.environ.get('KB', '9'))
        for g in range(8):
            if KB < 1:
                break
            wsl, wkey = wload(l, OFF_GRP + g * EL_GRP, EL_GRP)
            wv = wsl[:, 0:EL_GRP].rearrange("p (k j c) -> p k j c", k=8, j=6)
            gs = slice(g * 256, (g + 1) * 256)
            hs = slice(4 * g, 4 * g + 4)
            for nb in range(NB):
                bs = slice(nb * 512, (nb + 1) * 512)
                zs = wk("zs", [128, 2, 512], BF16)
                xc = wk("xc", [128, 4, 512], BF16)
                for j in range(6):
                    pbi = j % 2
                    for k in range(8):
                        mm(PB[pbi], wv[:, k, j, :], hT[:, k, bs], k == 0, k == 7, [wkey] + hkeys(nb), [("pb", pbi)])
                    if j < 2:
                        act(zs[:, j, :], PB[pbi], AF.Silu, [("pb", pbi)], ["zs"])
                        continue
                    ci = j - 2
                    ch = (2 * g + ci) if ci < 2 else (16 + g if ci == 2 else 24 + g)
                    ubk = "Ub%d" % ci
                    Ub = wk(ubk, [128, 3 + 512])
                    cp("dve", Ub[:, 0:3], hist[:, ch, :], [("hist", ch)], [ubk])
                    act(Ub[:, 3:515], PB[pbi], AF.Identity, [("pb", pbi)], [ubk])
                    cp("dve", hist[:, ch, :], Ub[:, 512:515], [ubk], [("hist", ch)])
                    acc = wk("cacc", [128, 512])
                    ts("dve", acc, Ub[:, 0:512], vl(V_CW + ch), ALU.mult, [ubk, "vec"], ["cacc"], vl(V_CB + ch), ALU.add)
                    for jj in range(1, 4):
                        stt(acc, Ub[:, jj:jj + 512], vl(V_CW + jj * 32 + ch), acc, ALU.mult, ALU.add, [ubk, "vec", "cacc"], ["cacc"])
                    act(xc[:, ci, :], acc, AF.Silu, ["cacc"], ["xc"])
                for t in range(4):
                    if KB < 2:
                        break
                    tg = nb * 4 + t
                    ct = slice(t * 128, (t + 1) * 128)
                    tpb = PB[7].bitcast(BF16)
                    for i3 in range(3):
                        tp(tpb[:, i3 * 128:(i3 + 1) * 128], xc[:, i3, ct], ident_b, ["xc", "ident_b"], [("pb", 7)])
                    xbt = wk("xbt", [128, 384], BF16)
                    cp("act", xbt, tpb[:, 0:384], [("pb", 7)], ["xbt"])
                    dtx = wk("dtx", [128, 256], BF16)
                    tt("dve", dtx.rearrange("p (k q) -> p k q", q=64), xbt[:, 0:256].rearrange("p (k q) -> p k q", q=64),
                       dt_all[:, tg, hs].unsqueeze(2).broadcast_to([128, 4, 64]), ALU.mult, ["xbt", "dt_all"], ["dtx"])
                    Btok = xbt[:, 256:384]
                    dtxh = wk("dtxh", [128, 256], BF16)
                    tt("dve", dtxh.rearrange("p (k q) -> p k q", q=64), dtx.rearrange("p (k q) -> p k q", q=64),
                       wend[:, tg, hs].unsqueeze(2).broadcast_to([128, 4, 64]), ALU.mult, ["dtx", "wend"], ["dtxh"])
                    if KB < 3:
                        continue
                    Rh = wk("Rh", [128, 4, 128], BF16)
                    Rl = wk("Rl", [128, 4, 128], BF16)
                    Ubc = U_b.unsqueeze(1).broadcast_to([128, 4, 128])
                    tt("dve", Rh, Ubc, lah[:, tg, hs].unsqueeze(2).broadcast_to([128, 4, 128]), ALU.mult, ["U_b", "lah"], ["Rh"])
                    tt("dve", Rl, Ubc, lal[:, tg, hs].unsqueeze(2).broadcast_to([128, 4, 128]), ALU.mult, ["U_b", "lal"], ["Rl"])
                    Rhf = Rh.rearrange("p k i -> p (k i)")
                    Rlf = Rl.rearrange("p k i -> p (k i)")
                    mm(PB[2], V_b, Rhf, True, False, ["V_b", "Rh"], [("pb", 2)])
                    mm(PB[2], V_b, Rlf, False, True, ["V_b", "Rl"], [("pb", 2)])
                    mm(PB[3], ones_b, Rhf, True, False, ["ones_b", "Rh"], [("pb", 3)])
                    mm(PB[3], ones_b, Rlf, False, True, ["ones_b", "Rl"], [("pb", 3)])
                    Lm = wk("Lm", [128, 512])
                    EC = wk("EC", [128, 512])
                    act(Lm, PB[2], AF.Exp, [("pb", 2)], ["Lm"])
                    act(EC, PB[3], AF.Exp, [("pb", 3)], ["EC"])
                    if KB < 4:
                        continue
                    mm(PB[0][:, 0:128], xc[:, 2, ct], xc[:, 3, ct], True, True, ["xc"], [("pb", 0)])
                    cbm = wk("cbm", [128, 128])
                    tt("dve", cbm, PB[0][:, 0:128], U_f, ALU.mult, [("pb", 0), "U_f"], ["cbm"])
                    Mm = wk("Mm", [128, 4, 128], BF16)
                    tt("dve", Mm, Lm.rearrange("p (k i) -> p k i", k=4), cbm.unsqueeze(1).broadcast_to([128, 4, 128]), ALU.mult,
                       ["Lm", "cbm"], ["Mm"])
                    Ct = wk("Ct", [128, 4, 128], BF16)
                    tt("dve", Ct, EC.rearrange("p (k i) -> p k i", k=4), xc[:, 3, ct].unsqueeze(1).broadcast_to([128, 4, 128]),
                       ALU.mult, ["EC", "xc"], ["Ct"])
                    if KB < 5:
                        continue
                    HTb = wk("HTb", [128, 256], BF16)
                    cp("act", HTb, HT[:, gs], [("HT", g)], ["HTb"])
                    for k in range(4):
                        cc, po = k // 2, (k % 2) * 64
                        outp = PB[4 + cc][po:po + 64, ct]
                        mm(outp, dtx[:, k * 64:(k + 1) * 64], Mm[:, k, :], True, False, ["dtx", "Mm"], [("pb", 4 + cc)])
                        mm(outp, HTb[:, k * 64:(k + 1) * 64], Ct[:, k, :], False, True, ["HTb", "Ct"], [("pb", 4 + cc)])
                    if KB < 6:
                        continue
                    mm(PB[6][:, 128:384], Btok, dtxh, True, True, ["xbt", "dtxh"], [("pb", 6)])
                    HTg = HT[:, gs].rearrange("p (k q) -> p k q", q=64)
                    tt("dve", HTg, HTg, etot[:, tg, hs].unsqueeze(2).broadcast_to([128, 4, 64]), ALU.mult, [("HT", g), "etot"], [("HT", g)])
                    tt("dve", HT[:, gs], HT[:, gs], PB[6][:, 128:384], ALU.add, [("HT", g), ("pb", 6)], [("HT", g)])
                if KB < 7:
                    continue
                ysb = wk("ysb", [128, 2, 512])
                for cc in range(2):
                    c16 = 2 * g + cc
                    stt(ysb[:, cc, :], xc[:, cc, :], vl(V_DSK + c16), PB[4 + cc], ALU.mult, ALU.add, ["xc", "vec", ("pb", 4 + cc)], ["ysb"])
                    tt("dve", ysb[:, cc, :], ysb[:, cc, :], zs[:, cc, :], ALU.mult, ["ysb", "zs"], ["ysb"])
                rs, rskey = rms_block(ysb, 2, 512, ["ysb"], "y", PB[2], ("pb", 2), 1.0 / 256)
                for cc in range(2):
                    c16 = 2 * g + cc
                    stt(yT[:, c16, bs], ysb[:, cc, :], vl(V_SN + c16), rs, ALU.mult, ALU.mult, ["ysb", "vec", rskey], [("BIG", 8 + c16)])
        if hf == NHALF - 1:
            dma("sp", o_ssp[l], HT, R=[("HT", g) for g in range(8)])
            dma("sp", o_cvp[l], hist, R=[("hist", c) for c in range(32)])
            if l + 1 < L:
                memset("dve", HT, 0.0, [("HT", g) for g in range(8)])
                memset("dve", hist, 0.0, [("hist", c) for c in range(32)])

    BIGf = BIG.rearrange("p a b -> p (a b)").bitcast(F32)
    CH_EL = TH // 2

    def carve(off, n, shape=None, dt=F32, parts=128):
        ap_ = BIGf[0:parts, off:off + n]
        if dt == BF16:
            ap_ = ap_.bitcast(BF16)
        if shape is not None:
            if len(shape) == 2:
                ap_ = ap_.rearrange("p (a b) -> p a b", a=shape[0])
            else:
                ap_ = ap_.rearrange("p (a b c) -> p a b c", a=shape[0], b=shape[1])
        keys = [("BIG", c) for c in range(off // CH_EL, (off + n - 1) // CH_EL + 1)]
        return ap_, keys

    def sample_pass(l):
        vl = lambda off, n=1: vec[:, l, off:off + n]
        R_ = lambda k: k * 1024
        rs, rskey = rms_block(xsT, 8, NS, ["xsT"], "ns", PB[0], ("pb", 0), 1.0 / D)
        tmp = wk("nstmp", [128, 8, NS])
        tt("dve", tmp, xsT, rs.unsqueeze(1).broadcast_to([128, 8, NS]), ALU.mult, ["xsT", rskey], ["nstmp"])
        tt("dve", tmp, tmp, mods[:, 0, :, 1:], ALU.mult, ["nstmp", "mods"], ["nstmp"])
        tt("dve", hsT, tmp, mods[:, 1, :, 1:], ALU.add, ["nstmp", "mods"], ["hsT"])
        Sel, kSel = carve(R_(5), 1024, [16, 128], BF16, parts=16)
        memset("pool", Sel, 1.0, kSel)
        add("pool", lambda e: e.affine_select(out=Sel, in_=Sel, pattern=[[-1, 16], [0, 128]], compare_op=ALU.is_equal,
                                              fill=0.0, base=0, channel_multiplier=1), kSel, kSel)
        vtok, kvtok = carve(R_(6), 512, None, BF16, parts=16)
        sm, ksm = carve(R_(7), 512, [4, 8, 16])
        SQ, SFG, SKK, SOG = sm[:, 0], sm[:, 1], sm[:, 2], sm[:, 3]
        for h in range(8):
            wsl, wkey = wload(l, OFF_HEAD + h * EL_HEAD, EL_HEAD)
            wv = wsl[:, 0:EL_HEAD].rearrange("p (k j c) -> p k j c", k=8, j=4)
            for j in range(4):
                for k in range(8):
                    mm(PB[0][:, j * 16:(j + 1) * 16], wv[:, k, j, :], hsT[:, k, :], k == 0, k == 7, [wkey, "hsT"], [("pb", 0)])
            for k in range(8):
                mm(PB[1][0:16, 0:128], hsT[:, k, :], wv[:, k, 2, :], k == 0, k == 7, [wkey, "hsT"], [("pb", 1)])
            qss = wk("qss", [128, 16])
            act(qss, PB[0][:, 0:16], AF.Silu, [("pb", 0)], ["qss"])
            ts("dve", SQ[:, h, :], qss, 128.0 ** -0.5, ALU.mult, ["qss"], ksm)
            ths = wk("ths", [128, 16])
            act(ths, PB[0][:, 16:32], AF.Tanh, [("pb", 0)], ["ths"], scale=0.5)
            ts("dve", SFG[:, h, :], ths, lbv[:, 1, h:h + 1], ALU.mult, ["ths", "lbv"], ksm, lbv[:, 2, h:h + 1], ALU.add)
            ts("dve", SKK[:, h, :], ths, lbv[:, 3, h:h + 1], ALU.mult, ["ths", "lbv"], ksm, lbv[:, 1, h:h + 1], ALU.add)
            act(SOG[:, h, :], PB[0][:, 48:64], AF.Silu, [("pb", 0)], ksm)
            cp("act", vtok[:, h * 128:(h + 1) * 128], PB[1][0:16, 0:128], [("pb", 1)], kvtok)
        bufs = [carve(R_(i), 1024, [8, 128]) for i in range(5)]
        for s in range(NS):
            Sin, kin = bufs[s % 2]
            Snw, knw = bufs[2 + s % 2]
            kvt, kkv = bufs[4]
            dma("sp", Sin, d_shg[l, s].rearrange("h d e -> d h e"), W=kin)
            for half in range(2):
                mm(PB[2 + half], Sel[:, s, :], vtok[:, half * 512:(half + 1) * 512], True, True, kSel + kvtok, [("pb", 2 + half)])
            tt("dve", Snw, Sin, SFG[:, :, s:s + 1].broadcast_to([128, 8, 128]), ALU.mult, kin + ksm, knw)
            for half in range(2):
                tt("dve", kvt[:, half * 4:(half + 1) * 4, :], PB[2 + half].rearrange("p (h e) -> p h e", h=4),
                   SKK[:, half * 4:(half + 1) * 4, s:s + 1].broadcast_to([128, 4, 128]), ALU.mult, [("pb", 2 + half)] + ksm, kkv)
            tt("dve", Snw, Snw, kvt, ALU.add, knw + kkv, knw)
            for h in range(8):
                mm(PB[4][:, h * 16 + s:h * 16 + s + 1], Snw[:, h, :], SQ[:, h, s:s + 1], True, True, knw + ksm, [("pb", 4)])
            dma("sp", o_hgs[l, s].rearrange("h d e -> d h e"), Snw, R=knw)
        osb = wk("osbs", [128, 1, 128])
        cp("act", osb[:, 0, :], PB[4][:, 0:128], [("pb", 4)], ["osbs"])
        rs, rskey = rms_block(osb, 1, 128, ["osbs"], "os", PB[5], ("pb", 5), 1.0 / 128)
        tmo = wk("tmos", [128, 128])
        tt("dve", tmo, osb[:, 0, :], rs, ALU.mult, ["osbs", rskey], ["tmos"])
        stt(ohsT.rearrange("p h s -> p (h s)"), tmo, vl(V_HGN), SOG.rearrange("p h s -> p (h s)"), ALU.mult, ALU.mult,
            ["tmos", "vec"] + ksm, ["ohsT"])
        sm2, ksm2 = carve(R_(3), 1024, [4, 16, 16])
        DT, DEC, DTX, ZS = sm2[:, 0], sm2[:, 1], sm2[:, 2], sm2[:, 3]
        YS, kYS = carve(R_(7) + 512, 256, [16, 16])
        XC, kXC = carve(R_(7), 512, [32, 16])
        for b4 in range(4):
            wsl, wkey = wload(l, OFF_DTX + b4 * EL_DTX, EL_DTX)
            wv = wsl[:, 0:EL_DTX].rearrange("p (c k j) -> p c k j", c=4, k=8)
            for c4 in range(4):
                c = b4 * 4 + c4
                for k in range(8):
                    mm(PB[0][:, c * 16:(c + 1) * 16], wv[:, c4, k, :], hsT[:, k, :], k == 0, k == 7, [wkey, "hsT"], [("pb", 0)])
        tt("dve", DT, PB[0][:, 0:256].rearrange("p (c s) -> p c s", s=16), vl(V_DTB, 16).unsqueeze(2).broadcast_to([128, 16, 16]),
           ALU.add, [("pb", 0), "vec"], ksm2)
        act(DT, DT, AF.Exp, ksm2, ksm2)
        act(DT, DT, AF.Ln, ksm2 + ["ones_f"], ksm2, bias=ones_f[:, 0:1])
        Aex = wk("Aex", [128, 16])
        act(Aex, vl(V_ALG, 16), AF.Exp, ["vec"], ["Aex"])
        tt("dve", DEC, DT, Aex.unsqueeze(2).broadcast_to([128, 16, 16]), ALU.mult, ksm2 + ["Aex"], ksm2)
        act(DEC, DEC, AF.Exp, ksm2, ksm2, scale=-1.0)
        cvs, kcv = carve(R_(0), 2048, [32, 4, 16])
        dma("sp", cvs[:, :, 0:3, :], d_scv[l], W=kcv)
        for g in range(8):
            wsl, wkey = wload(l, OFF_GRP + g * EL_GRP, EL_GRP)
            wv = wsl[:, 0:EL_GRP].rearrange("p (k j c) -> p k j c", k=8, j=6)
            for j in range(6):
                for k in range(8):
                    mm(PB[1][:, j * 16:(j + 1) * 16], wv[:, k, j, :], hsT[:, k, :], k == 0, k == 7, [wkey, "hsT"], [("pb", 1)])
            for j in range(2):
                act(ZS[:, 2 * g + j, :], PB[1][:, j * 16:(j + 1) * 16], AF.Silu, [("pb", 1)], ksm2)
            for ci in range(4):
                ch = (2 * g + ci) if ci < 2 else (16 + g if ci == 2 else 24 + g)
                cp("act", cvs[:, ch, 3, :], PB[1][:, (2 + ci) * 16:(3 + ci) * 16], [("pb", 1)], kcv)
                acc = wk("caccs", [128, 16])
                ts("dve", acc, cvs[:, ch, 0, :], vl(V_CW + ch), ALU.mult, kcv + ["vec"], ["caccs"], vl(V_CB + ch), ALU.add)
                for jj in range(1, 4):
                    stt(acc, cvs[:, ch, jj, :], vl(V_CW + jj * 32 + ch), acc, ALU.mult, ALU.add, kcv + ["vec", "caccs"], ["caccs"])
                act(XC[:, ch, :], acc, AF.Silu, ["caccs"], kXC)
        dma("sp", o_cvs[l], cvs[:, :, 1:4, :], R=kcv)
        tt("dve", DTX, XC[:, 0:16, :], DT, ALU.mult, kXC + ksm2, ksm2)
        BCt, kBC = carve(R_(6), 1024, None, BF16, parts=16)
        XCb = wk("XCb", [128, 16, 16], BF16)
        cp("dve", XCb, XC[:, 16:32, :], kXC, ["XCb"])
        tpb = PB[6].bitcast(BF16)
        tpb2 = PB[7].bitcast(BF16)
        for c in range(16):
            dst = (tpb if c < 8 else tpb2)[0:16, (c % 8) * 128:(c % 8 + 1) * 128]
            tp(dst, XCb[:, c, :], ident_b, ["XCb", "ident_b"], [("pb", 6 + c // 8)])
        cp("act", BCt[:, 0:1024], tpb[0:16, 0:1024], [("pb", 6)], kBC)
        cp("act", BCt[:, 1024:2048], tpb2[0:16, 0:1024], [("pb", 7)], kBC)
        hb = [carve(R_(0) + i * 512, 512, [4, 128]) for i in range(2)] + [carve(R_(1) + i * 512, 512, [4, 128]) for i in range(2)]
        tb, ktb = carve(R_(2), 512, [4, 128])
        it = 0
        for s in range(NS):
            for q in range(4):
                Hin, kin = hb[it % 2]
                Hnw, knw = hb[2 + it % 2]
                it += 1
                dma("sp", Hin, d_ssm[l, s, 8 * q:8 * q + 8].rearrange("(c hh) p n -> (hh p) c n", hh=2), W=kin)
                mm(PB[2][:, 0:256], Sel[:, s, :], BCt[:, (2 * q) * 128:(2 * q + 2) * 128], True, True, kSel + kBC, [("pb", 2)])
                mm(PB[3][:, 0:256], Sel[:, s, :], BCt[:, 1024 + (2 * q) * 128:1024 + (2 * q + 2) * 128], True, True, kSel + kBC, [("pb", 3)])
                cs = slice(4 * q, 4 * q + 4)
                tt("dve", Hnw, Hin, DEC[:, cs, s:s + 1].broadcast_to([128, 4, 128]), ALU.mult, kin + ksm2, knw)
                Bbc = PB[2][:, 0:256].rearrange("p (g n) -> p g n", g=2).unsqueeze(2).broadcast_to([128, 2, 2, 128])
                Cbc = PB[3][:, 0:256].rearrange("p (g n) -> p g n", g=2).unsqueeze(2).broadcast_to([128, 2, 2, 128])
                tb4 = tb.rearrange("p (g c) n -> p g c n", g=2)
                tt("dve", tb4, Bbc, DTX[:, cs, s:s + 1].rearrange("p (g c) o -> p g c o", g=2).broadcast_to([128, 2, 2, 128]), ALU.mult,
                   [("pb", 2)] + ksm2, ktb)
                tt("dve", Hnw, Hnw, tb, ALU.add, knw + ktb, knw)
                tt("dve", tb4, Cbc, Hnw.rearrange("p (g c) n -> p g c n", g=2), ALU.mult, [("pb", 3)] + knw, ktb)
                add("dve", lambda e, s=s, cs=cs: e.tensor_reduce(out=YS[:, cs, s], in_=tb, axis=mybir.AxisListType.X, op=ALU.add),
                    ktb, kYS)
                dma("sp", o_sss[l, s, 8 * q:8 * q + 8].rearrange("(c hh) p n -> (hh p) c n", hh=2), Hnw, R=knw)
        yy = wk("yys", [128, 16, 16])
        tt("dve", yy, XC[:, 0:16, :], vl(V_DSK, 16).unsqueeze(2).broadcast_to([128, 16, 16]), ALU.mult, kXC + ["vec"], ["yys"])
        tt("dve", yy, yy, YS, ALU.add, ["yys"] + kYS, ["yys"])
        tt("dve", yy, yy, ZS, ALU.mult, ["yys"] + ksm2, ["yys"])
        sqs = wk("sqys", [128, 16, 16])
        act(sqs, yy, AF.Square, ["yys"], ["sqys"])
        for g in range(8):
            for j in range(2):
                mm(PB[5][:, g * 16:(g + 1) * 16], ones_f, sqs[:, 2 * g + j, :], j == 0, j == 1, ["sqys", "ones_f"], [("pb", 5)])
        lnv = wk("lnvs", [128, 128])
        act(lnv, PB[5][:, 0:128], AF.Ln, [("pb", 5)], ["lnvs"], scale=1.0 / 256, bias=epsT)
        act(lnv, lnv, AF.Exp, ["lnvs"], ["lnvs"], scale=-0.5)
        tt("dve", yy.rearrange("p (g j) s -> p g j s", j=2), yy.rearrange("p (g j) s -> p g j s", j=2),
           lnv.rearrange("p (g s) -> p g s", s=16).unsqueeze(2).broadcast_to([128, 8, 2, 16]), ALU.mult, ["yys", "lnvs"], ["yys"])
        tt("dve", ysT, yy, vl(V_SN, 16).unsqueeze(2).broadcast_to([128, 16, 16]), ALU.mult, ["yys", "vec"], ["ysT"])


    for l in range(L):
        vl = lambda off, n=1: vec[:, l, off:off + n]
        if stage < 1:
            break
        mod_ps = [PB[6], PB[7]]
        for blk in range(12):
            wsl, wkey = wload(l, OFF_ADA + blk * EL_ADA, EL_ADA)
            wv = wsl[:, 0:EL_ADA].rearrange("p (m k c) -> p m k c", m=4, k=8)
            for m in range(4):
                ch = blk * 4 + m
                ps = mod_ps[ch // 24][:, (ch % 24) * 17:(ch % 24) * 17 + 17]
                for k in range(8):
                    mm(ps, wv[:, m, k, :], scT[:, k, :], k == 0, k == 7, [wkey, "scT"], [("pb", 6 + ch // 24)])
        for hh in range(2):
            tt("dve", modT[:, hh * 24:(hh + 1) * 24, :],
               mod_ps[hh][:, 0:24 * 17].rearrange("p (c t) -> p c t", t=17),
               vl(V_BADA + hh * 24, 24).unsqueeze(2).broadcast_to([128, 24, 17]), ALU.add,
               [("pb", 6 + hh), "vec"], ["modT"])
        mods = wk("mods", [128, 6, 8, 1 + NS])
        for (dst, sc_off, nrm) in ((0, 8, V_NMIX), (3, 32, V_NMLP)):
            ts("dve", mods[:, dst], modT[:, sc_off:sc_off + 8, :], 1.0, ALU.add, ["modT"], ["mods"])
            tt("dve", mods[:, dst], mods[:, dst], vl(nrm, 8).unsqueeze(2).broadcast_to([128, 8, 1 + NS]), ALU.mult,
               ["mods", "vec"], ["mods"])
        cp("dve", mods[:, 1], modT[:, 0:8, :], ["modT"], ["mods"])
        cp("dve", mods[:, 4], modT[:, 24:32, :], ["modT"], ["mods"])
        ts("dve", mods[:, 2], modT[:, 16:24, :], 0.5, ALU.mult, ["modT"], ["mods"])
        cp("dve", mods[:, 5], modT[:, 40:48, :], ["modT"], ["mods"])
        lbv = wk("lbv", [128, 4, 8])
        cp("dve", lbv[:, 0], lbs[:, l, :], ["lbs"], ["lbv"])
        ts("dve", lbv[:, 1], lbs[:, l, :], -0.5, ALU.mult, ["lbs"], ["lbv"], 0.5, ALU.add)
        tt("dve", lbv[:, 2], lbv[:, 1], lbs[:, l, :], ALU.add, ["lbv", "lbs"], ["lbv"])
        ts("dve", lbv[:, 3], lbv[:, 1], -1.0, ALU.mult, ["lbv"], ["lbv"])
        Abc = wk("Abc", [128, 32])
        act(Abc, tmv[:, l, 32:64], AF.Exp, ["tmv"], ["Abc"])
        ts("dve", Abc, Abc, -1.0, ALU.mult, ["Abc"], ["Abc"])
        hbm = wk("hbm", [128, 16])
        ts("dve", hbm, vl(V_BM, 16), 0.5, ALU.mult, ["vec"], ["hbm"])

        if do_sample and stage >= 2:
            sample_pass(l)
        for hf in range(NHALF):
            if stage < 2:
                break
            t0 = hf * TH
            smp = do_sample and hf == 0
            xkeys = lambda nb: [("xT", (t0 + nb * 512) // 128 + i) for i in range(4)]

            def norm_mod(ai, bi, dstkey):
                for nb in range(NB):
                    cs = slice(t0 + nb * 512, t0 + nb * 512 + 512)
                    rs, rskey = rms_block(xT[:, :, cs], 8, 512, xkeys(nb), "n", PB[0], ("pb", 0), 1.0 / D)
                    tmp = wk("ntmp", [128, 4, 512])
                    for hh in range(2):
                        tt("dve", tmp, xT[:, hh * 4:(hh + 1) * 4, cs], rs.unsqueeze(1).broadcast_to([128, 4, 512]), ALU.mult,
                           xkeys(nb) + [rskey], ["ntmp"])
                        for k4 in range(4):
                            k = hh * 4 + k4
                            act(hT[:, k, nb * 512:(nb + 1) * 512], tmp[:, k4, :], AF.Identity, ["ntmp", "mods"],
                                [(dstkey, k, nb)], scale=mods[:, ai, k, 0:1], bias=mods[:, bi, k, 0:1])
                if smp and ai == 3:
                    rs, rskey = rms_block(xsT, 8, NS, ["xsT"], "ns", PB[0], ("pb", 0), 1.0 / D)
                    tmp = wk("nstmp", [128, 8, NS])
                    tt("dve", tmp, xsT, rs.unsqueeze(1).broadcast_to([128, 8, NS]), ALU.mult, ["xsT", rskey], ["nstmp"])
                    tt("dve", tmp, tmp, mods[:, ai, :, 1:], ALU.mult, ["nstmp", "mods"], ["nstmp"])
                    tt("dve", hsT, tmp, mods[:, bi, :, 1:], ALU.add, ["nstmp", "mods"], ["hsT"])

            norm_mod(0, 1, "hT")
            hkeys = lambda nb: [("hT", k, nb) for k in range(8)]

            for h in range(8):
                if stage < 3:
                    break
                wsl, wkey = wload(l, OFF_HEAD + h * EL_HEAD, EL_HEAD)
                wv = wsl[:, 0:EL_HEAD].rearrange("p (k j c) -> p k j c", k=8, j=4)
                for nb in range(NB):
                    bs = slice(nb * 512, (nb + 1) * 512)
                    for j in range(4):
                        for k in range(8):
                            mm(PB[j], wv[:, k, j, :], hT[:, k, bs], k == 0, k == 7, [wkey] + hkeys(nb), [("pb", j)])
                    qs = wk("qs", [128, 512])
                    act(qs, PB[0], AF.Silu, [("pb", 0)], ["qs"])
                    th = wk("th", [128, 512])
                    act(th, PB[1], AF.Tanh, [("pb", 1)], ["th"], scale=0.5)
                    vTb = wk("vTb", [128, 512], BF16)
                    cp("act", vTb, PB[2], [("pb", 2)], ["vTb"])
                    sog = wk("sog", [128, 512])
                    act(sog, PB[3], AF.Silu, [("pb", 3)], ["sog"])
                    logf = wk("logf", [128, 512])
                    act(logf, th, AF.Ln, ["th", "lbv"], ["logf"], scale=lbv[:, 1, h:h + 1], bias=lbv[:, 2, h:h + 1])
                    kk = wk("kk", [128, 512])
                    ts("dve", kk, th, lbv[:, 3, h:h + 1], ALU.mult, ["th", "lbv"], ["kk"], lbv[:, 1, h:h + 1], ALU.add)
                    bcs = wk("bcs", [128, 512])
                    for t in range(8):
                        add("dve", lambda e, t=t: e.tensor_tensor_scan(out=bcs[:, t * 64:(t + 1) * 64],
                                                                        data0=ones_f[:, 0:64], data1=logf[:, t * 64:(t + 1) * 64],
                                                                        initial=0.0, op0=ALU.mult, op1=ALU.add),
                            ["logf", "ones_f"], ["bcs"])
                    sc4 = wk("sc4", [128, 6, 8])
                    bmv = bcs[:, 31:512:64]
                    bev = bcs[:, 63:512:64]
                    ts("dve", sc4[:, 0], bmv, -1.0, ALU.mult, ["bcs"], ["sc4"])
                    cp("dve", sc4[:, 1], bmv, ["bcs"], ["sc4"])
                    tt("dve", sc4[:, 2], bev, bmv, ALU.subtract, ["bcs"], ["sc4"])
                    act(sc4[:, 3], bmv, AF.Exp, ["bcs"], ["sc4"])
                    act(sc4[:, 4], bev, AF.Exp, ["bcs"], ["sc4"])
                    act(sc4[:, 5], sc4[:, 2], AF.Exp, ["sc4"], ["sc4"])
                    E1 = wk("E1", [128, 512])
                    E2 = wk("E2", [128, 512])
                    qtT = wk("qtT", [128, 512], BF16)
                    ktT = wk("ktT", [128, 512], BF16)
                    for t in range(8):
                        c4 = slice(t * 64, (t + 1) * 64)
                        act(E1[:, c4], bcs[:, c4], AF.Exp, ["bcs", "sc4"], ["E1"], bias=sc4[:, 0, t:t + 1])
                        act(E2[:, c4], bcs[:, c4], AF.Exp, ["bcs", "sc4"], ["E2"], scale=-1.0, bias=sc4[:, 1, t:t + 1])
                    stt(qtT, qs, 128.0 ** -0.5, E1, ALU.mult, ALU.mult, ["qs", "E1"], ["qtT"])
                    tt("dve", ktT, kk, E2, ALU.mult, ["kk", "E2"], ["ktT"])
                    for t in range(8):
                        if stage < 4:
                            break
                        c0 = t * 64
                        c4 = slice(c0, c0 + 64)
                        tpb = PB[7].bitcast(BF16)
                        tp(tpb[0:64, 0:128], ktT[:, c4], ident_b, ["ktT", "ident_b"], [("pb", 7)])
                        tp(tpb[0:64, 128:256], vTb[:, c4], ident_b, ["vTb", "ident_b"], [("pb", 7)])
                        kv = wk("kv_tok", [64, 256], BF16)
                        cp("act", kv, tpb[0:64, 0:256], [("pb", 7)], ["kv_tok"])
                        scp = PB[5][0:64, 0:64]
                        mm(scp[0:32, :], ktT[:, c0:c0 + 32], qtT[:, c4], True, True, ["ktT", "qtT"], [("pb", 5)])
                        mm(scp[32:64, 32:64], ktT[:, c0 + 32:c0 + 64], qtT[:, c0 + 32:c0 + 64],
                           True, True, ["ktT", "qtT"], [("pb", 5)])
                        sk = "scb%d" % (t % 2)
                        scb = wk(sk, [64, 64], BF16)
                        if l == 0 and hf == 0 and h == 0 and nb == 0 and t < 2:
                            memset("dve", scb, 0.0, [sk])
                        tt("dve", scb[0:32, :], scp[0:32, :], U_f[0:32, 0:64], ALU.mult, [("pb", 5), "U_f"], [sk])
                        tt("dve", scb[32:64, 32:64], scp[32:64, 32:64], U_f[32:64, 32:64], ALU.mult,
                           [("pb", 5), "U_f"], [sk])
                        Sb = wk("Sb", [128, 128], BF16)
                        ts("dve", Sb, Shg[:, h, :], sc4[:, 3, t:t + 1], ALU.mult, [("Shg", h), "sc4"], ["Sb"])
                        mm(PB[4][:, c4], kv[:, 128:256], scb, True, False, ["kv_tok", sk], [("pb", 4)])
                        mm(PB[4][:, c4], Sb, qtT[:, c4], False, True, ["Sb", "qtT"], [("pb", 4)])
                        kvp = PB[6][:, 128:256]
                        mm(kvp, kv[:, 0:128], kv[:, 128:256], True, True, ["kv_tok"], [("pb", 6)])
                        ts("dve", Shg[:, h, :], Shg[:, h, :], sc4[:, 4, t:t + 1], ALU.mult, [("Shg", h), "sc4"], [("Shg", h)])
                        stt(Shg[:, h, :], kvp, sc4[:, 5, t:t + 1], Shg[:, h, :], ALU.mult, ALU.add,
                            [("pb", 6), "sc4", ("Shg", h)], [("Shg", h)])
                    if stage < 5:
                        continue
                    osb = wk("osb", [128, 1, 512])
                    cp("act", osb[:, 0, :], PB[4], [("pb", 4)], ["osb"])
                    rs, rskey = rms_block(osb, 1, 512, ["osb"], "o", PB[5], ("pb", 5), 1.0 / 128)
                    tmo = wk("tmo", [128, 512])
                    tt("dve", tmo, osb[:, 0, :], rs, ALU.mult, ["osb", rskey], ["tmo"])
                    stt(ohT[:, h, bs], tmo, vl(V_HGN), sog, ALU.mult, ALU.mult, ["tmo", "vec", "sog"], [("BIG", h)])
                if hf == NHALF - 1:
                    dma("sp", o_hgp[l, h], Shg[:, h, :], R=[("Shg", h)])
                    if l + 1 < L:
                        memset("dve", Shg[:, h, :], 0.0, [("Shg", h)])

            if stage >= 6 and not skipB:
                phase_B(l, hf, t0, smp, hkeys)
            else:
                for c in range(16):
                    memset("pool", yT[:, c, :], 0.0, [("BIG", 8 + c)])
                if smp:
                    memset("pool", ysT, 0.0, ["ysT"])
            if stage < 7:
                continue
            blocks = [(nb, 512) for nb in range(NB)] + ([("s", NS)] if smp else [])
            for c in range(8):
                wsl, wkey = wload(l, OFF_C + c * EL_C, EL_C)
                wv = wsl[:, 0:EL_C].rearrange("p (k j) -> p k j", k=40)
                for (nb, n) in blocks:
                    if nb == "s":
                        oh_, y_, h_, u_ = ohsT, ysT, hsT, usT
                        bs = slice(0, NS)
                        kr = lambda k: ["ohsT"]
                        ky = lambda k: ["ysT"]
                        kh = ["hsT"]
                        ku = ["usT"]
                    else:
                        oh_, y_, h_, u_ = ohT, yT, hT, uT
                        bs = slice(nb * 512, (nb + 1) * 512)
                        kr = lambda k: [("BIG", k)]
                        ky = lambda k: [("BIG", 8 + k)]
                        kh = hkeys(nb)
                        ku = [("BIG", 24 + c)]
                    for k in range(8):
                        mm(PB[0][:, 0:n], wv[:, k, :], oh_[:, k, bs], k == 0, k == 7, [wkey] + kr(k), [("pb", 0)])
                    for k in range(16):
                        mm(PB[1][:, 0:n], wv[:, 8 + k, :], y_[:, k, bs], k == 0, k == 15, [wkey] + ky(k), [("pb", 1)])
                    for k in range(8):
                        mm(PB[2][:, 0:n], wv[:, 24 + k, :], h_[:, k, bs], k == 0, k == 7, [wkey] + kh, [("pb", 2)])
                    for k in range(8):
                        mm(PB[3][:, 0:n], wv[:, 32 + k, :], h_[:, k, bs], k == 0, k == 7, [wkey] + kh, [("pb", 3)])
                    thA = wk("thA", [128, 512])
                    thB = wk("thB", [128, 512])
                    act(thA[:, 0:n], PB[2][:, 0:n], AF.Tanh, [("pb", 2), "hbm"], ["thA"], scale=0.5, bias=hbm[:, c:c + 1])
                    act(thB[:, 0:n], PB[3][:, 0:n], AF.Tanh, [("pb", 3), "hbm"], ["thB"], scale=0.5, bias=hbm[:, 8 + c:9 + c])
                    t1 = wk("t1", [128, 512])
                    t2 = wk("t2", [128, 512])
                    stt(t1[:, 0:n], thA[:, 0:n], 1.0, PB[0][:, 0:n], ALU.add, ALU.mult, ["thA", ("pb", 0)], ["t1"])
                    stt(t2[:, 0:n], thB[:, 0:n], 1.0, PB[1][:, 0:n], ALU.add, ALU.mult, ["thB", ("pb", 1)], ["t2"])
                    tt("dve", u_[:, c, bs], t1[:, 0:n], t2[:, 0:n], ALU.add, ["t1", "t2"], ku)
            rot = 0
            for b2 in range(2):
                wsl, wkey = wload(l, OFF_WO + b2 * EL_WO, EL_WO)
                wv = wsl[:, 0:EL_WO].rearrange("p (c k j) -> p c k j", c=4, k=8)
                for cc in range(4):
                    c2 = b2 * 4 + cc
                    for (nb, n) in blocks:
                        pbi = 4 + rot % 2
                        rot += 1
                        if nb == "s":
                            for k in range(8):
                                mm(PB[pbi][:, 0:n], wv[:, cc, k, :], usT[:, k, :], k == 0, k == 7, [wkey, "usT"], [("pb", pbi)])
                            t3 = wk("t3s", [128, NS])
                            tt("dve", t3, PB[pbi][:, 0:n], mods[:, 2, c2, 1:], ALU.mult, [("pb", pbi), "mods"], ["t3s"])
                            tt("dve", xsT[:, c2, :], xsT[:, c2, :], t3, ALU.add, ["t3s", "xsT"], ["xsT"])
                        else:
                            bs = slice(nb * 512, (nb + 1) * 512)
                            xs_ = slice(t0 + nb * 512, t0 + (nb + 1) * 512)
                            for k in range(8):
                                mm(PB[pbi], wv[:, cc, k, :], uT[:, k, bs], k == 0, k == 7, [wkey, ("BIG", 24 + k)], [("pb", pbi)])
                            stt(xT[:, c2, xs_], PB[pbi], mods[:, 2, c2, 0:1], xT[:, c2, xs_], ALU.mult, ALU.add,
                                [("pb", pbi), "mods"] + xkeys(nb), xkeys(nb))
            if stage < 8:
                continue
            norm_mod(3, 4, "hT")
            rot = 0
            for b8 in range(8):
                wsl, wkey = wload(l, OFF_UP + b8 * EL_UP, EL_UP)
                wv = wsl[:, 0:EL_UP].rearrange("p (m k j) -> p m k j", m=4, k=8)
                for mi in range(4):
                    m = b8 * 4 + mi
                    for (nb, n) in blocks:
                        pbi = rot % 4
                        rot += 1
                        if nb == "s":
                            h_, a_, bs, kh, ka = hsT, asT, slice(0, NS), ["hsT"], ["asT"]
                        else:
                            h_, a_, bs, kh, ka = hT, aT, slice(nb * 512, (nb + 1) * 512), hkeys(nb), [("BIG", m)]
                        for k in range(8):
                            mm(PB[pbi][:, 0:n], wv[:, mi, k, :], h_[:, k, bs], k == 0, k == 7, [wkey] + kh, [("pb", pbi)])
                        rl = wk("rl%d" % (rot % 2), [128, 512])
                        act(rl[:, 0:n], PB[pbi][:, 0:n], AF.Relu, [("pb", pbi)], ["rl%d" % (rot % 2)])
                        tt("dve", a_[:, m, bs], rl[:, 0:n], rl[:, 0:n], ALU.mult, ["rl%d" % (rot % 2)], ka)
            for c in range(8):
                wsl, wkey = wload(l, OFF_DN + c * EL_DN, EL_DN)
                wv = wsl[:, 0:EL_DN].rearrange("p (m j) -> p m j", m=32)
                for (nb, n) in blocks:
                    pbi = 4 + rot % 4
                    rot += 1
                    if nb == "s":
                        for m in range(32):
                            mm(PB[pbi][:, 0:n], wv[:, m, :], asT[:, m, :], m == 0, m == 31, [wkey, "asT"], [("pb", pbi)])
                        t3 = wk("t3s", [128, NS])
                        tt("dve", t3, PB[pbi][:, 0:n], mods[:, 5, c, 1:], ALU.mult, [("pb", pbi), "mods"], ["t3s"])
                        tt("dve", xsT[:, c, :], xsT[:, c, :], t3, ALU.add, ["t3s", "xsT"], ["xsT"])
                    else:
                        bs = slice(nb * 512, (nb + 1) * 512)
                        xs_ = slice(t0 + nb * 512, t0 + (nb + 1) * 512)
                        for m in range(32):
                            mm(PB[pbi], wv[:, m, :], aT[:, m, bs], m == 0, m == 31, [wkey, ("BIG", m)], [("pb", pbi)])
                        stt(xT[:, c, xs_], PB[pbi], mods[:, 5, c, 0:1], xT[:, c, xs_], ALU.mult, ALU.add,
                            [("pb", pbi), "mods"] + xkeys(nb), xkeys(nb))
    if stage >= 9:
        for nb in range(NTOK // 512):
            cs = slice(nb * 512, (nb + 1) * 512)
            xk = [("xT", nb * 4 + i) for i in range(4)]
            rs, rskey = rms_block(xT[:, :, cs], 8, 512, xk, "n", PB[0], ("pb", 0), 1.0 / D)
            tmp = wk("ntmp", [128, 4, 512])
            for hh in range(2):
                tt("dve", tmp, xT[:, hh * 4:(hh + 1) * 4, cs], rs.unsqueeze(1).broadcast_to([128, 4, 512]), ALU.mult, xk + [rskey], ["ntmp"])
                tt("dve", tmp, tmp, vec[:, 0, V_NF + hh * 4:V_NF + hh * 4 + 4].unsqueeze(2).broadcast_to([128, 4, 512]), ALU.mult,
                   ["ntmp", "vec"], ["ntmp"])
                dma("sp", o_yT[:, hh * 4:(hh + 1) * 4, cs], tmp, R=["ntmp"])
        if do_sample:
            rs, rskey = rms_block(xsT, 8, NS, ["xsT"], "ns", PB[0], ("pb", 0), 1.0 / D)
            tmp = wk("nstmp", [128, 8, NS])
            tt("dve", tmp, xsT, rs.unsqueeze(1).broadcast_to([128, 8, NS]), ALU.mult, ["xsT", rskey], ["nstmp"])
            tt("dve", tmp, tmp, vec[:, 0, V_NF:V_NF + 8].unsqueeze(2).broadcast_to([128, 8, NS]), ALU.mult,
               ["nstmp", "vec"], ["nstmp"])
            dma("sp", o_ysT, tmp, R=["nstmp"])
    if dbg:
        for nm, ap_, shp, dt_ in (("BIG", BIG, [128, 32, TH], BF16), ("hT", hT, [128, KC, TH], BF16),
                                  ("modT", modT, [128, 48, 1 + NS], F32), ("xTd", xT, [128, KC, NTOK], F32),
                                  ("Shg", Shg, [128, 8, 128], F32), ("HT", HT, [128, 2048], F32)):
            dd = nc.dram_tensor("dbg_" + nm, shp, dt_, kind="ExternalOutput").ap()
            dma("sp", dd, ap_, R=list(S.last_w.keys()))
        for nm, ap_ in WK.items():
            dd = nc.dram_tensor("dbgw_" + nm, list(ap_.shape), ap_.dtype, kind="ExternalOutput").ap()
            dma("sp", dd, ap_, R=list(S.last_w.keys()))
    S.final_wait()
    S.emit()
    return nc


def _fm(v):
    return np.ascontiguousarray(v.reshape(-1, 128).T)


def pack_weights(inp, L):
    w_in, w_ada = inp["w_in"], inp["w_ada"]
    out = np.empty((L, 128, EL_TOT), np.float32)
    for l in range(L):
        wi = w_in[l].reshape(8, 128, -1)
        o = out[l]
        for h in range(8):
            blk = np.stack([wi[:, :, j * 1024 + h * 128: j * 1024 + (h + 1) * 128] for j in range(4)], axis=2)
            o[:, OFF_HEAD + h * EL_HEAD: OFF_HEAD + (h + 1) * EL_HEAD] = blk.transpose(1, 0, 2, 3).reshape(128, -1)
        for g in range(8):
            cols = [4096 + 256 * g, 4096 + 256 * g + 128, 6144 + 256 * g, 6144 + 256 * g + 128,
                    6144 + 2048 + 128 * g, 6144 + 3072 + 128 * g]
            blk = np.stack([wi[:, :, c:c + 128] for c in cols], axis=2)
            o[:, OFF_GRP + g * EL_GRP: OFF_GRP + (g + 1) * EL_GRP] = blk.transpose(1, 0, 2, 3).reshape(128, -1)
        o[:, OFF_DT:OFF_DT + EL_DT] = wi[:, :, 10240:10272].transpose(1, 0, 2).reshape(128, -1)
        wa = inp["w_br_a"][l].reshape(8, 128, 1024)
        wb = inp["w_br_b"][l].reshape(16, 128, 1024)
        for c in range(8):
            cs = slice(c * 128, (c + 1) * 128)
            blk = np.concatenate([wa[:, :, cs], wb[:, :, cs], wi[:, :, 10272 + c * 128:10272 + (c + 1) * 128],
                                  wi[:, :, 11296 + c * 128:11296 + (c + 1) * 128]], axis=0)
            o[:, OFF_C + c * EL_C: OFF_C + (c + 1) * EL_C] = blk.transpose(1, 0, 2).reshape(128, -1)
        wo = inp["w_out"][l].reshape(8, 128, 8, 128)
        for b in range(2):
            blk = wo[:, :, b * 4:(b + 1) * 4, :].transpose(1, 2, 0, 3)
            o[:, OFF_WO + b * EL_WO: OFF_WO + (b + 1) * EL_WO] = blk.reshape(128, -1)
        wu = inp["w_up"][l].reshape(8, 128, 32, 128)
        for b in range(8):
            blk = wu[:, :, b * 4:(b + 1) * 4, :].transpose(1, 2, 0, 3)
            o[:, OFF_UP + b * EL_UP: OFF_UP + (b + 1) * EL_UP] = blk.reshape(128, -1)
        wd = inp["w_down"][l].reshape(32, 128, 8, 128)
        for c in range(8):
            blk = wd[:, :, c, :].transpose(1, 0, 2)
            o[:, OFF_DN + c * EL_DN: OFF_DN + (c + 1) * EL_DN] = blk.reshape(128, -1)
        wad = w_ada[l].reshape(8, 128, 48, 128)
        for b in range(12):
            blk = wad[:, :, b * 4:(b + 1) * 4, :].transpose(1, 2, 0, 3)
            o[:, OFF_ADA + b * EL_ADA: OFF_ADA + (b + 1) * EL_ADA] = blk.reshape(128, -1)
        wdt = wi[:, :, 10240:10272]
        wexp = np.repeat(wdt, 64, axis=2).reshape(8, 128, 16, 128)
        for b in range(4):
            blk = wexp[:, :, b * 4:(b + 1) * 4, :].transpose(1, 2, 0, 3)
            o[:, OFF_DTX + b * EL_DTX: OFF_DTX + (b + 1) * EL_DTX] = blk.reshape(128, -1)
    return out


def pack_vecs(inp, L):
    vec = np.zeros((128, L, NV), np.float32)
    tmv = np.zeros((128, L, 64), np.float32)
    for l in range(L):
        vec[:, l, V_NMIX:V_NMIX + 8] = _fm(inp["norm_mix"][l])
        vec[:, l, V_NMLP:V_NMLP + 8] = _fm(inp["norm_mlp"][l])
        vec[:, l, V_BADA:V_BADA + 48] = _fm(inp["b_ada"][l])
        for j in range(4):
            vec[:, l, V_CW + j * 32:V_CW + (j + 1) * 32] = _fm(inp["conv_w"][l, j])
        vec[:, l, V_CB:V_CB + 32] = _fm(inp["conv_b"][l])
        vec[:, l, V_SN:V_SN + 16] = _fm(inp["ssm_norm"][l])
        vec[:, l, V_BM:V_BM + 16] = _fm(inp["b_merge"][l])
        vec[:, l, V_HGN] = inp["hg_norm"][l]
        vec[:, l, V_DSK:V_DSK + 16] = _fm(np.repeat(inp["d_skip"][l], 64))
        lb = inp["lower_bounds"][:L].reshape(L, 8, 128)
        vec[:, l, V_LB:V_LB + 8 * L] = lb.transpose(2, 0, 1).reshape(128, -1)
        vec[:, l, V_NF:V_NF + 8] = _fm(inp["norm_final"])
        vec[:, l, V_DTB:V_DTB + 16] = _fm(np.repeat(inp["dt_bias"][l], 64))
        vec[:, l, V_ALG:V_ALG + 16] = _fm(np.repeat(inp["a_log"][l], 64))
        tmv[:, l, 0:32] = inp["dt_bias"][l][None, :]
        tmv[:, l, 32:64] = inp["a_log"][l][None, :]
    return vec, tmv


_L, _NTOK, _TH = 4, 2048, 512


def kernel(**inp):
    inp = {k: np.asarray(v) for k, v in inp.items()}
    L = _L
    wpk = pack_weights(inp, L)
    vec, tmv = pack_vecs(inp, L)
    in_maps = []
    for b in range(8):
        m = {}
        xp = inp["x_prompt"][b]
        m["xT"] = np.ascontiguousarray(xp.T.reshape(8, 128, _NTOK).transpose(1, 0, 2))
        xs = inp["x_sample"][16 * b:16 * b + 16, 0]
        m["xsT"] = np.ascontiguousarray(xs.T.reshape(8, 128, 16).transpose(1, 0, 2))
        c = np.concatenate([inp["c_prompt"][b:b + 1], inp["c_sample"][16 * b:16 * b + 16]], 0)
        m["cT"] = np.ascontiguousarray(c.T.reshape(8, 128, 17).transpose(1, 0, 2))
        m["wpk"] = wpk
        m["vec"] = vec
        m["tmv"] = tmv
        m["s_hg"] = np.ascontiguousarray(inp["state_hgrn"][:, 16 * b:16 * b + 16])
        m["s_ssm"] = np.ascontiguousarray(inp["state_ssm"][:, 16 * b:16 * b + 16])
        cv = inp["state_conv"][:, 16 * b:16 * b + 16]
        m["s_cvT"] = np.ascontiguousarray(cv.reshape(L, 16, 3, 32, 128).transpose(0, 4, 3, 2, 1))
        in_maps.append(m)
    nc = build(L, _NTOK, _TH, do_sample=True)
    res = run_bass_kernel_spmd(nc, in_maps, core_ids=list(range(8)))
    R = res.results
    y_prompt = np.stack([r["o_yT"].transpose(2, 1, 0).reshape(_NTOK, 1024) for r in R], 0)
    y_sample = np.concatenate([r["o_ysT"].transpose(2, 1, 0).reshape(16, 1, 1024) for r in R], 0)
    hg_p = np.stack([r["o_hgp"] for r in R], 1)
    ss_p = np.stack([r["o_ssp"].reshape(L, 128, 32, 64).transpose(0, 2, 3, 1) for r in R], 1)
    cv_p = np.stack([r["o_cvp"].transpose(0, 3, 2, 1).reshape(L, 3, 4096) for r in R], 1)
    hg_s = np.concatenate([r["o_hgs"] for r in R], 1)
    ss_s = np.concatenate([r["o_sss"] for r in R], 1)
    cv_s = np.concatenate([r["o_cvs"].transpose(0, 4, 3, 2, 1).reshape(L, 16, 3, 4096) for r in R], 1)
    f = lambda a: np.ascontiguousarray(a, dtype=np.float32)
    return (f(y_prompt), f(y_sample), f(hg_p), f(ss_p), f(cv_p), f(hg_s), f(ss_s), f(cv_s))
```
